# Optimizing a Trainium2 kernel written in Bass

```python
import math
import jax, jax.numpy as jnp
from jax import lax
import numpy as np

D_MODEL = 2048
BATCH = 2
SEQ = 4096
DEPTH = 1

MIX_WIDTH = D_MODEL
ATTN_HEAD_DIM = 64
ATTN_WIDTH = MIX_WIDTH // 2
ATTN_HEADS = ATTN_WIDTH // ATTN_HEAD_DIM
DILATION_PATTERNS = ((128, 1), (512, 4), (2048, 16))
QBLK = 128
GLA_WIDTH = MIX_WIDTH - ATTN_WIDTH
GLA_HEADS = 4
GLA_KEY_WIDTH = GLA_WIDTH // 2
GLA_HEAD_DK = GLA_KEY_WIDTH // GLA_HEADS
GLA_HEAD_DV = GLA_WIDTH // GLA_HEADS
GLA_GATE_RANK = 16
GLA_GATE_TAU = 16.0
GLA_CHUNK = 64
FFN_HIDDEN = -(-(8 * D_MODEL) // (3 * 256)) * 256
REL_BUCKETS = 32
REL_MAX_DISTANCE = 1024
NORM_EPS = 1e-6
IN_SIZES = (ATTN_WIDTH, ATTN_WIDTH, ATTN_WIDTH,
            GLA_KEY_WIDTH, GLA_KEY_WIDTH, GLA_WIDTH, GLA_WIDTH, 2 * GLA_GATE_RANK)
IN_WIDTH = 3 * ATTN_WIDTH + 2 * GLA_KEY_WIDTH + 2 * GLA_WIDTH + 2 * GLA_GATE_RANK

kernel_name = "hybrid_dilated_attn_gla_encoder_layer"


def rmsnorm(x, w):
    x32 = x.astype(jnp.float32)
    y = x32 * lax.rsqrt(jnp.mean(x32 * x32, axis=-1, keepdims=True) + NORM_EPS)
    return y.astype(x.dtype) * w


def _t5_bucket(rel):
    nb = REL_BUCKETS // 2
    ret = np.where(rel > 0, nb, 0)
    n = np.abs(rel)
    max_exact = nb // 2
    large = max_exact + (np.log(np.maximum(n, 1) / max_exact)
                         / math.log(REL_MAX_DISTANCE / max_exact)
                         * (nb - max_exact)).astype(np.int64)
    large = np.minimum(large, nb - 1)
    return (ret + np.where(n < max_exact, n, large)).astype(np.int32)


def _band_layout(L, radius, dilation):
    nb = -(-L // QBLK)
    span = QBLK + 2 * radius
    key_idx = np.arange(nb)[:, None] * QBLK + np.arange(span)[None, :]
    kpos = key_idx - radius
    qpos = np.arange(nb)[:, None] * QBLK + np.arange(QBLK)[None, :]
    off = kpos[:, None, :] - qpos[:, :, None]
    valid = (np.abs(off) <= radius) & (kpos[:, None, :] >= 0) & (kpos[:, None, :] < L)
    bucket = _t5_bucket(off[0] * dilation)
    return nb, key_idx, valid, bucket


def _from_sub(t, L, S):
    B, H, r = t.shape[:3]
    rest = t.shape[5:]
    t = t.reshape((B, H, r, -1) + rest)[:, :, :, :L]
    t = jnp.swapaxes(t, 2, 3)
    return t.reshape((B, H, S) + rest)


def dilated_window_attention(q, k, v, rel_table, window, dilation):
    B, H, S, Dh = q.shape
    L = S // dilation
    R = (window // 2) // dilation
    nb, key_idx, valid, bucket = _band_layout(L, R, dilation)
    Lp = nb * QBLK

    def to_sub(t):
        return t.reshape(B, H, L, dilation, Dh).transpose(0, 1, 3, 2, 4)

    zpad = ((0, 0), (0, 0), (0, 0))
    qs = jnp.pad(to_sub(q), zpad + ((0, Lp - L), (0, 0)))
    ks = jnp.pad(to_sub(k), zpad + ((R, Lp - L + R), (0, 0)))
    vs = jnp.pad(to_sub(v), zpad + ((R, Lp - L + R), (0, 0)))
    qb = qs.reshape(B, H, dilation, nb, QBLK, Dh)
    kb = ks[:, :, :, key_idx]
    vb = vs[:, :, :, key_idx]

    bias = jnp.transpose(rel_table[bucket].astype(jnp.float32), (2, 0, 1))
    logits = (jnp.einsum('bhrnqd,bhrnkd->bhrnqk', qb, kb).astype(jnp.float32)
              * (Dh ** -0.5) + bias[None, :, None, None])
    logits = jnp.where(valid[None, None, None], logits, -1e30)
    m = jnp.max(logits, axis=-1, keepdims=True)
    p = jnp.exp(logits - m)
    s = jnp.sum(p, axis=-1, keepdims=True)
    o = jnp.einsum('bhrnqk,bhrnkd->bhrnqd', p, vb.astype(jnp.float32)) / s
    lse = (m + jnp.log(s))[..., 0]
    return _from_sub(o, L, S), _from_sub(lse, L, S)


def mixture_of_dilations(q, k, v, rel_table):
    outs, lses = [], []
    for window, dilation in DILATION_PATTERNS:
        o, l = dilated_window_attention(q, k, v, rel_table, window, dilation)
        outs.append(o)
        lses.append(l)
    w = jax.nn.softmax(jnp.stack(lses, axis=0), axis=0)
    out = sum(w[i][..., None] * outs[i] for i in range(len(outs)))
    return out


def gla_causal(q, k, v, log_a):
    B, H, S, Dk = q.shape
    Dv = v.shape[-1]
    n = S // GLA_CHUNK
    qc = q.astype(jnp.float32).reshape(B, H, n, GLA_CHUNK, Dk)
    kc = k.astype(jnp.float32).reshape(B, H, n, GLA_CHUNK, Dk)
    vc = v.astype(jnp.float32).reshape(B, H, n, GLA_CHUNK, Dv)
    G = jnp.cumsum(log_a.astype(jnp.float32).reshape(B, H, n, GLA_CHUNK, Dk), axis=-2)
    q_in = qc * jnp.exp(G)
    k_in = kc * jnp.exp(-G)
    tril = jnp.tril(jnp.ones((GLA_CHUNK, GLA_CHUNK), jnp.float32))
    A = jnp.einsum('bhncd,bhnjd->bhncj', q_in, k_in) * tril
    intra = jnp.einsum('bhncj,bhnjv->bhncv', A, vc)
    G_last = G[..., -1:, :]
    k_end = kc * jnp.exp(G_last - G)
    U = jnp.einsum('bhnjd,bhnjv->bhndv', k_end, vc)
    decay = jnp.exp(G_last[..., 0, :])

    def step(state, inp):
        dec, u = inp
        return state * dec[..., None] + u, state

    _, S_prev = lax.scan(step, jnp.zeros((B, H, Dk, Dv), jnp.float32),
                         (jnp.moveaxis(decay, 2, 0), jnp.moveaxis(U, 2, 0)))
    S_prev = jnp.moveaxis(S_prev, 0, 2)
    inter = jnp.einsum('bhncd,bhndv->bhncv', q_in, S_prev)
    return (intra + inter).reshape(B, H, S, Dv)


def _heads(t, n_heads):
    B, S, W = t.shape
    return t.reshape(B, S, n_heads, W // n_heads).transpose(0, 2, 1, 3)


def _merge_heads(t):
    B, H, S, Dh = t.shape
    return t.transpose(0, 2, 1, 3).reshape(B, S, H * Dh)


def setup_inputs(seed: int = 0) -> dict:
    key = jax.random.key(seed)
    ks = jax.random.split(key, 16)
    f32 = jnp.float32
    nrm = lambda k, shape, scale: jax.random.normal(k, shape, f32) * scale
    return {
        "x": nrm(ks[0], (BATCH, SEQ, D_MODEL), 1.0),
        "attn_norm_w": 1.0 + nrm(ks[1], (DEPTH, D_MODEL), 0.02),
        "w_in": nrm(ks[2], (DEPTH, D_MODEL, IN_WIDTH), D_MODEL ** -0.5),
        "rel_bias_table": nrm(ks[3], (REL_BUCKETS, ATTN_HEADS), 0.3),
        "gla_gate_w_fwd": nrm(ks[4], (DEPTH, GLA_GATE_RANK, GLA_KEY_WIDTH), GLA_GATE_RANK ** -0.5),
        "gla_gate_b_fwd": nrm(ks[5], (DEPTH, GLA_KEY_WIDTH), 0.1),
        "gla_gate_w_bwd": nrm(ks[6], (DEPTH, GLA_GATE_RANK, GLA_KEY_WIDTH), GLA_GATE_RANK ** -0.5),
        "gla_gate_b_bwd": nrm(ks[7], (DEPTH, GLA_KEY_WIDTH), 0.1),
        "gla_norm_w": 1.0 + nrm(ks[8], (DEPTH, GLA_WIDTH), 0.02),
        "w_out": nrm(ks[9], (DEPTH, MIX_WIDTH, D_MODEL), MIX_WIDTH ** -0.5),
        "ffn_norm_w": 1.0 + nrm(ks[10], (DEPTH, D_MODEL), 0.02),
        "w_gate": nrm(ks[11], (DEPTH, D_MODEL, FFN_HIDDEN), D_MODEL ** -0.5),
        "w_up": nrm(ks[12], (DEPTH, D_MODEL, FFN_HIDDEN), D_MODEL ** -0.5),
        "w_down": nrm(ks[13], (DEPTH, FFN_HIDDEN, D_MODEL), FFN_HIDDEN ** -0.5),
        "final_norm_w": 1.0 + nrm(ks[14], (D_MODEL,), 0.02),
    }


def reference(x, attn_norm_w, w_in, rel_bias_table, gla_gate_w_fwd, gla_gate_b_fwd,
              gla_gate_w_bwd, gla_gate_b_bwd, gla_norm_w, w_out, ffn_norm_w,
              w_gate, w_up, w_down, final_norm_w):
    split_idx = []
    acc = 0
    for size in IN_SIZES[:-1]:
        acc += size
        split_idx.append(acc)

    for l in range(DEPTH):
        h = rmsnorm(x, attn_norm_w[l])
        proj = h @ w_in[l]
        aq, ak, av, gq, gk, gv, gr, glr = jnp.split(proj, split_idx, axis=-1)

        attn = mixture_of_dilations(_heads(aq, ATTN_HEADS), _heads(ak, ATTN_HEADS),
                                    _heads(av, ATTN_HEADS), rel_bias_table)
        attn_out = _merge_heads(attn).astype(x.dtype)

        lr_f, lr_b = glr[..., :GLA_GATE_RANK], glr[..., GLA_GATE_RANK:]
        log_a_f = jax.nn.log_sigmoid((lr_f @ gla_gate_w_fwd[l] + gla_gate_b_fwd[l])
                                     .astype(jnp.float32)) / GLA_GATE_TAU
        log_a_b = jax.nn.log_sigmoid((lr_b @ gla_gate_w_bwd[l] + gla_gate_b_bwd[l])
                                     .astype(jnp.float32)) / GLA_GATE_TAU
        qh = _heads(gq, GLA_HEADS) * (GLA_HEAD_DK ** -0.5)
        kh = _heads(gk, GLA_HEADS)
        vh = _heads(gv, GLA_HEADS)
        fwd = gla_causal(qh, kh, vh, _heads(log_a_f, GLA_HEADS))
        flip = lambda t: jnp.flip(t, axis=2)
        bwd = flip(gla_causal(flip(qh), flip(kh), flip(vh), flip(_heads(log_a_b, GLA_HEADS))))
        o = fwd + bwd
        o = o * lax.rsqrt(jnp.mean(o * o, axis=-1, keepdims=True) + NORM_EPS)
        gla_out = (_merge_heads(o).astype(x.dtype) * gla_norm_w[l]) * jax.nn.silu(gr)

        mixed = jnp.concatenate([attn_out, gla_out], axis=-1) @ w_out[l]
        x = x + mixed

        h = rmsnorm(x, ffn_norm_w[l])
        x = x + (jax.nn.silu(h @ w_gate[l]) * (h @ w_up[l])) @ w_down[l]

    return rmsnorm(x, final_norm_w)
```

```python
import numpy as np
import concourse.bass as bass
import concourse.mybir as mybir
from concourse.bass_utils import run_bass_kernel_spmd

F32 = mybir.dt.float32
BF16 = mybir.dt.bfloat16
AF = mybir.ActivationFunctionType
ALU = mybir.AluOpType
AX = mybir.AxisListType

ENGS = ("tensor", "vector", "scalar", "gpsimd", "sync")
SAME_ENG_ALL = False
RAW_GAP = 0


ALL_TRKS = []


class Trk:
    __slots__ = ("last_w", "readers", "dma_sem", "dma_cnt", "name", "excl")

    def __init__(self, name):
        ALL_TRKS.append(self)
        self.name = name
        self.excl = False
        self.last_w = None
        self.readers = {}
        self.dma_sem = None
        self.dma_cnt = 0


class V:
    __slots__ = ("ap", "trk")

    def __init__(self, ap, trk):
        self.ap = ap
        self.trk = trk


class T:
    def __init__(self, h, name, trk=None):
        self.h = h
        self.name = name
        self.trk = trk if trk is not None else Trk(name)

    def __getitem__(self, idx):
        return V(self.h[idx], self.trk)

    def alias(self, name):
        return T(self.h, name, Trk(name))

    def v(self, ap):
        return V(ap, self.trk)


class Region:
    def __init__(self, base, size, name=""):
        self.base, self.size, self.ptr, self.name = base, size, base, name

    def alloc(self, nb, what=""):
        off = (self.ptr + 31) // 32 * 32
        assert off + nb <= self.base + self.size, (self.name, what, nb, off - self.base, self.size)
        self.ptr = off + nb
        return off

    def sub(self, size, name=""):
        off = self.alloc(size, name)
        return Region(off, size, name)

    def reset(self):
        self.ptr = self.base


SB_LO, SB_HI = 16512, 229344


class Ins:
    __slots__ = ("eng", "fn", "kw", "deps", "idx", "milestone", "count", "dma", "waits", "cidx")

    def __init__(self, eng, fn, kw, deps, idx, dma=None):
        self.cidx = 0
        self.eng = eng
        self.fn = fn
        self.kw = kw
        self.deps = deps
        self.idx = idx
        self.milestone = False
        self.count = 0
        self.dma = dma
        self.waits = []


class Prog:
    def __init__(self, nc):
        ALL_TRKS.clear()
        self.nc = nc
        self.q = {e: [] for e in ENGS}
        self.sems = {}
        self.ncomp = {}
        self.n_dma_sems = 0

    def sbuf(self, name, shape, dt, reg=None):
        nb = int(np.prod(shape[1:])) * (4 if dt == F32 else 2)
        off = reg.alloc(nb, name)
        self.nsb = getattr(self, "nsb", 0) + 1
        return T(self.nc.alloc_sbuf_tensor_at(f"{name}_{self.nsb}", list(shape), dt, offset=off), name)

    def psum(self, name, shape, dt=F32):
        t = T(self.nc.alloc_psum_tensor(name, list(shape), dt), name)
        t.trk.excl = True
        return t

    def dram(self, name, shape, dt, kind="Internal"):
        return T(self.nc.dram_tensor(name, list(shape), dt, kind=kind), name)

    def _collect(self, eng, reads, writes, same_eng_raw=True):
        deps = []
        for trk in reads:
            if trk.last_w is not None:
                deps.append(trk.last_w)
            if trk.excl:
                deps.extend(ev for key, ev in trk.readers.items() if key != eng)
        for trk in writes:
            if trk.last_w is not None:
                deps.append(trk.last_w)
            deps.extend(trk.readers.values())
        out = []
        for d in deps:
            if d[0] == "eng" and d[1] == eng:
                if eng == "tensor":
                    continue
                if not same_eng_raw:
                    continue
                if not SAME_ENG_ALL:
                    is_raw = any(t.last_w is d for t in reads)
                    if not is_raw:
                        continue
            out.append(d)
        return out

    def op(self, eng, fn, **kw):
        reads, writes = [], []
        kw2 = {}
        for k, v in kw.items():
            if isinstance(v, V):
                if k in ("out", "accum_out", "ap") or k.startswith("out"):
                    writes.append(v.trk)
                else:
                    reads.append(v.trk)
                kw2[k] = v.ap
            else:
                kw2[k] = v
        if fn == "matmul" and kw.get("start") is False:
            pass
        deps = self._collect(eng, reads, writes)
        ncomp = self.ncomp.get(eng, 0)
        if eng != "tensor" and RAW_GAP:
            deps = [d for d in deps if not (d[0] == "eng" and d[1] == eng and ncomp - d[2].cidx > RAW_GAP)]
        ins = Ins(eng, fn, kw2, deps, len(self.q[eng]))
        ins.cidx = ncomp
        self.ncomp[eng] = ncomp + 1
        self.q[eng].append(ins)
        ev = ("eng", eng, ins)
        for t in reads:
            t.readers[eng] = ev
        for t in writes:
            t.last_w = ev
            t.readers = {}
        return ins

    def dma(self, queue, out, in_, extra_w=(), **kw):
        deps = self._collect(queue, [in_.trk], [out.trk] + list(extra_w), same_eng_raw=True)
        trk = out.trk
        if trk.dma_sem is None:
            trk.dma_sem = self.nc.alloc_semaphore(name=f"dsem{self.n_dma_sems}")
            self.n_dma_sems += 1
        trk.dma_cnt += 16
        kw2 = dict(kw)
        kw2["out"] = out.ap
        kw2["in_"] = in_.ap
        ins = Ins(queue, "dma_start", kw2, deps, len(self.q[queue]), dma=(trk.dma_sem, trk.dma_cnt))
        self.q[queue].append(ins)
        ev = ("dma", trk.dma_sem, trk.dma_cnt)
        in_.trk.readers[("d", id(trk.dma_sem))] = ev
        for t in [trk] + list(extra_w):
            t.last_w = ev
            t.readers = {}
        return ins

    def coll(self, kind, op, groups, in_, out, sem_name="ccsem"):
        deps = self._collect("gpsimd", [in_.trk], [out.trk])
        trk = out.trk
        if trk.dma_sem is None:
            trk.dma_sem = self.nc.alloc_semaphore(name=f"dsem{self.n_dma_sems}")
            self.n_dma_sems += 1
        trk.dma_cnt += 1
        kw2 = dict(kind=kind, op=op, replica_groups=groups, ins=[in_.ap], outs=[out.ap])
        ins = Ins("gpsimd", "collective_compute", kw2, deps, len(self.q["gpsimd"]),
                  dma=(trk.dma_sem, trk.dma_cnt, 1))
        self.q["gpsimd"].append(ins)
        ev = ("dma", trk.dma_sem, trk.dma_cnt)
        in_.trk.readers[("d", id(trk.dma_sem))] = ev
        trk.last_w = ev
        trk.readers = {}
        return ins

    def wait_all(self, eng, trks):
        deps = [t.last_w for t in trks if t.last_w is not None]
        ins = Ins(eng, None, {}, deps, len(self.q[eng]))
        self.q[eng].append(ins)
        return ins

    def barrier(self):
        deps = []
        for t in ALL_TRKS:
            if t.last_w is not None:
                deps.append(t.last_w)
            deps.extend(t.readers.values())
        for e in ENGS:
            ins = Ins(e, None, {}, list(deps), len(self.q[e]))
            self.q[e].append(ins)
        for t in ALL_TRKS:
            t.last_w = None
            t.readers = {}

    def emit(self):
        nc = self.nc
        for e in ENGS:
            for ins in self.q[e]:
                for d in ins.deps:
                    if d[0] == "eng":
                        d[2].milestone = True
        for e in ENGS:
            c = 0
            for ins in self.q[e]:
                if ins.milestone:
                    c += 1
                ins.count = c
        esem = {e: nc.alloc_semaphore(name=f"esem_{e}") for e in ENGS}
        for e in ENGS:
            seen = {}
            for ins in self.q[e]:
                need = {}
                for d in ins.deps:
                    if d[0] == "eng":
                        key, sem, val = ("e", d[1]), esem[d[1]], d[2].count
                    else:
                        key, sem, val = ("d", id(d[1])), d[1], d[2]
                    if seen.get(key, 0) >= val:
                        continue
                    if key not in need or need[key][1] < val:
                        need[key] = (sem, val)
                for key, (sem, val) in need.items():
                    seen[key] = val
                ins.waits = list(need.values())
        stats = {e: (len(self.q[e]), sum(len(i.waits) for i in self.q[e]),
                     sum(1 for i in self.q[e] if i.milestone)) for e in ENGS}
        self.stats = stats

        def replay(e):
            def body(engine):
                for ins in self.q[e]:
                    for sem, val in ins.waits:
                        engine.wait_ge(sem, val)
                    if ins.fn is None:
                        continue
                    r = getattr(engine, ins.fn)(**ins.kw)
                    if ins.dma is not None:
                        if len(ins.dma) == 3:
                            r.then_inc(ins.dma[0], 1)
                        else:
                            r.then_inc(ins.dma[0], 16)
                    elif ins.milestone:
                        r.then_inc(esem[e], 1)
            return body

        with nc.Block() as block:
            block.tensor(replay("tensor"))
            block.vector(replay("vector"))
            block.scalar(replay("scalar"))
            block.gpsimd(replay("gpsimd"))
            block.sync(replay("sync"))
        return stats


D = 2048
SEQ = 4096
NB = 2
TOK = 1024
WIN = 3072
KC = 16
FH = 5632
FC = 44
EPS = 1e-6
INW = 6176
SQD = float(np.sqrt(D))

C_ONES = 0
C_IDENT = 128
C_N = 256


def build_consts():
    c = np.zeros((128, C_N), np.float32)
    c[:, C_ONES:C_ONES + 128] = 1.0
    c[:, C_IDENT:C_IDENT + 128] = np.eye(128, dtype=np.float32)
    return c


class Builder:
    def __init__(self, stage="full"):
        self.stage = stage
        self.nc = bass.Bass("TRN2", target_bir_lowering=False)
        self.P = Prog(self.nc)
        P = self.P
        self.xw = P.dram("xw", [D, WIN], F32, kind="ExternalInput")
        self.nw = P.dram("nw", [128, 48], F32, kind="ExternalInput")
        self.cst = P.dram("cst", [128, C_N2], F32, kind="ExternalInput")
        self.xo = P.dram("xo", [3, D, TOK], F32, kind="ExternalInput")
        self.wlr_o = P.dram("wlr_o", [3, D, 16], F32, kind="ExternalInput")
        self.gw_o = P.dram("gw_o", [3, 16, 512], F32, kind="ExternalInput")
        self.gb_o = P.dram("gb_o", [3, 1, 512], F32, kind="ExternalInput")
        self.flgd = P.dram("flgd", [128, 8], F32, kind="ExternalInput")
        self.gwf = P.dram("gwf", [16, 512], F32, kind="ExternalInput")
        self.gwb = P.dram("gwb", [16, 512], F32, kind="ExternalInput")
        self.gbf = P.dram("gbf", [1, 512], F32, kind="ExternalInput")
        self.gbb = P.dram("gbb", [1, 512], F32, kind="ExternalInput")
        self.gnwd = P.dram("gnwd", [128, 1024], F32, kind="ExternalInput")
        self.w_out = P.dram("w_out", [D, D], F32, kind="ExternalInput")
        self.w_gate = P.dram("w_gate", [D, FH], F32, kind="ExternalInput")
        self.w_up = P.dram("w_up", [D, FH], F32, kind="ExternalInput")
        self.w_down = P.dram("w_down", [FH, D], F32, kind="ExternalInput")
        self.out = P.dram("out", [D, TOK], F32, kind="ExternalOutput")
        if stage == "ffn":
            self.mixd = P.dram("mixd", [D, TOK], F32, kind="ExternalInput")
        self.w_in = P.dram("w_in", [D, INW], F32, kind="ExternalInput")
        self.relt = P.dram("relt", [32, 16], F32, kind="ExternalInput")
        self.ohd = P.dram("ohd", [33, 3, 512], F32, kind="ExternalInput")
        self.valtd = P.dram("valtd", [128, 53], F32, kind="ExternalInput")
        self.uvd = P.dram("uvd", [16, 3, 512], F32)
        if stage in ("attn", "gla"):
            self.dbg = P.dram("dbg", [128, 8, TOK], BF16, kind="ExternalOutput")
        if stage == "gla":
            self.dbg2 = P.dram("dbg2", [128, 2, 4, 256], F32, kind="ExternalOutput")
        top = Region(SB_LO, SB_HI - SB_LO, "top")
        self.R_const = top.sub(5 * 1024, "const")
        self.R_mix = top.sub(32 * 1024, "mix")
        self.R_work = top.sub(SB_HI - top.ptr - 64, "work")
        wb_ = self.R_work.base
        self.RA = Region(wb_, 48 * 1024 + 64, "RA")
        self.RB = Region(wb_ + 48 * 1024 + 64, 78 * 1024, "RB")
        self.RC = Region(self.RB.base + 78 * 1024, self.R_work.base + self.R_work.size - (self.RB.base + 78 * 1024), "RC")
        RC = self.R_const
        self.cf = P.sbuf("cf", [128, C_N2], F32, RC)
        self.cb = P.sbuf("cb", [128, C_N], BF16, RC)
        self.flg = P.sbuf("flg", [128, 8], F32, RC)
        self.onec = P.sbuf("onec", [128, 1], F32, RC)
        P.dma("sync", out=self.flg[:, :], in_=self.flgd[:, :])
        P.op("vector", "memset", ap=self.onec[:, :], constant=1.0)
        self.nws = P.sbuf("nws", [128, 48], F32, RC)
        self.nwq = P.sbuf("nwq", [128, 48], F32, RC)
        P.dma("sync", out=self.cf[:, :], in_=self.cst[:, :])
        P.dma("sync", out=self.nws[:, :], in_=self.nw[:, :])
        P.op("vector", "tensor_copy", out=self.cb[:, :], in_=self.cf[:, 0:C_N])
        P.op("vector", "tensor_copy", out=self.nwq[:, :], in_=self.nws[:, :])
        self.epsc = P.sbuf("epsc", [128, 1], F32, RC)
        P.op("vector", "memset", ap=self.epsc[:, :], constant=EPS)
        self.ones_b = self.cb.v(self.cb.h[:, C_ONES:C_ONES + 128])
        self.ps = [P.psum(f"ps{i}", [128, 512], F32) for i in range(7)]
        self.ps_t = P.psum("pst", [128, 1024], BF16)
        self.ident_b = self.cb.v(self.cb.h[:, C_IDENT:C_IDENT + 128])

    def rms_rstd(self, srcT, rstd, sq, ps):
        P = self.P
        for h in range(2):
            hs = slice(512 * h, 512 * h + 512)
            for kc in range(KC):
                P.op("scalar", "activation", out=sq[:, kc, :], in_=srcT[kc][:, kc, hs], func=AF.Square)
            for kc in range(KC):
                P.op("tensor", "matmul", out=ps[:, :], lhsT=self.ones_b, rhs=sq[:, kc, :],
                     start=(kc == 0), stop=(kc == KC - 1))
            P.op("scalar", "activation", out=rstd[:, hs], in_=ps[:, :], func=AF.Sqrt, bias=self.epsc[:, 0:1],
                 scale=1.0 / D)
            P.op("vector", "reciprocal", out=rstd[:, hs], in_=rstd[:, hs])

    def phase_out(self, mixT_attn, mixT_gla):
        P = self.P
        ps = self.ps
        RW = self.R_work
        RW.reset()
        x1T_all = P.sbuf("x1T", [128, KC, TOK], F32, RW)
        x1c = [x1T_all.alias(f"x1T{n}") for n in range(KC)]
        rstd = P.sbuf("rstd", [128, TOK], F32, RW)
        mark = RW.ptr
        xsrc = self.xw.h[:, 1024:2048].rearrange("(c p) t -> p c t", p=128)
        for q in range(4):
            P.dma("sync", out=x1c[4 * q][:, 4 * q:4 * q + 4, :], in_=self.xw.v(xsrc[:, 4 * q:4 * q + 4, :]),
                  extra_w=[x1c[4 * q + i].trk for i in range(1, 4)])
        wob = [P.sbuf(f"wob{i}", [128, KC, 256], BF16, RW) for i in range(2)]
        wo_src = self.w_out.h.rearrange("(c p) n -> p c n", p=128)
        for nn in range(8):
            wo = wob[nn % 2]
            cs = slice(256 * nn, 256 * nn + 256)
            P.dma("gpsimd", out=wo[:, :, :], in_=self.w_out.v(wo_src[:, :, cs]))
            for j in range(2):
                n = 2 * nn + j
                for h in range(2):
                    hs = slice(512 * h, 512 * h + 512)
                    pp = ps[(2 * j + h) % 4]
                    for kc in range(KC):
                        rhs = mixT_attn[:, kc, hs] if kc < 8 else mixT_gla[:, kc - 8, hs]
                        P.op("tensor", "matmul", out=pp[:, :], lhsT=wo[:, kc, 128 * j:128 * j + 128],
                             rhs=rhs, start=(kc == 0), stop=(kc == KC - 1))
                    P.op("vector", "tensor_tensor", out=x1c[n][:, n, hs], in0=x1c[n][:, n, hs], in1=pp[:, :],
                         op=ALU.add)
        P.barrier()
        RW.ptr = mark
        self.R_mix.reset()
        hnT = P.sbuf("hnT", [128, KC, TOK], BF16, self.R_mix)
        sq = P.sbuf("sq", [128, KC, 512], BF16, RW)
        self.rms_rstd(x1c, rstd, sq, ps[4])
        for kc in range(KC):
            P.op("vector", "scalar_tensor_tensor", out=hnT[:, kc, :], in0=x1c[kc][:, kc, :],
                 scalar=self.nwq[:, 16 + kc:17 + kc], in1=rstd[:, :], op0=ALU.mult, op1=ALU.mult)
        NG = 11
        wgb = [P.sbuf(f"wgb{i}", [128, KC, 256], BF16, RW) for i in range(2)]
        wub = [P.sbuf(f"wub{i}", [128, KC, 256], BF16, RW) for i in range(2)]
        wdb = [P.sbuf(f"wdb{i}", [128, 4, 1024], BF16, RW) for i in range(2)]
        aT = [P.sbuf(f"aT{i}", [128, 4, TOK], BF16, RW) for i in range(2)]
        sg = [P.sbuf(f"sg{i}", [128, 512], F32, RW) for i in range(2)]
        wg_src = self.w_gate.h.rearrange("(c p) f -> p c f", p=128)
        wu_src = self.w_up.h.rearrange("(c p) f -> p c f", p=128)
        wd_src = self.w_down.h.rearrange("(c p) n -> p c n", p=128)

        def load_gu(si):
            b = si % 2
            fs = slice(256 * si, 256 * si + 256)
            P.dma("gpsimd", out=wgb[b][:, :, :], in_=self.w_gate.v(wg_src[:, :, fs]))
            P.dma("gpsimd", out=wub[b][:, :, :], in_=self.w_up.v(wu_src[:, :, fs]))

        def load_d(gi, nh):
            P.dma("gpsimd", out=wdb[nh][:, :, :],
                  in_=self.w_down.v(wd_src[:, 4 * gi:4 * gi + 4, 1024 * nh:1024 * nh + 1024]))

        dcnt = [0]

        def down_half(gi, nh):
            ab = aT[gi % 2]
            for n in range(8 * nh, 8 * nh + 8):
                for h in range(2):
                    hs = slice(512 * h, 512 * h + 512)
                    pp = ps[4 + dcnt[0] % 2]
                    dcnt[0] += 1
                    nl = n - 8 * nh
                    for j in range(4):
                        P.op("tensor", "matmul", out=pp[:, :], lhsT=wdb[nh][:, j, 128 * nl:128 * nl + 128],
                             rhs=ab[:, j, hs], start=(j == 0), stop=(j == 3))
                    P.op("vector", "tensor_tensor", out=x1c[n][:, n, hs], in0=x1c[n][:, n, hs], in1=pp[:, :],
                         op=ALU.add)

        load_gu(0)
        load_gu(1)
        load_d(0, 0)
        load_d(0, 1)
        cnt = 0
        for gi in range(NG):
            for s_ in range(2):
                si = 2 * gi + s_
                b = si % 2
                for jl in range(2):
                    j = 2 * s_ + jl
                    for h in range(2):
                        hs = slice(512 * h, 512 * h + 512)
                        pg = ps[cnt % 2]
                        pu = ps[2 + cnt % 2]
                        sgt = sg[cnt % 2]
                        cnt += 1
                        for kc in range(KC):
                            P.op("tensor", "matmul", out=pg[:, :], lhsT=wgb[b][:, kc, 128 * jl:128 * jl + 128],
                                 rhs=hnT[:, kc, hs], start=(kc == 0), stop=(kc == KC - 1))
                        for kc in range(KC):
                            P.op("tensor", "matmul", out=pu[:, :], lhsT=wub[b][:, kc, 128 * jl:128 * jl + 128],
                                 rhs=hnT[:, kc, hs], start=(kc == 0), stop=(kc == KC - 1))
                        P.op("scalar", "activation", out=sgt[:, :], in_=pg[:, :], func=AF.Silu)
                        P.op("vector", "tensor_tensor", out=aT[gi % 2][:, j, hs], in0=sgt[:, :], in1=pu[:, :],
                             op=ALU.mult)
                if si + 2 < 2 * NG:
                    load_gu(si + 2)
                if gi > 0:
                    down_half(gi - 1, s_)
                    load_d(gi, s_)
        down_half(NG - 1, 0)
        down_half(NG - 1, 1)
        self.rms_rstd(x1c, rstd, sq, ps[6])
        for kc in range(KC):
            P.op("vector", "scalar_tensor_tensor", out=x1c[kc][:, kc, :], in0=x1c[kc][:, kc, :],
                 scalar=self.nwq[:, 32 + kc:33 + kc], in1=rstd[:, :], op0=ALU.mult, op1=ALU.mult)
        odst = self.out.h.rearrange("(c p) t -> p c t", p=128)
        for n in range(KC):
            P.dma("sync", out=self.out.v(odst[:, n, :]), in_=x1c[n][:, n, :])
        P.wait_all("sync", [self.out.trk])

    def build(self):
        P = self.P
        if self.stage == "attn":
            attnT = P.sbuf("attnT", [128, 8, TOK], BF16, self.R_mix)
            self.xwT = P.sbuf("xwT", [128, KC, WIN], BF16, self.R_work)
            self.attn_setup()
            self.phase_window()
            self.phase_attn(attnT)
            P.dma("sync", out=self.dbg[:, :, :], in_=attnT[:, :, :])
            P.wait_all("sync", [self.dbg.trk])
        if self.stage == "full":
            attnT = P.sbuf("attnT", [128, 8, TOK], BF16, self.R_mix)
            glaT = P.sbuf("glaT", [128, 8, TOK], BF16, self.R_mix)
            RS_ = Region(self.R_mix.base, 16 * 1024, "states")
            SF = P.sbuf("SF", [128, 4, 256], F32, RS_)
            SB = P.sbuf("SB", [128, 4, 256], F32, RS_)
            self.R_work.reset()
            self.attn_setup()
            P.barrier()
            self.phase_gla_others(SF, SB)
            self.phase_gla_own(SF, SB, glaT)
            self.R_work.reset()
            self.xwT = P.sbuf("xwT", [128, KC, WIN], BF16, self.R_work)
            self.phase_window()
            self.phase_attn(attnT)
            self.phase_out(attnT, glaT)
        if self.stage == "gla":
            glaT = P.sbuf("glaT", [128, 8, TOK], BF16, self.R_mix)
            SF = P.sbuf("SF", [128, 4, 256], F32, self.R_mix)
            SB = P.sbuf("SB", [128, 4, 256], F32, self.R_mix)
            self.phase_gla_others(SF, SB)
            P.dma("sync", out=self.dbg2[:, 0, :, :], in_=SF[:, :, :])
            P.dma("sync", out=self.dbg2[:, 1, :, :], in_=SB[:, :, :])
            if not os.environ.get("GLA_OTHERS_ONLY"):
                self.phase_gla_own(SF, SB, glaT)
            P.dma("sync", out=self.dbg[:, :, :], in_=glaT[:, :, :])
            P.wait_all("sync", [self.dbg.trk, self.dbg2.trk])
        if self.stage == "ffn":
            ma = P.sbuf("mixa", [128, 8, TOK], BF16, self.R_mix)
            mg = P.sbuf("mixg", [128, 8, TOK], BF16, self.R_mix)
            P.dma("gpsimd", out=ma[:, :, :],
                  in_=self.mixd.v(self.mixd.h[0:1024, :].rearrange("(c p) t -> p c t", p=128)))
            P.dma("gpsimd", out=mg[:, :, :],
                  in_=self.mixd.v(self.mixd.h[1024:2048, :].rearrange("(c p) t -> p c t", p=128)))
            self.phase_out(ma, mg)
        self.stats = P.emit()
        return self.nc


PATS = ((1, 128), (4, 128), (16, 64))


def _t5_bucket_np(rel):
    nb = 16
    ret = np.where(rel > 0, nb, 0)
    n = np.abs(rel)
    max_exact = nb // 2
    large = max_exact + (np.log(np.maximum(n, 1) / max_exact) / np.log(1024 / max_exact)
                         * (nb - max_exact)).astype(np.int64)
    large = np.minimum(large, nb - 1)
    return (ret + np.where(n < max_exact, n, large)).astype(np.int32)


def build_onehot():
    oh = np.zeros((33, 3, 512), np.float32)
    for pi, (r, _) in enumerate(PATS):
        for i in range(512):
            dlt = i - 255
            if abs(dlt) <= 64:
                oh[int(_t5_bucket_np(np.array(dlt * r))), pi, i] = 1.0
            else:
                oh[32, pi, i] = 1.0
    return oh


def build_valt(start):
    v = np.zeros((128, 53), np.float32)
    k = np.arange(128)

    def ok(w):
        t = start - 1024 + w
        return ((t >= 0) & (t < SEQ)).astype(np.float32)
    for j in range(9):
        v[:, j] = ok(960 + 128 * j + k)
    for p in range(4):
        for j in range(3):
            v[:, 9 + 3 * p + j] = ok(4 * (192 + 128 * j + k) + p)
    for p in range(16):
        v[:, 21 + 2 * p] = ok(16 * k + p)
        vb = ok(16 * (128 + k) + p)
        vb[64:] = 0.0
        v[:, 21 + 2 * p + 1] = vb
    return v


def _attn_setup(self):
    P = self.P
    RW = self.R_work
    m = RW.ptr
    tab = P.sbuf("tab", [33, 16], F32, RW)
    ohs = P.sbuf("ohs", [33, 3, 512], F32, RW)
    ev = P.sbuf("ev", [16, 3, 512], F32, RW)
    P.op("vector", "memset", ap=tab[32:33, :], constant=-10000.0)
    P.dma("sync", out=tab[0:32, :], in_=self.relt[:, :])
    P.dma("sync", out=ohs[:, :, :], in_=self.ohd[:, :, :])
    for pi in range(3):
        pp = self.ps[pi]
        P.op("tensor", "matmul", out=pp[0:16, :], lhsT=tab[:, :], rhs=ohs[:, pi, :], start=True, stop=True)
        P.op("scalar", "activation", out=ev[:, pi, :], in_=pp[0:16, :], func=AF.Exp)
    P.dma("sync", out=self.uvd[:, :, :], in_=ev[:, :, :])
    RW.ptr = m


def _phase_window(self):
    P = self.P
    RW = self.R_work
    m = RW.ptr
    NT = 256
    xs = [P.sbuf(f"xs{i}", [128, KC, NT], F32, RW) for i in range(3)]
    sqs = [P.sbuf(f"sqw{i}", [128, KC, NT], BF16, RW) for i in range(2)]
    rss = [P.sbuf(f"rsw{i}", [128, NT], F32, RW) for i in range(2)]
    src = self.xw.h.rearrange("(c p) t -> p c t", p=128)
    for wt in range(WIN // NT):
        x = xs[wt % 3]
        sq, rs = sqs[wt % 2], rss[wt % 2]
        ts = slice(NT * wt, NT * wt + NT)
        P.dma("sync", out=x[:, :, :], in_=self.xw.v(src[:, :, ts]))
        P.op("scalar", "activation", out=sq[:, :, :], in_=x[:, :, :], func=AF.Square)
        pp = self.ps[5 + wt % 2]
        for kc in range(KC):
            P.op("tensor", "matmul", out=pp[:, 0:NT], lhsT=self.ones_b, rhs=sq[:, kc, :],
                 start=(kc == 0), stop=(kc == KC - 1))
        P.op("scalar", "activation", out=rs[:, :], in_=pp[:, 0:NT], func=AF.Sqrt, bias=self.epsc[:, 0:1],
             scale=1.0 / D)
        P.op("vector", "reciprocal", out=rs[:, :], in_=rs[:, :])
        for kc in range(KC):
            P.op("vector", "scalar_tensor_tensor", out=self.xwT[:, kc, ts], in0=x[:, kc, :],
                 scalar=self.nwq[:, kc:kc + 1], in1=rs[:, :], op0=ALU.mult, op1=ALU.mult)
    P.barrier()
    RW.ptr = m


import os
NDUM_DEFAULT = 0


def _phase_attn(self, attnT):
    P = self.P
    ps = self.ps
    RW = self.R_work
    m = RW.ptr
    NWB = 3
    wb = [P.sbuf(f"awb{i}", [128, KC, 128], BF16, RW) for i in range(NWB)]
    Qn = P.sbuf("Qn", [128, 1024], BF16, RW)
    Qd4 = P.sbuf("Qd4", [128, 4, 256], BF16, RW)
    Qd16 = P.sbuf("Qd16", [128, 16, 64], BF16, RW)
    Kn = P.sbuf("Kn", [128, 1152], BF16, RW)
    Kd4 = P.sbuf("Kd4", [128, 4, 384], BF16, RW)
    Kd16 = P.sbuf("Kd16", [128, 16, 192], BF16, RW)
    Vn = P.sbuf("Vn", [128, 1152], BF16, RW)
    Vd4 = P.sbuf("Vd4", [128, 4, 384], BF16, RW)
    Vd16 = P.sbuf("Vd16", [128, 16, 192], BF16, RW)
    G = P.sbuf("G", [128, 6, 2, 128], F32, RW)
    es = [P.sbuf(f"es{i}", [128, 512], F32, RW) for i in range(3)]
    pT = [P.sbuf(f"pT{i}", [128, 512], BF16, RW) for i in range(3)]
    acc = [P.sbuf("accA", [65, 1024], F32, RW), P.sbuf("accB", [128, 1024], F32, RW)]
    NVT = 16
    VT0 = P.sbuf("VT0", [128, NVT, 194], BF16, RW)
    VTs = [VT0, VT0]
    valt = P.sbuf("valt", [128, 53], F32, RW)
    P.dma("sync", out=valt[:, :], in_=self.valtd[:, :])
    P.op("vector", "memset", ap=VT0[:, :, :], constant=0.0)
    ps_s = [ps[0], ps[1], ps[6]]
    ps_o = [ps[2], ps[3]]
    ps_t = self.ps_t
    ps_p = [ps[4], ps[5]]
    ps_b = ps[6]
    win_src = self.w_in.h.rearrange("(c p) n -> p c n", p=128)
    wcnt = [0]
    scnt = [0]
    pcnt = [0]
    ecnt = [0]
    ocnt = [0]

    def load_w(col0):
        b = wb[wcnt[0] % NWB]
        wcnt[0] += 1
        P.dma("gpsimd", out=b[:, :, :], in_=self.w_in.v(win_src[:, :, col0:col0 + 128]))
        return b

    def proj(wt_, tok0):
        pp = ps_p[pcnt[0] % 2]
        pcnt[0] += 1
        for kc in range(KC):
            P.op("tensor", "matmul", out=pp[:, :], lhsT=wt_[:, kc, :], rhs=self.xwT[:, kc, tok0:tok0 + 512],
                 start=(kc == 0), stop=(kc == KC - 1))
        return pp

    def evac(eng, out, in_):
        em = os.environ.get("EVAC", "")
        if em == "none":
            return
        if em == "vec":
            eng = "vector"
        if em == "nostride" and len(in_.ap.ap) > 2:
            return
        if eng == "scalar":
            P.op("scalar", "activation", out=out, in_=in_, func=AF.Copy)
        else:
            P.op("vector", "tensor_copy", out=out, in_=in_)

    def kv_evac(pp, wt, Nn, D4, D16):
        if wt == 1:
            evac("scalar", Nn[:, 0:64], V(pp.ap[:, 448:512], pp.trk))
        elif wt in (2, 3):
            evac("scalar", Nn[:, 64 + 512 * (wt - 2):64 + 512 * (wt - 2) + 512], pp)
        elif wt == 4:
            evac("scalar", Nn[:, 1088:1152], V(pp.ap[:, 0:64], pp.trk))
        if wt == 1:
            evac("vector", D4[:, :, 0:64], V(pp.ap[:, 256:512].rearrange("q (l p) -> q p l", p=4), pp.trk))
        elif wt in (2, 3):
            l0 = 64 + 128 * (wt - 2)
            evac("vector", D4[:, :, l0:l0 + 128], V(pp.ap.rearrange("q (l p) -> q p l", p=4), pp.trk))
        elif wt == 4:
            evac("vector", D4[:, :, 320:384], V(pp.ap[:, 0:256].rearrange("q (l p) -> q p l", p=4), pp.trk))
        evac("vector", D16[:, :, 32 * wt:32 * wt + 32],
             V(pp.ap.rearrange("q (l p) -> q p l", p=16), pp.trk))

    LVL = int(os.environ.get("ATTN_LVL", "9"))
    NDUM = int(os.environ.get("ATTN_DUMMY", str(NDUM_DEFAULT)))
    NCH = int(os.environ.get("ATTN_NCH", "8"))

    def norm_chunk(c):
        banks = [ps_s[0], ps_s[1], ps_o[0], ps_o[1]]
        for hl in range(2):
            a = acc[hl]
            dr = 64 if hl == 0 else 0
            P.op("vector", "reciprocal", out=a[dr:dr + 1, :], in_=a[dr:dr + 1, :])
        for hl in range(2):
            a = acc[hl]
            dr = 64 if hl == 0 else 0
            for tt in range(2):
                ts = slice(512 * tt, 512 * tt + 512)
                P.op("tensor", "matmul", out=banks[2 * hl + tt][64 * hl:64 * hl + 64, :],
                     lhsT=self.cf.v(self.cf.h[dr:dr + 1, C_ONES:C_ONES + 64]), rhs=a[dr:dr + 1, ts],
                     start=True, stop=True)
        for hl in range(2):
            a = acc[hl]
            for tt in range(2):
                ts = slice(512 * tt, 512 * tt + 512)
                P.op("vector", "tensor_tensor", out=attnT[64 * hl:64 * hl + 64, c, ts],
                     in0=a[64 * hl:64 * hl + 64, ts], in1=banks[2 * hl + tt][64 * hl:64 * hl + 64, :], op=ALU.mult)

    for c in range(NCH):
        srcap = bass.AP(tensor=self.uvd.h, offset=(2 * c * 3) * 512 + 64,
                        ap=[[1, 128], [512, 6], [128, 2], [1, 128]])
        for e_ in range(int(os.environ.get("NG_", "12"))):
            hl_, rem = divmod(e_, 6)
            pi_, tl_ = divmod(rem, 2)
            srcap = bass.AP(tensor=self.uvd.h, offset=((2 * c + hl_) * 3 + pi_) * 512 + 64 + 128 * tl_,
                            ap=[[1, 128], [1, 128]])
            P.dma("sync", out=G[:, hl_ * 3 + pi_, tl_, :], in_=self.uvd.v(srcap))
        if c == 0:
            wts = [load_w(128 * c), load_w(1024 + 128 * c), load_w(2048 + 128 * c)]
        wq, wk, wv = wts
        for tt in range(2):
            pp = proj(wq, 1024 + 512 * tt)
            evac("scalar", Qn[:, 512 * tt:512 * tt + 512], pp[:, :])
            evac("vector", Qd4[:, :, 128 * tt:128 * tt + 128],
                 V(pp.h[:, :].rearrange("q (l p) -> q p l", p=4), pp.trk))
            evac("vector", Qd16[:, :, 32 * tt:32 * tt + 32],
                 V(pp.h[:, :].rearrange("q (l p) -> q p l", p=16), pp.trk))
        if c > 0:
            norm_chunk(c - 1)
        for wt in range(6):
            pp = proj(wk, 512 * wt)
            kv_evac(V(pp.h[:, :], pp.trk), wt, Kn, Kd4, Kd16)
        for wt in range(6):
            pp = proj(wv, 512 * wt)
            kv_evac(V(pp.h[:, :], pp.trk), wt, Vn, Vd4, Vd16)
        if c + 1 < NCH:
            wts = [load_w(128 * (c + 1)), load_w(1024 + 128 * (c + 1)), load_w(2048 + 128 * (c + 1))]
        sets_ = []
        for pi, (r, QB) in enumerate(PATS):
            if r == 1:
                sets_.append((pi, r, QB, 0, [(Kn[:, 128 * j:128 * j + 128], Vn[:, 128 * j:128 * j + 128], j, 128)
                                             for j in range(9)]))
            elif r == 4:
                sets_.append((pi, r, QB, 0, [(Kd4[:, p, 128 * j:128 * j + 128], Vd4[:, p, 128 * j:128 * j + 128],
                                              9 + 3 * p + j, 128) for p in range(4) for j in range(3)]))
            else:
                for hf in range(2):
                    sets_.append((pi, r, QB, hf,
                                  [(Kd16[:, p, 128 * t:128 * t + (128 if t == 0 else 64)],
                                    Vd16[:, p, 128 * t:128 * t + (128 if t == 0 else 64)], 21 + 2 * p + t,
                                    128 if t == 0 else 64) for p in range(8 * hf, 8 * hf + 8) for t in range(2)]))

        def build_set(tiles, VT):
            pstride = VT.h[:, :, :].ap[0][0]
            v0 = tiles[0][2]
            for t0 in range(0, len(tiles), 4):
                nt_ = min(4, len(tiles) - t0)
                for ti in range(t0, t0 + nt_):
                    ksrc, vsrc, vcol, nk = tiles[ti]
                    tcol = 128 * (ti - t0)
                    P.op("tensor", "transpose", out=ps_t[0:nk, tcol:tcol + 128], in_=vsrc, identity=self.ident_b)
                oap = bass.AP(tensor=VT.h, offset=VT.h[:, t0, 0:1].offset,
                              ap=[[pstride, 128], [194, nt_], [129, 2], [1, 64]])
                P.op("vector", "tensor_copy", out=VT.v(oap),
                     in_=ps_t.v(ps_t.h[:, 0:128 * nt_].rearrange("k (t a b) -> k t a b", t=nt_, a=2)))
                P.op("gpsimd", "tensor_copy", out=VT[:, t0:t0 + nt_, 64:66],
                     in_=valt.v(valt.h[:, v0 + t0:v0 + t0 + nt_].unsqueeze(2).broadcast_to([128, nt_, 2])))

        for si, (pi, r, QB, hf, tiles) in enumerate(sets_):
            VT = VTs[0]
            build_set(tiles, VT)
            if True:
                if LVL < 3:
                    continue
                if r == 1:
                    groups = [[(hl, Qn[:, :], 128 * (2 * g + b), 2 * g + b, 2 * g + b + 1) for b in range(2)]
                              for g in range(4) for hl in range(2)]
                elif r == 4:
                    groups = [[(hl, Qd4[:, p, :], 128 * b, 3 * p + b, 3 * p + b + 1) for b in range(2)]
                              for p in range(4) for hl in range(2)]
                else:
                    groups = [[(hl, Qd16[:, 8 * hf + 4 * g + b, :], 0, 2 * (4 * g + b), 2 * (4 * g + b) + 1)
                               for b in range(4)] for g in range(2) for hl in range(2)]
                def stage1(grp):
                    hl = grp[0][0]
                    hp = slice(64 * hl, 64 * hl + 64)
                    sp = ps_s[ecnt[0] % 3]
                    est = es[ecnt[0] % 3]
                    ptt = pT[ecnt[0] % 3]
                    ecnt[0] += 1
                    nb = len(grp)
                    W = 2 * QB
                    for bi, (_, qsrc, q0, ta, tb) in enumerate(grp):
                        qap = qsrc.ap[hp, q0:q0 + QB]
                        qtrk = qsrc.trk
                        for t_, tix in enumerate((ta, tb)):
                            ksrc, vsrc, vcol, nk = tiles[tix]
                            P.op("tensor", "matmul", out=sp[0:nk, W * bi + QB * t_:W * bi + QB * t_ + QB],
                                 lhsT=V(ksrc.ap[hp], ksrc.trk), rhs=V(qap, qtrk), start=True, stop=True)
                    nkb = tiles[grp[0][4]][3]
                    for _ in range(NDUM):
                        P.op("tensor", "matmul", out=ps_p[0][:, :], lhsT=self.ident_b, rhs=self.xwT[:, 0, 0:512],
                             start=True, stop=True)
                    P.op("scalar", "activation", out=est[:, 0:W * nb], in_=sp[:, 0:W * nb], func=AF.Exp,
                         scale=0.125)
                    gsl = G.h[:, hl * 3 + pi, :, :]
                    erev = gsl[:, :, ::-1][:, :, 0:QB]
                    ebc = erev.unsqueeze(1).broadcast_to([128, nb, 2, QB])
                    P.op("vector", "tensor_tensor",
                         out=ptt.v(ptt.h[:, 0:W * nb].rearrange("k (b t q) -> k b t q", b=nb, t=2)),
                         in0=est.v(est.h[:, 0:W * nb].rearrange("k (b t q) -> k b t q", b=nb, t=2)),
                         in1=G.v(ebc), op=ALU.mult)
                    return (hl, ptt, W, nb)

                def stage2(grp, ctx):
                    hl, ptt, W, nb = ctx
                    op_ = ps_o[ocnt[0] % 2]
                    ocnt[0] += 1
                    rows = 65 if hl == 0 else 128
                    for bi, (_, qsrc, q0, ta, tb) in enumerate(grp):
                        for t_, tix in enumerate((ta, tb)):
                            ksrc, vsrc, vcol, nk = tiles[tix]
                            lh = VT[0:nk, tix, 0:65] if hl == 0 else VT[0:nk, tix, 65:193]
                            P.op("tensor", "matmul", out=op_[0:rows, QB * bi:QB * bi + QB], lhsT=lh,
                                 rhs=ptt[0:nk, W * bi + QB * t_:W * bi + QB * t_ + QB],
                                 start=(t_ == 0), stop=(t_ == 1))
                    a = acc[hl]
                    if r == 1:
                        g0 = grp[0][2]
                        P.op("scalar", "activation", out=a[0:rows, g0:g0 + 256], in_=op_[0:rows, 0:256],
                             func=AF.Copy)
                    elif r == 4:
                        p = grp[0][3] // 3
                        dst = a.h[0:rows, :].rearrange("r (l p) -> r p l", p=4)[:, p, :]
                        P.op("vector", "tensor_tensor", out=a.v(dst), in0=a.v(dst), in1=op_[0:rows, 0:256],
                             op=ALU.add)
                    else:
                        p0 = 8 * hf + grp[0][3] // 2
                        dst = a.h[0:rows, :].rearrange("r (l p) -> r p l", p=16)[:, p0:p0 + 4, :]
                        P.op("vector", "tensor_tensor", out=a.v(dst), in0=a.v(dst),
                             in1=op_.v(op_.h[0:rows, 0:256].rearrange("r (p l) -> r p l", p=4)), op=ALU.add)
                pend = []
                for grp in groups:
                    ctx = stage1(grp)
                    pend.append((grp, ctx))
                    if len(pend) > 2:
                        stage2(*pend.pop(0))
                while pend:
                    stage2(*pend.pop(0))
    if NCH > 0:
        norm_chunk(NCH - 1)
    P.barrier()
    RW.ptr = m


Builder.attn_setup = _attn_setup
Builder.phase_window = _phase_window
Builder.phase_attn = _phase_attn


C_TRIF, C_TRIB, C_TRIRF, C_TRIRB, C_CI = 256, 384, 512, 640, 768
C_N2 = 772


def build_consts2():
    c = np.zeros((128, C_N2), np.float32)
    c[:, :C_N] = build_consts()
    s = np.arange(128)[:, None]
    t = np.arange(128)[None, :]
    same = (s // 64) == (t // 64)
    c[:, C_TRIF:C_TRIF + 128] = (same & (s <= t))
    c[:, C_TRIB:C_TRIB + 128] = (same & (s >= t))
    c[:, C_TRIRF:C_TRIRF + 128] = (same & (s > t))
    c[:, C_TRIRB:C_TRIRB + 128] = (same & (s < t))
    c[:, C_CI] = (np.arange(128) < 64)
    c[:, C_CI + 1] = (np.arange(128) >= 64)
    return c


def _cfv(self, c0, n, rows=slice(0, 128)):
    return self.cf.v(self.cf.h[rows, c0:c0 + n])


def _norm_tile(self, src_ap, x, sq, rs, dst_fn, nt, pp=None):
    P = self.P
    P.dma("sync", out=x[:, :, :], in_=src_ap)
    P.op("scalar", "activation", out=sq[:, :, :], in_=x[:, :, :], func=AF.Square)
    if pp is None:
        pp = self.ps[6]
    for kc in range(KC):
        P.op("tensor", "matmul", out=pp[:, 0:nt], lhsT=self.ones_b, rhs=sq[:, kc, :],
             start=(kc == 0), stop=(kc == KC - 1))
    P.op("scalar", "activation", out=rs[:, :], in_=pp[:, 0:nt], func=AF.Sqrt, bias=self.epsc[:, 0:1],
         scale=1.0 / D)
    P.op("vector", "reciprocal", out=rs[:, :], in_=rs[:, :])
    for kc in range(KC):
        P.op("vector", "scalar_tensor_tensor", out=dst_fn(kc), in0=x[:, kc, :],
             scalar=self.nwq[:, kc:kc + 1], in1=rs[:, :], op0=ALU.mult, op1=ALU.mult)


def _gate_l(self, lrT, gw, gb, l_out, pz, ework):
    P = self.P
    P.op("tensor", "matmul", out=pz[:, :], lhsT=lrT, rhs=gw[:, :], start=True, stop=True)
    P.op("scalar", "activation", out=ework[:, :], in_=pz[:, :], func=AF.Exp, scale=-1.0)
    P.op("scalar", "activation", out=l_out, in_=ework[:, :], func=AF.Ln, bias=self.onec[:, 0:1], scale=1.0)


def _phase_gla_others(self, SF, SB):
    P = self.P
    ps = self.ps
    RA, RW, RC = self.RA, self.RB, self.RC
    RA.reset()
    RW.reset()
    RC.reset()
    NT = 256
    Wgk = P.sbuf("Wgk", [128, KC, 512], BF16, RA)
    Wgv = P.sbuf("Wgv", [128, KC, 1024], BF16, RA)
    self.Wgk, self.Wgv = Wgk, Wgv
    wsrc = self.w_in.h.rearrange("(c p) n -> p c n", p=128)
    for q in range(2):
        P.dma("gpsimd", out=Wgk[:, 8 * q:8 * q + 8, :], in_=self.w_in.v(wsrc[:, 8 * q:8 * q + 8, 3584:4096]))
    for q in range(4):
        P.dma("gpsimd", out=Wgv[:, 4 * q:4 * q + 4, :], in_=self.w_in.v(wsrc[:, 4 * q:4 * q + 4, 4096:5120]))
    xs = [P.sbuf(f"oxs{i}", [128, KC, NT], F32, RW) for i in range(2)]
    sqs = [P.sbuf(f"osq{i}", [128, KC, NT], BF16, RW) for i in range(2)]
    rss = [P.sbuf(f"ors{i}", [128, NT], F32, RW) for i in range(2)]
    xn = [P.sbuf(f"oxn{i}", [128, KC, NT], BF16, RW) for i in range(3)]
    wlr = [P.sbuf(f"owlr{i}", [128, KC, 16], BF16, RC) for i in range(3)]
    gw = [P.sbuf(f"ogw{i}", [17, 512], F32, RC) for i in range(3)]
    gb = [None, None, None]
    lrT = [P.sbuf(f"olrT{i}", [17, 128], F32, RC) for i in range(2)]
    for i in range(2):
        P.op("vector", "memset", ap=lrT[i][:, :], constant=1.0)
    vtm = [P.sbuf(f"ovtm{i}", [128, 1024], BF16, RC) for i in range(2)]
    ew = P.sbuf("oew", [128, 512], F32, RC)
    l = [P.sbuf(f"ol{i}", [128, 512], F32, RC) for i in range(2)]
    wend = P.sbuf("owend", [128, 512], F32, RC)
    kend = [P.sbuf(f"okend{i}", [128, 512], BF16, RC) for i in range(2)]
    dec = [P.sbuf(f"odec{i}", [128, 4, 2], F32, RC) for i in range(2)]
    csum = P.sbuf("ocsum", [128, 4], F32, RC)
    dtot = P.sbuf("odtot", [128, 4], F32, RC)
    dF = P.sbuf("odF", [128, 4], F32, RC)
    Rj_all = P.sbuf("oRj", [128, 4, 256], F32, RC)
    Rj = [Rj_all.alias(f"oRj{h}") for h in range(4)]
    P.op("vector", "memset", ap=SF[:, :, :], constant=0.0)
    P.op("vector", "memset", ap=SB[:, :, :], constant=0.0)
    xsrc = self.xo.h.rearrange("j (c p) t -> j p c t", p=128)
    for j in range(3):
        P.dma("gpsimd", out=wlr[j][:, :, :],
              in_=self.wlr_o.v(self.wlr_o.h[j].rearrange("(c p) n -> p c n", p=128)))
        P.dma("sync", out=gw[j][0:16, :], in_=self.gw_o[j])
        P.dma("sync", out=gw[j][16:17, :], in_=self.gb_o[j])
    subs = [(j, tt, sub) for j in range(3) for tt in range(TOK // NT) for sub in range(NT // 128)]
    state = {"tcnt": 0}

    tiles_ = [(j, tt) for j in range(3) for tt in range(TOK // NT)]

    def stage0(t):
        j, tt = tiles_[t]
        x, xnn = xs[t % 2], xn[t % 3]
        _norm_tile(self, self.xo.v(xsrc[j, :, :, NT * tt:NT * tt + NT]), x, sqs[t % 2], rss[t % 2],
                   lambda kc: xnn[:, kc, :], NT, ps[3])

    def stage1(idx):
        j, tt, sub = subs[idx]
        xnn = xn[(idx // 2) % 3]
        tsl = slice(128 * sub, 128 * sub + 128)
        pk = ps[idx % 2]
        for kc in range(KC):
            P.op("tensor", "matmul", out=pk[:, :], lhsT=xnn[:, kc, tsl], rhs=Wgk[:, kc, :],
                 start=(kc == 0), stop=(kc == KC - 1))
        vt = vtm[idx % 2]
        for hv in range(2):
            pv = ps[2]
            for kc in range(KC):
                P.op("tensor", "matmul", out=pv[:, :], lhsT=xnn[:, kc, tsl],
                     rhs=Wgv[:, kc, 512 * hv:512 * hv + 512], start=(kc == 0), stop=(kc == KC - 1))
            P.op("scalar", "activation", out=vt[:, 512 * hv:512 * hv + 512], in_=pv[:, :], func=AF.Copy)
        pl = ps[4]
        for kc in range(KC):
            P.op("tensor", "matmul", out=pl[0:16, 0:128], lhsT=wlr[j][:, kc, :], rhs=xnn[:, kc, tsl],
                 start=(kc == 0), stop=(kc == KC - 1))
        P.op("vector", "tensor_copy", out=lrT[idx % 2][0:16, :], in_=pl[0:16, 0:128])

    def stage2(idx):
        j, tt, sub = subs[idx]
        pk = ps[idx % 2]
        vt = vtm[idx % 2]
        lt = l[idx % 2]
        ke = kend[idx % 2]
        de = dec[idx % 2]
        if tt == 0 and sub == 0:
            for h in range(4):
                P.op("vector", "memset", ap=Rj[h][:, h, :], constant=0.0)
            P.op("vector", "memset", ap=csum[:, :], constant=0.0)
        _gate_l(self, lrT[idx % 2][:, :], gw[j], gb[j], lt[:, :], ps[3], ew)
        pr = ps[5]
        P.op("tensor", "matmul", out=pr[:, :], lhsT=_cfv(self, C_TRIRF, 128), rhs=lt[:, :], start=True, stop=True)
        P.op("scalar", "activation", out=wend[:, :], in_=pr[:, :], func=AF.Exp, scale=-1.0 / 16)
        P.op("vector", "tensor_tensor", out=ke[:, :], in0=wend[:, :], in1=pk[:, :], op=ALU.mult)
        pc = ps[4]
        for h in range(4):
            P.op("tensor", "matmul", out=pc[:, 256 + 2 * h:256 + 2 * h + 2], lhsT=lt[:, 128 * h:128 * h + 128],
                 rhs=_cfv(self, C_CI, 2), start=True, stop=True)
        pcv = pc.v(pc.h[:, 256:264].rearrange("d (h c) -> d h c", h=4))
        P.op("scalar", "activation", out=de[:, :, :], in_=pcv, func=AF.Exp, scale=-1.0 / 16)
        P.op("vector", "tensor_reduce", out=dtot[:, :], in_=pcv, axis=AX.X, op=ALU.add)
        P.op("vector", "tensor_tensor", out=csum[:, :], in0=csum[:, :], in1=dtot[:, :], op=ALU.add)
        for ci in range(2):
            cp = slice(64 * ci, 64 * ci + 64)
            for h in range(4):
                pu = ps[5 + h % 2]
                P.op("tensor", "matmul", out=pu[:, 0:256], lhsT=ke[cp, 128 * h:128 * h + 128],
                     rhs=vt[cp, 256 * h:256 * h + 256], start=True, stop=True)
                P.op("vector", "scalar_tensor_tensor", out=Rj[h][:, h, :], in0=Rj[h][:, h, :],
                     scalar=de[:, h, ci:ci + 1], in1=pu[:, 0:256], op0=ALU.mult, op1=ALU.add)
        if tt == TOK // NT - 1 and sub == NT // 128 - 1:
            P.op("scalar", "activation", out=dtot[:, :], in_=csum[:, :], func=AF.Exp, scale=-1.0 / 16)
            for (S, fc) in ((SF, j), (SB, 3 + j)):
                P.op("vector", "tensor_scalar", out=dF[:, :], in0=dtot[:, :], scalar1=-1.0,
                     scalar2=self.flg[:, fc:fc + 1], op0=ALU.add, op1=ALU.mult)
                P.op("vector", "tensor_scalar", out=dF[:, :], in0=dF[:, :], scalar1=1.0, scalar2=None,
                     op0=ALU.add)
                for h in range(4):
                    P.op("vector", "tensor_scalar", out=S[:, h, :], in0=S[:, h, :], scalar1=dF[:, h:h + 1],
                         scalar2=None, op0=ALU.mult)
                    P.op("vector", "scalar_tensor_tensor", out=S[:, h, :], in0=Rj[h][:, h, :],
                         scalar=self.flg[:, fc:fc + 1], in1=S[:, h, :], op0=ALU.mult, op1=ALU.add)

    stage0(0)
    stage0(1)
    stage1(0)
    for idx in range(len(subs)):
        if idx % 2 == 0 and idx // 2 + 2 < len(tiles_):
            stage0(idx // 2 + 2)
        if idx + 1 < len(subs):
            stage1(idx + 1)
        stage2(idx)
    P.barrier()
    RW.reset()
    RC.reset()


Builder.phase_gla_others = _phase_gla_others


def _phase_gla_own(self, SF, SB, glaT):
    P = self.P
    ps = self.ps
    RA, RW, RC = self.RA, self.RB, self.RC
    RW.reset()
    RC.reset()
    Wgk, Wgv = self.Wgk, self.Wgv
    NT = 256
    wsrc = self.w_in.h.rearrange("(c p) n -> p c n", p=128)
    gqT = P.sbuf("gqT", [128, 4, TOK], BF16, RW)
    gkT = P.sbuf("gkT", [128, 4, TOK], BF16, RW)
    gktm = P.sbuf("gktm", [128, 8, 512], BF16, RW)
    vtm = P.sbuf("gvtm", [128, 8, 1024], BF16, RW)
    sgr = P.sbuf("sgr", [128, 8, 1024], BF16, RW)
    lrT = [P.sbuf(f"glrT{d}", [17, TOK], F32, RW) for d in range(2)]
    gw = [P.sbuf(f"ggw{d}", [17, 512], F32, RW) for d in range(2)]
    gb = [None, None]
    gnw = P.sbuf("gnw", [128, 1024], F32, RW)
    xn = P.sbuf("gxn", [128, KC, TOK], BF16, RC)
    wq = [P.sbuf(f"gwq{i}", [128, KC, 128], BF16, RC) for i in range(1)]
    wlr = P.sbuf("gwlr", [128, KC, 32], BF16, RC)
    RS = Region(RW.base, 56 * 1024, "stage")
    xs = [P.sbuf(f"gxs{i}", [128, KC, NT], F32, RS) for i in range(3)]
    sqs = [P.sbuf("gsq0", [128, KC, NT], BF16, RS)] * 2
    rss = [P.sbuf(f"grs{i}", [128, NT], F32, RC) for i in range(2)]
    P.dma("sync", out=gw[0][0:16, :], in_=self.gwf[:, :])
    P.dma("sync", out=gw[1][0:16, :], in_=self.gwb[:, :])
    P.dma("sync", out=gw[0][16:17, :], in_=self.gbf[:, :])
    P.dma("sync", out=gw[1][16:17, :], in_=self.gbb[:, :])
    P.dma("sync", out=gnw[:, :], in_=self.gnwd[:, :])
    P.dma("gpsimd", out=wlr[:, :, :], in_=self.w_in.v(wsrc[:, :, 6144:6176]))
    xsrc = self.xw.h[:, 1024:2048].rearrange("(c p) t -> p c t", p=128)
    for tt in range(TOK // NT):
        _norm_tile(self, self.xw.v(xsrc[:, :, NT * tt:NT * tt + NT]), xs[tt % 3], sqs[tt % 2], rss[tt % 2],
                   lambda kc: xn[:, kc, NT * tt:NT * tt + NT], NT, ps[tt % 2])
    P.barrier()
    for d in range(2):
        P.op("vector", "memset", ap=lrT[d][:, :], constant=1.0)
    pc_ = [0]

    def nxt():
        pc_[0] += 1
        return ps[pc_[0] % 4]
    for h in range(4):
        w = wq[0]
        P.dma("gpsimd", out=w[:, :, :], in_=self.w_in.v(wsrc[:, :, 3072 + 128 * h:3072 + 128 * h + 128]))
        for tt in range(2):
            pp = nxt()
            for kc in range(KC):
                P.op("tensor", "matmul", out=pp[:, :], lhsT=w[:, kc, :], rhs=xn[:, kc, 512 * tt:512 * tt + 512],
                     start=(kc == 0), stop=(kc == KC - 1))
            P.op("scalar", "activation", out=gqT[:, h, 512 * tt:512 * tt + 512], in_=pp[:, :], func=AF.Copy)
        for tt in range(2):
            pp = nxt()
            for kc in range(KC):
                P.op("tensor", "matmul", out=pp[:, :], lhsT=Wgk[:, kc, 128 * h:128 * h + 128],
                     rhs=xn[:, kc, 512 * tt:512 * tt + 512], start=(kc == 0), stop=(kc == KC - 1))
            P.op("vector", "tensor_copy", out=gkT[:, h, 512 * tt:512 * tt + 512], in_=pp[:, :])
    for d in range(2):
        for tt in range(2):
            pp = nxt()
            for kc in range(KC):
                P.op("tensor", "matmul", out=pp[0:16, :], lhsT=wlr[:, kc, 16 * d:16 * d + 16],
                     rhs=xn[:, kc, 512 * tt:512 * tt + 512], start=(kc == 0), stop=(kc == KC - 1))
            P.op("vector", "tensor_copy", out=lrT[d][0:16, 512 * tt:512 * tt + 512], in_=pp[0:16, :])
    for st in range(8):
        tsl = slice(128 * st, 128 * st + 128)
        pp = nxt()
        for kc in range(KC):
            P.op("tensor", "matmul", out=pp[:, :], lhsT=xn[:, kc, tsl], rhs=Wgk[:, kc, :],
                 start=(kc == 0), stop=(kc == KC - 1))
        P.op("vector", "tensor_copy", out=gktm[:, st, :], in_=pp[:, :])
        for hv in range(2):
            pp = nxt()
            for kc in range(KC):
                P.op("tensor", "matmul", out=pp[:, :], lhsT=xn[:, kc, tsl], rhs=Wgv[:, kc, 512 * hv:512 * hv + 512],
                     start=(kc == 0), stop=(kc == KC - 1))
            P.op("scalar", "activation", out=vtm[:, st, 512 * hv:512 * hv + 512], in_=pp[:, :], func=AF.Copy)
    P.barrier()
    RA.reset()
    Wgr = P.sbuf("Wgr", [128, KC, 512], BF16, RA)
    for hv in range(2):
        for q in range(2):
            P.dma("gpsimd", out=Wgr[:, 8 * q:8 * q + 8, :],
                  in_=self.w_in.v(wsrc[:, 8 * q:8 * q + 8, 5120 + 512 * hv:5120 + 512 * hv + 512]))
        for st in range(8):
            tsl = slice(128 * st, 128 * st + 128)
            pp = nxt()
            for kc in range(KC):
                P.op("tensor", "matmul", out=pp[:, :], lhsT=xn[:, kc, tsl], rhs=Wgr[:, kc, :],
                     start=(kc == 0), stop=(kc == KC - 1))
            P.op("scalar", "activation", out=sgr[:, st, 512 * hv:512 * hv + 512], in_=pp[:, :], func=AF.Silu)
    P.barrier()
    RA.reset()
    RC.reset()
    RW = RC
    lg = [P.sbuf(f"lg{d}", [128, 8, 512], F32, RA) for d in range(2)]
    ew = P.sbuf("gew", [128, 512], F32, RW)
    for d in range(2):
        for st in range(8):
            _gate_l(self, lrT[d][:, 128 * st:128 * st + 128], gw[d], gb[d], lg[d][:, st, :], ps[st % 2], ew)
    eq = P.sbuf("geq", [128, 512], F32, RA)
    ek = P.sbuf("gek", [128, 512], F32, RA)
    qin = [P.sbuf(f"gqin{d}", [128, TOK], BF16, RA) for d in range(2)]
    kin = [P.sbuf(f"gkin{d}", [128, TOK], BF16, RA) for d in range(2)]
    wend = P.sbuf("gwend", [128, 128], F32, RW)
    kend = [P.sbuf(f"gkend{d}", [128, 8, 128], BF16, RW) for d in range(2)]
    dec = [P.sbuf(f"gdec{d}", [128, 16], F32, RW) for d in range(2)]
    Scur = [[P.sbuf(f"gScur{d}{i}", [128, 256], F32, RW) for i in range(2)] for d in range(2)]
    Sall = [P.sbuf(f"gSall{d}", [128, 16, 256], BF16, RW) for d in range(2)]
    Am = [[P.sbuf(f"gAm{d}{i}", [128, 128], BF16, RW) for i in range(2)] for d in range(2)]
    ot = [P.sbuf(f"got{i}", [128, 256], F32, RW) for i in range(2)]
    og = [P.sbuf(f"gog{i}", [128, 256], BF16, RW) for i in range(2)]
    ssum = [P.sbuf(f"gssum{i}", [128, 1], F32, RW) for i in range(2)]
    junk = P.sbuf("gjunk", [128, 256], F32, RW)
    TRI = (C_TRIF, C_TRIB)
    TRIR = (C_TRIRF, C_TRIRB)
    for h in range(4):
        hc = slice(128 * h, 128 * h + 128)
        for d in (1, 0):
            for g4 in range(2):
                pcs = ps[g4]
                for i in range(4):
                    st = 4 * g4 + i
                    P.op("tensor", "matmul", out=pcs[:, 128 * i:128 * i + 128], lhsT=lg[d][:, st, hc],
                         rhs=_cfv(self, TRI[d], 128), start=True, stop=True)
                ts4 = slice(512 * g4, 512 * g4 + 512)
                P.op("scalar", "activation", out=eq[:, :], in_=pcs[:, :], func=AF.Exp, scale=-1.0 / 16)
                P.op("scalar", "activation", out=ek[:, :], in_=pcs[:, :], func=AF.Exp, scale=1.0 / 16)
                P.op("vector", "scalar_tensor_tensor", out=qin[d][:, ts4], in0=gqT[:, h, ts4],
                     scalar=float(128 ** -0.5), in1=eq[:, :], op0=ALU.mult, op1=ALU.mult)
                P.op("vector", "tensor_tensor", out=kin[d][:, ts4], in0=gkT[:, h, ts4], in1=ek[:, :], op=ALU.mult)
            pdc = ps[2]
            for st in range(8):
                P.op("tensor", "matmul", out=pdc[:, 2 * st:2 * st + 2], lhsT=lg[d][:, st, hc],
                     rhs=_cfv(self, C_CI, 2), start=True, stop=True)
            P.op("scalar", "activation", out=dec[d][:, :], in_=pdc[:, 0:16], func=AF.Exp, scale=-1.0 / 16)
            for st in range(8):
                pr = ps[3 + st % 2]
                P.op("tensor", "matmul", out=pr[:, 0:128], lhsT=_cfv(self, TRIR[d], 128), rhs=lg[d][:, st, hc],
                     start=True, stop=True)
                P.op("scalar", "activation", out=wend[:, :], in_=pr[:, 0:128], func=AF.Exp, scale=-1.0 / 16)
                P.op("vector", "tensor_tensor", out=kend[d][:, st, :], in0=gktm[:, st, hc], in1=wend[:, :],
                     op=ALU.mult)
        order = {0: list(range(16)), 1: list(range(15, -1, -1))}
        P.op("vector", "tensor_copy", out=Scur[0][0][:, :], in_=SF[:, h, :])
        P.op("vector", "tensor_copy", out=Scur[1][0][:, :], in_=SB[:, h, :])
        for step in range(16):
            for d in (1, 0):
                n = order[d][step]
                st, ci = divmod(n, 2)
                cp = slice(64 * ci, 64 * ci + 64)
                sc, sn = Scur[d][step % 2], Scur[d][(step + 1) % 2]
                P.op("scalar", "activation", out=Sall[d][:, n, :], in_=sc[:, :], func=AF.Copy)
                pu = ps[3 + 2 * d + step % 2]
                P.op("tensor", "matmul", out=pu[:, 0:256], lhsT=kend[d][cp, st, :],
                     rhs=vtm[cp, st, 256 * h:256 * h + 256], start=True, stop=True)
                P.op("vector", "scalar_tensor_tensor", out=sn[:, :], in0=sc[:, :],
                     scalar=dec[d][:, n:n + 1], in1=pu[:, 0:256], op0=ALU.mult, op1=ALU.add)
        def outA(st):
            tsl = slice(128 * st, 128 * st + 128)
            for d in (1, 0):
                pa = ps[2 * d + st % 2]
                P.op("tensor", "matmul", out=pa[:, 0:128], lhsT=kin[d][:, tsl], rhs=qin[d][:, tsl],
                     start=True, stop=True)
                P.op("vector", "tensor_tensor", out=Am[d][st % 2][:, :], in0=pa[:, 0:128],
                     in1=_cfv(self, TRI[d], 128), op=ALU.mult)
            po = ps[4 + st % 2]
            P.op("tensor", "matmul", out=po[:, 0:256], lhsT=Am[1][st % 2][:, :],
                 rhs=vtm[:, st, 256 * h:256 * h + 256], start=True, stop=False)
            P.op("tensor", "matmul", out=po[:, 0:256], lhsT=Am[0][st % 2][:, :],
                 rhs=vtm[:, st, 256 * h:256 * h + 256], start=False, stop=False)
            for d in (1, 0):
                for ci in range(2):
                    n = 2 * st + ci
                    P.op("tensor", "matmul", out=po[64 * ci:64 * ci + 64, 0:256],
                         lhsT=qin[d][:, 128 * st + 64 * ci:128 * st + 64 * ci + 64], rhs=Sall[d][:, n, :],
                         start=False, stop=(d == 0))

        def outB(st):
            tsl = slice(128 * st, 128 * st + 128)
            po = ps[4 + st % 2]
            o_, g_, s_ = ot[st % 2], og[st % 2], ssum[st % 2]
            P.op("scalar", "activation", out=junk[:, :], in_=po[:, 0:256], func=AF.Square, accum_out=s_[:, :])
            P.op("scalar", "activation", out=s_[:, :], in_=s_[:, :], func=AF.Sqrt, bias=self.epsc[:, 0:1],
                 scale=1.0 / 256)
            P.op("vector", "reciprocal", out=s_[:, :], in_=s_[:, :])
            P.op("vector", "scalar_tensor_tensor", out=o_[:, :], in0=po[:, 0:256], scalar=s_[:, 0:1],
                 in1=gnw[:, 256 * h:256 * h + 256], op0=ALU.mult, op1=ALU.mult)
            P.op("vector", "tensor_tensor", out=g_[:, :], in0=o_[:, :], in1=sgr[:, st, 256 * h:256 * h + 256],
                 op=ALU.mult)
            for i in range(2):
                P.op("tensor", "transpose", out=self.ps_t[:, 256 * (st % 2) + 128 * i:256 * (st % 2) + 128 * i + 128],
                     in_=g_[:, 128 * i:128 * i + 128], identity=self.ident_b)
            P.op("vector", "tensor_copy", out=glaT[:, 2 * h:2 * h + 2, tsl],
                 in_=self.ps_t.v(self.ps_t.h[:, 256 * (st % 2):256 * (st % 2) + 256].rearrange("f (i t) -> f i t", i=2)))

        outA(0)
        for st in range(8):
            if st + 1 < 8:
                outA(st + 1)
            outB(st)
    P.barrier()


Builder.phase_gla_own = _phase_gla_own


def _norm_layout(w):
    return np.ascontiguousarray(np.asarray(w, np.float32).reshape(KC, 128).T)


def shared_inputs(inp):
    f = lambda a: np.ascontiguousarray(np.asarray(a, np.float32))
    nw = np.concatenate([_norm_layout(inp["attn_norm_w"][0]), _norm_layout(inp["ffn_norm_w"][0]),
                         _norm_layout(inp["final_norm_w"])], axis=1)
    return {
        "nw": nw, "cst": build_consts2(), "w_out": f(inp["w_out"][0]), "w_gate": f(inp["w_gate"][0]),
        "w_up": f(inp["w_up"][0]), "w_down": f(inp["w_down"][0]), "w_in": f(inp["w_in"][0]),
        "relt": f(inp["rel_bias_table"]), "ohd": build_onehot(),
        "gwf": f(inp["gla_gate_w_fwd"][0]), "gwb": f(inp["gla_gate_w_bwd"][0]),
        "gbf": f(inp["gla_gate_b_fwd"][0])[None, :], "gbb": f(inp["gla_gate_b_bwd"][0])[None, :],
        "gnwd": np.ascontiguousarray(np.broadcast_to(f(inp["gla_norm_w"][0])[None, :], (128, 1024))),
    }


def core_inputs(inp, c):
    b, g = divmod(c, 4)
    start = TOK * g
    x = np.asarray(inp["x"][b], np.float32)
    w_in = np.asarray(inp["w_in"][0], np.float32)
    xw = np.zeros((D, WIN), np.float32)
    lo, hi = max(0, start - 1024), min(SEQ, start + 2048)
    xw[:, lo - (start - 1024):hi - (start - 1024)] = x[lo:hi].T
    slots = [(blk, 0) for blk in range(g)] + [(blk, 1) for blk in range(3, g, -1)]
    xo = np.zeros((3, D, TOK), np.float32)
    wlr_o = np.zeros((3, D, 16), np.float32)
    gw_o = np.zeros((3, 16, 512), np.float32)
    gb_o = np.zeros((3, 1, 512), np.float32)
    flg = np.zeros((128, 8), np.float32)
    for j, (blk, isb) in enumerate(slots):
        xb = x[TOK * blk:TOK * blk + TOK]
        if isb:
            xb = xb[::-1]
        xo[j] = xb.T
        wlr_o[j] = w_in[:, 6160:6176] if isb else w_in[:, 6144:6160]
        gw_o[j] = inp["gla_gate_w_bwd"][0] if isb else inp["gla_gate_w_fwd"][0]
        gb_o[j, 0] = inp["gla_gate_b_bwd"][0] if isb else inp["gla_gate_b_fwd"][0]
        flg[:, 3 * isb + j] = 1.0
    return {"xw": xw, "xo": xo, "wlr_o": wlr_o, "gw_o": gw_o, "gb_o": gb_o, "flgd": flg,
            "valtd": build_valt(start)}


_CACHE = {}


def kernel(**inputs):
    if "nc" not in _CACHE:
        b = Builder(stage="full")
        _CACHE["nc"] = b.build()
    nc = _CACHE["nc"]
    sh = shared_inputs(inputs)
    in_maps = []
    for c in range(8):
        m = dict(sh)
        m.update(core_inputs(inputs, c))
        in_maps.append(m)
    res = run_bass_kernel_spmd(nc, in_maps, core_ids=list(range(8)))
    out = np.zeros((NB, SEQ, D), np.float32)
    for c in range(8):
        b, g = divmod(c, 4)
        out[b, TOK * g:TOK * g + TOK, :] = np.asarray(res.results[c]["out"]).T
    return out
```

```python
import numpy as np
import concourse.bass as bass
import concourse.mybir as mybir
from concourse.bass_utils import run_bass_kernel_spmd

F32 = mybir.dt.float32
BF16 = mybir.dt.bfloat16
AF = mybir.ActivationFunctionType
ALU = mybir.AluOpType
AX = mybir.AxisListType

ENGS = ("tensor", "vector", "scalar", "gpsimd", "sync")
SAME_ENG_ALL = False


ALL_TRKS = []


class Trk:
    __slots__ = ("last_w", "readers", "dma_sem", "dma_cnt", "name", "excl")

    def __init__(self, name):
        ALL_TRKS.append(self)
        self.name = name
        self.excl = False
        self.last_w = None
        self.readers = {}
        self.dma_sem = None
        self.dma_cnt = 0


class V:
    __slots__ = ("ap", "trk")

    def __init__(self, ap, trk):
        self.ap = ap
        self.trk = trk


class T:
    def __init__(self, h, name, trk=None):
        self.h = h
        self.name = name
        self.trk = trk if trk is not None else Trk(name)

    def __getitem__(self, idx):
        return V(self.h[idx], self.trk)

    def alias(self, name):
        return T(self.h, name, Trk(name))

    def v(self, ap):
        return V(ap, self.trk)


class Region:
    def __init__(self, base, size, name=""):
        self.base, self.size, self.ptr, self.name = base, size, base, name

    def alloc(self, nb, what=""):
        off = (self.ptr + 31) // 32 * 32
        assert off + nb <= self.base + self.size, (self.name, what, nb, off - self.base, self.size)
        self.ptr = off + nb
        return off

    def sub(self, size, name=""):
        off = self.alloc(size, name)
        return Region(off, size, name)

    def reset(self):
        self.ptr = self.base


SB_LO, SB_HI = 16512, 229344


class Ins:
    __slots__ = ("eng", "fn", "kw", "deps", "idx", "milestone", "count", "dma", "waits")

    def __init__(self, eng, fn, kw, deps, idx, dma=None):
        self.eng = eng
        self.fn = fn
        self.kw = kw
        self.deps = deps
        self.idx = idx
        self.milestone = False
        self.count = 0
        self.dma = dma
        self.waits = []


class Prog:
    def __init__(self, nc):
        ALL_TRKS.clear()
        self.nc = nc
        self.q = {e: [] for e in ENGS}
        self.sems = {}
        self.n_dma_sems = 0

    def sbuf(self, name, shape, dt, reg=None):
        nb = int(np.prod(shape[1:])) * (4 if dt == F32 else 2)
        off = reg.alloc(nb, name)
        self.nsb = getattr(self, "nsb", 0) + 1
        return T(self.nc.alloc_sbuf_tensor_at(f"{name}_{self.nsb}", list(shape), dt, offset=off), name)

    def psum(self, name, shape, dt=F32):
        t = T(self.nc.alloc_psum_tensor(name, list(shape), dt), name)
        t.trk.excl = True
        return t

    def dram(self, name, shape, dt, kind="Internal"):
        return T(self.nc.dram_tensor(name, list(shape), dt, kind=kind), name)

    def _collect(self, eng, reads, writes, same_eng_raw=True):
        deps = []
        for trk in reads:
            if trk.last_w is not None:
                deps.append(trk.last_w)
            if trk.excl:
                deps.extend(ev for key, ev in trk.readers.items() if key != eng)
        for trk in writes:
            if trk.last_w is not None:
                deps.append(trk.last_w)
            deps.extend(trk.readers.values())
        out = []
        for d in deps:
            if d[0] == "eng" and d[1] == eng:
                if eng == "tensor":
                    continue
                if not same_eng_raw:
                    continue
                if not SAME_ENG_ALL:
                    is_raw = any(t.last_w is d for t in reads)
                    if not is_raw:
                        continue
            out.append(d)
        return out

    def op(self, eng, fn, **kw):
        reads, writes = [], []
        kw2 = {}
        for k, v in kw.items():
            if isinstance(v, V):
                if k in ("out", "accum_out", "ap") or k.startswith("out"):
                    writes.append(v.trk)
                else:
                    reads.append(v.trk)
                kw2[k] = v.ap
            else:
                kw2[k] = v
        if fn == "matmul" and kw.get("start") is False:
            pass
        deps = self._collect(eng, reads, writes)
        ins = Ins(eng, fn, kw2, deps, len(self.q[eng]))
        self.q[eng].append(ins)
        ev = ("eng", eng, ins)
        for t in reads:
            t.readers[eng] = ev
        for t in writes:
            t.last_w = ev
            t.readers = {}
        return ins

    def dma(self, queue, out, in_, extra_w=(), **kw):
        deps = self._collect(queue, [in_.trk], [out.trk] + list(extra_w), same_eng_raw=True)
        trk = out.trk
        if trk.dma_sem is None:
            trk.dma_sem = self.nc.alloc_semaphore(name=f"dsem{self.n_dma_sems}")
            self.n_dma_sems += 1
        trk.dma_cnt += 16
        kw2 = dict(kw)
        kw2["out"] = out.ap
        kw2["in_"] = in_.ap
        ins = Ins(queue, "dma_start", kw2, deps, len(self.q[queue]), dma=(trk.dma_sem, trk.dma_cnt))
        self.q[queue].append(ins)
        ev = ("dma", trk.dma_sem, trk.dma_cnt)
        in_.trk.readers[("d", id(trk.dma_sem))] = ev
        for t in [trk] + list(extra_w):
            t.last_w = ev
            t.readers = {}
        return ins

    def coll(self, kind, op, groups, in_, out, sem_name="ccsem"):
        deps = self._collect("gpsimd", [in_.trk], [out.trk])
        trk = out.trk
        if trk.dma_sem is None:
            trk.dma_sem = self.nc.alloc_semaphore(name=f"dsem{self.n_dma_sems}")
            self.n_dma_sems += 1
        trk.dma_cnt += 1
        kw2 = dict(kind=kind, op=op, replica_groups=groups, ins=[in_.ap], outs=[out.ap])
        ins = Ins("gpsimd", "collective_compute", kw2, deps, len(self.q["gpsimd"]),
                  dma=(trk.dma_sem, trk.dma_cnt, 1))
        self.q["gpsimd"].append(ins)
        ev = ("dma", trk.dma_sem, trk.dma_cnt)
        in_.trk.readers[("d", id(trk.dma_sem))] = ev
        trk.last_w = ev
        trk.readers = {}
        return ins

    def wait_all(self, eng, trks):
        deps = [t.last_w for t in trks if t.last_w is not None]
        ins = Ins(eng, None, {}, deps, len(self.q[eng]))
        self.q[eng].append(ins)
        return ins

    def barrier(self):
        deps = []
        for t in ALL_TRKS:
            if t.last_w is not None:
                deps.append(t.last_w)
            deps.extend(t.readers.values())
        for e in ENGS:
            ins = Ins(e, None, {}, list(deps), len(self.q[e]))
            self.q[e].append(ins)
        for t in ALL_TRKS:
            t.last_w = None
            t.readers = {}

    def emit(self):
        nc = self.nc
        for e in ENGS:
            for ins in self.q[e]:
                for d in ins.deps:
                    if d[0] == "eng":
                        d[2].milestone = True
        for e in ENGS:
            c = 0
            for ins in self.q[e]:
                if ins.milestone:
                    c += 1
                ins.count = c
        esem = {e: nc.alloc_semaphore(name=f"esem_{e}") for e in ENGS}
        for e in ENGS:
            seen = {}
            for ins in self.q[e]:
                need = {}
                for d in ins.deps:
                    if d[0] == "eng":
                        key, sem, val = ("e", d[1]), esem[d[1]], d[2].count
                    else:
                        key, sem, val = ("d", id(d[1])), d[1], d[2]
                    if seen.get(key, 0) >= val:
                        continue
                    if key not in need or need[key][1] < val:
                        need[key] = (sem, val)
                for key, (sem, val) in need.items():
                    seen[key] = val
                ins.waits = list(need.values())
        stats = {e: (len(self.q[e]), sum(len(i.waits) for i in self.q[e]),
                     sum(1 for i in self.q[e] if i.milestone)) for e in ENGS}
        self.stats = stats

        def replay(e):
            def body(engine):
                for ins in self.q[e]:
                    for sem, val in ins.waits:
                        engine.wait_ge(sem, val)
                    if ins.fn is None:
                        continue
                    r = getattr(engine, ins.fn)(**ins.kw)
                    if ins.dma is not None:
                        if len(ins.dma) == 3:
                            r.then_inc(ins.dma[0], 1)
                        else:
                            r.then_inc(ins.dma[0], 16)
                    elif ins.milestone:
                        r.then_inc(esem[e], 1)
            return body

        with nc.Block() as block:
            block.tensor(replay("tensor"))
            block.vector(replay("vector"))
            block.scalar(replay("scalar"))
            block.gpsimd(replay("gpsimd"))
            block.sync(replay("sync"))
        return stats


D = 2048
SEQ = 4096
NB = 2
TOK = 1024
WIN = 3072
KC = 16
FH = 5632
FC = 44
EPS = 1e-6
INW = 6176
SQD = float(np.sqrt(D))

C_ONES = 0
C_IDENT = 128
C_N = 256


def build_consts():
    c = np.zeros((128, C_N), np.float32)
    c[:, C_ONES:C_ONES + 128] = 1.0
    c[:, C_IDENT:C_IDENT + 128] = np.eye(128, dtype=np.float32)
    return c


class Builder:
    def __init__(self, stage="full"):
        self.stage = stage
        self.nc = bass.Bass("TRN2", target_bir_lowering=False)
        self.P = Prog(self.nc)
        P = self.P
        self.xw = P.dram("xw", [D, WIN], F32, kind="ExternalInput")
        self.nw = P.dram("nw", [128, 48], F32, kind="ExternalInput")
        self.cst = P.dram("cst", [128, C_N2], F32, kind="ExternalInput")
        self.xo = P.dram("xo", [3, D, TOK], F32, kind="ExternalInput")
        self.wlr_o = P.dram("wlr_o", [3, D, 16], F32, kind="ExternalInput")
        self.gw_o = P.dram("gw_o", [3, 16, 512], F32, kind="ExternalInput")
        self.gb_o = P.dram("gb_o", [3, 1, 512], F32, kind="ExternalInput")
        self.flgd = P.dram("flgd", [128, 8], F32, kind="ExternalInput")
        self.gwf = P.dram("gwf", [16, 512], F32, kind="ExternalInput")
        self.gwb = P.dram("gwb", [16, 512], F32, kind="ExternalInput")
        self.gbf = P.dram("gbf", [1, 512], F32, kind="ExternalInput")
        self.gbb = P.dram("gbb", [1, 512], F32, kind="ExternalInput")
        self.gnwd = P.dram("gnwd", [128, 1024], F32, kind="ExternalInput")
        self.w_out = P.dram("w_out", [D, D], F32, kind="ExternalInput")
        self.w_gate = P.dram("w_gate", [D, FH], F32, kind="ExternalInput")
        self.w_up = P.dram("w_up", [D, FH], F32, kind="ExternalInput")
        self.w_down = P.dram("w_down", [FH, D], F32, kind="ExternalInput")
        self.out = P.dram("out", [D, TOK], F32, kind="ExternalOutput")
        if stage == "ffn":
            self.mixd = P.dram("mixd", [D, TOK], F32, kind="ExternalInput")
        self.w_in = P.dram("w_in", [D, INW], F32, kind="ExternalInput")
        self.relt = P.dram("relt", [32, 16], F32, kind="ExternalInput")
        self.ohd = P.dram("ohd", [33, 3, 512], F32, kind="ExternalInput")
        self.valtd = P.dram("valtd", [128, 53], F32, kind="ExternalInput")
        self.uvd = P.dram("uvd", [16, 3, 512], F32)
        if stage in ("attn", "gla"):
            self.dbg = P.dram("dbg", [128, 8, TOK], BF16, kind="ExternalOutput")
        if stage == "gla":
            self.dbg2 = P.dram("dbg2", [128, 2, 4, 256], F32, kind="ExternalOutput")
        top = Region(SB_LO, SB_HI - SB_LO, "top")
        self.R_const = top.sub(5 * 1024, "const")
        self.R_mix = top.sub(32 * 1024, "mix")
        self.R_work = top.sub(SB_HI - top.ptr - 64, "work")
        wb_ = self.R_work.base
        self.RA = Region(wb_, 48 * 1024 + 64, "RA")
        self.RB = Region(wb_ + 48 * 1024 + 64, 78 * 1024, "RB")
        self.RC = Region(self.RB.base + 78 * 1024, self.R_work.base + self.R_work.size - (self.RB.base + 78 * 1024), "RC")
        RC = self.R_const
        self.cf = P.sbuf("cf", [128, C_N2], F32, RC)
        self.cb = P.sbuf("cb", [128, C_N], BF16, RC)
        self.flg = P.sbuf("flg", [128, 8], F32, RC)
        self.onec = P.sbuf("onec", [128, 1], F32, RC)
        P.dma("sync", out=self.flg[:, :], in_=self.flgd[:, :])
        P.op("vector", "memset", ap=self.onec[:, :], constant=1.0)
        self.nws = P.sbuf("nws", [128, 48], F32, RC)
        self.nwq = P.sbuf("nwq", [128, 48], F32, RC)
        P.dma("sync", out=self.cf[:, :], in_=self.cst[:, :])
        P.dma("sync", out=self.nws[:, :], in_=self.nw[:, :])
        P.op("vector", "tensor_copy", out=self.cb[:, :], in_=self.cf[:, 0:C_N])
        P.op("vector", "tensor_copy", out=self.nwq[:, :], in_=self.nws[:, :])
        self.epsc = P.sbuf("epsc", [128, 1], F32, RC)
        P.op("vector", "memset", ap=self.epsc[:, :], constant=EPS)
        self.ones_b = self.cb.v(self.cb.h[:, C_ONES:C_ONES + 128])
        self.ps = [P.psum(f"ps{i}", [128, 512], F32) for i in range(7)]
        self.ps_t = P.psum("pst", [128, 1024], BF16)
        self.ident_b = self.cb.v(self.cb.h[:, C_IDENT:C_IDENT + 128])

    def rms_rstd(self, srcT, rstd, sq, ps):
        P = self.P
        for h in range(2):
            hs = slice(512 * h, 512 * h + 512)
            for kc in range(KC):
                P.op("scalar", "activation", out=sq[:, kc, :], in_=srcT[kc][:, kc, hs], func=AF.Square)
            for kc in range(KC):
                P.op("tensor", "matmul", out=ps[:, :], lhsT=self.ones_b, rhs=sq[:, kc, :],
                     start=(kc == 0), stop=(kc == KC - 1))
            P.op("scalar", "activation", out=rstd[:, hs], in_=ps[:, :], func=AF.Ln, bias=self.epsc[:, 0:1],
                 scale=1.0 / D)
            P.op("scalar", "activation", out=rstd[:, hs], in_=rstd[:, hs], func=AF.Exp, scale=-0.5)

    def phase_out(self, mixT_attn, mixT_gla):
        P = self.P
        ps = self.ps
        RW = self.R_work
        RW.reset()
        x1T_all = P.sbuf("x1T", [128, KC, TOK], F32, RW)
        x1c = [x1T_all.alias(f"x1T{n}") for n in range(KC)]
        rstd = P.sbuf("rstd", [128, TOK], F32, RW)
        mark = RW.ptr
        xsrc = self.xw.h[:, 1024:2048].rearrange("(c p) t -> p c t", p=128)
        for q in range(4):
            P.dma("sync", out=x1c[4 * q][:, 4 * q:4 * q + 4, :], in_=self.xw.v(xsrc[:, 4 * q:4 * q + 4, :]),
                  extra_w=[x1c[4 * q + i].trk for i in range(1, 4)])
        wob = [P.sbuf(f"wob{i}", [128, KC, 256], BF16, RW) for i in range(2)]
        wo_src = self.w_out.h.rearrange("(c p) n -> p c n", p=128)
        for nn in range(8):
            wo = wob[nn % 2]
            cs = slice(256 * nn, 256 * nn + 256)
            P.dma("gpsimd", out=wo[:, :, :], in_=self.w_out.v(wo_src[:, :, cs]))
            for j in range(2):
                n = 2 * nn + j
                for h in range(2):
                    hs = slice(512 * h, 512 * h + 512)
                    pp = ps[(2 * j + h) % 4]
                    for kc in range(KC):
                        rhs = mixT_attn[:, kc, hs] if kc < 8 else mixT_gla[:, kc - 8, hs]
                        P.op("tensor", "matmul", out=pp[:, :], lhsT=wo[:, kc, 128 * j:128 * j + 128],
                             rhs=rhs, start=(kc == 0), stop=(kc == KC - 1))
                    P.op("vector", "tensor_tensor", out=x1c[n][:, n, hs], in0=x1c[n][:, n, hs], in1=pp[:, :],
                         op=ALU.add)
        P.barrier()
        RW.ptr = mark
        self.R_mix.reset()
        hnT = P.sbuf("hnT", [128, KC, TOK], BF16, self.R_mix)
        sq = P.sbuf("sq", [128, KC, 512], BF16, RW)
        self.rms_rstd(x1c, rstd, sq, ps[4])
        for kc in range(KC):
            P.op("vector", "scalar_tensor_tensor", out=hnT[:, kc, :], in0=x1c[kc][:, kc, :],
                 scalar=self.nwq[:, 16 + kc:17 + kc], in1=rstd[:, :], op0=ALU.mult, op1=ALU.mult)
        NG = 11
        wgb = [P.sbuf(f"wgb{i}", [128, KC, 256], BF16, RW) for i in range(2)]
        wub = [P.sbuf(f"wub{i}", [128, KC, 256], BF16, RW) for i in range(2)]
        wdb = [P.sbuf(f"wdb{i}", [128, 4, 1024], BF16, RW) for i in range(2)]
        aT = [P.sbuf(f"aT{i}", [128, 4, TOK], BF16, RW) for i in range(2)]
        sg = [P.sbuf(f"sg{i}", [128, 512], F32, RW) for i in range(2)]
        wg_src = self.w_gate.h.rearrange("(c p) f -> p c f", p=128)
        wu_src = self.w_up.h.rearrange("(c p) f -> p c f", p=128)
        wd_src = self.w_down.h.rearrange("(c p) n -> p c n", p=128)

        def load_gu(si):
            b = si % 2
            fs = slice(256 * si, 256 * si + 256)
            P.dma("gpsimd", out=wgb[b][:, :, :], in_=self.w_gate.v(wg_src[:, :, fs]))
            P.dma("gpsimd", out=wub[b][:, :, :], in_=self.w_up.v(wu_src[:, :, fs]))

        def load_d(gi, nh):
            P.dma("gpsimd", out=wdb[nh][:, :, :],
                  in_=self.w_down.v(wd_src[:, 4 * gi:4 * gi + 4, 1024 * nh:1024 * nh + 1024]))

        dcnt = [0]

        def down_half(gi, nh):
            ab = aT[gi % 2]
            for n in range(8 * nh, 8 * nh + 8):
                for h in range(2):
                    hs = slice(512 * h, 512 * h + 512)
                    pp = ps[4 + dcnt[0] % 2]
                    dcnt[0] += 1
                    nl = n - 8 * nh
                    for j in range(4):
                        P.op("tensor", "matmul", out=pp[:, :], lhsT=wdb[nh][:, j, 128 * nl:128 * nl + 128],
                             rhs=ab[:, j, hs], start=(j == 0), stop=(j == 3))
                    P.op("vector", "tensor_tensor", out=x1c[n][:, n, hs], in0=x1c[n][:, n, hs], in1=pp[:, :],
                         op=ALU.add)

        load_gu(0)
        load_gu(1)
        load_d(0, 0)
        load_d(0, 1)
        cnt = 0
        for gi in range(NG):
            for s_ in range(2):
                si = 2 * gi + s_
                b = si % 2
                for jl in range(2):
                    j = 2 * s_ + jl
                    for h in range(2):
                        hs = slice(512 * h, 512 * h + 512)
                        pg = ps[cnt % 2]
                        pu = ps[2 + cnt % 2]
                        sgt = sg[cnt % 2]
                        cnt += 1
                        for kc in range(KC):
                            P.op("tensor", "matmul", out=pg[:, :], lhsT=wgb[b][:, kc, 128 * jl:128 * jl + 128],
                                 rhs=hnT[:, kc, hs], start=(kc == 0), stop=(kc == KC - 1))
                        for kc in range(KC):
                            P.op("tensor", "matmul", out=pu[:, :], lhsT=wub[b][:, kc, 128 * jl:128 * jl + 128],
                                 rhs=hnT[:, kc, hs], start=(kc == 0), stop=(kc == KC - 1))
                        P.op("scalar", "activation", out=sgt[:, :], in_=pg[:, :], func=AF.Silu)
                        P.op("vector", "tensor_tensor", out=aT[gi % 2][:, j, hs], in0=sgt[:, :], in1=pu[:, :],
                             op=ALU.mult)
                if si + 2 < 2 * NG:
                    load_gu(si + 2)
                if gi > 0:
                    down_half(gi - 1, s_)
                    load_d(gi, s_)
        down_half(NG - 1, 0)
        down_half(NG - 1, 1)
        self.rms_rstd(x1c, rstd, sq, ps[6])
        for kc in range(KC):
            P.op("vector", "scalar_tensor_tensor", out=x1c[kc][:, kc, :], in0=x1c[kc][:, kc, :],
                 scalar=self.nwq[:, 32 + kc:33 + kc], in1=rstd[:, :], op0=ALU.mult, op1=ALU.mult)
        odst = self.out.h.rearrange("(c p) t -> p c t", p=128)
        for n in range(KC):
            P.dma("sync", out=self.out.v(odst[:, n, :]), in_=x1c[n][:, n, :])
        P.wait_all("sync", [self.out.trk])

    def build(self):
        P = self.P
        if self.stage == "attn":
            attnT = P.sbuf("attnT", [128, 8, TOK], BF16, self.R_mix)
            self.xwT = P.sbuf("xwT", [128, KC, WIN], BF16, self.R_work)
            self.attn_setup()
            self.phase_window()
            self.phase_attn(attnT)
            P.dma("sync", out=self.dbg[:, :, :], in_=attnT[:, :, :])
            P.wait_all("sync", [self.dbg.trk])
        if self.stage == "full":
            attnT = P.sbuf("attnT", [128, 8, TOK], BF16, self.R_mix)
            glaT = P.sbuf("glaT", [128, 8, TOK], BF16, self.R_mix)
            RS_ = Region(self.R_mix.base, 16 * 1024, "states")
            SF = P.sbuf("SF", [128, 4, 256], F32, RS_)
            SB = P.sbuf("SB", [128, 4, 256], F32, RS_)
            self.R_work.reset()
            self.attn_setup()
            P.barrier()
            self.phase_gla_others(SF, SB)
            self.phase_gla_own(SF, SB, glaT)
            self.R_work.reset()
            self.xwT = P.sbuf("xwT", [128, KC, WIN], BF16, self.R_work)
            self.phase_window()
            self.phase_attn(attnT)
            self.phase_out(attnT, glaT)
        if self.stage == "gla":
            glaT = P.sbuf("glaT", [128, 8, TOK], BF16, self.R_mix)
            SF = P.sbuf("SF", [128, 4, 256], F32, self.R_mix)
            SB = P.sbuf("SB", [128, 4, 256], F32, self.R_mix)
            self.phase_gla_others(SF, SB)
            P.dma("sync", out=self.dbg2[:, 0, :, :], in_=SF[:, :, :])
            P.dma("sync", out=self.dbg2[:, 1, :, :], in_=SB[:, :, :])
            if not os.environ.get("GLA_OTHERS_ONLY"):
                self.phase_gla_own(SF, SB, glaT)
            P.dma("sync", out=self.dbg[:, :, :], in_=glaT[:, :, :])
            P.wait_all("sync", [self.dbg.trk, self.dbg2.trk])
        if self.stage == "ffn":
            ma = P.sbuf("mixa", [128, 8, TOK], BF16, self.R_mix)
            mg = P.sbuf("mixg", [128, 8, TOK], BF16, self.R_mix)
            P.dma("gpsimd", out=ma[:, :, :],
                  in_=self.mixd.v(self.mixd.h[0:1024, :].rearrange("(c p) t -> p c t", p=128)))
            P.dma("gpsimd", out=mg[:, :, :],
                  in_=self.mixd.v(self.mixd.h[1024:2048, :].rearrange("(c p) t -> p c t", p=128)))
            self.phase_out(ma, mg)
        self.stats = P.emit()
        return self.nc


PATS = ((1, 128), (4, 128), (16, 64))


def _t5_bucket_np(rel):
    nb = 16
    ret = np.where(rel > 0, nb, 0)
    n = np.abs(rel)
    max_exact = nb // 2
    large = max_exact + (np.log(np.maximum(n, 1) / max_exact) / np.log(1024 / max_exact)
                         * (nb - max_exact)).astype(np.int64)
    large = np.minimum(large, nb - 1)
    return (ret + np.where(n < max_exact, n, large)).astype(np.int32)


def build_onehot():
    oh = np.zeros((33, 3, 512), np.float32)
    for pi, (r, _) in enumerate(PATS):
        for i in range(512):
            dlt = i - 255
            if abs(dlt) <= 64:
                oh[int(_t5_bucket_np(np.array(dlt * r))), pi, i] = 1.0
            else:
                oh[32, pi, i] = 1.0
    return oh


def build_valt(start):
    v = np.zeros((128, 53), np.float32)
    k = np.arange(128)

    def ok(w):
        t = start - 1024 + w
        return ((t >= 0) & (t < SEQ)).astype(np.float32)
    for j in range(9):
        v[:, j] = ok(960 + 128 * j + k)
    for p in range(4):
        for j in range(3):
            v[:, 9 + 3 * p + j] = ok(4 * (192 + 128 * j + k) + p)
    for p in range(16):
        v[:, 21 + 2 * p] = ok(16 * k + p)
        vb = ok(16 * (128 + k) + p)
        vb[64:] = 0.0
        v[:, 21 + 2 * p + 1] = vb
    return v


def _attn_setup(self):
    P = self.P
    RW = self.R_work
    m = RW.ptr
    tab = P.sbuf("tab", [33, 16], F32, RW)
    ohs = P.sbuf("ohs", [33, 3, 512], F32, RW)
    ev = P.sbuf("ev", [16, 3, 512], F32, RW)
    P.op("vector", "memset", ap=tab[32:33, :], constant=-10000.0)
    P.dma("sync", out=tab[0:32, :], in_=self.relt[:, :])
    P.dma("sync", out=ohs[:, :, :], in_=self.ohd[:, :, :])
    for pi in range(3):
        pp = self.ps[pi]
        P.op("tensor", "matmul", out=pp[0:16, :], lhsT=tab[:, :], rhs=ohs[:, pi, :], start=True, stop=True)
        P.op("scalar", "activation", out=ev[:, pi, :], in_=pp[0:16, :], func=AF.Exp)
    P.dma("sync", out=self.uvd[:, :, :], in_=ev[:, :, :])
    RW.ptr = m


def _phase_window(self):
    P = self.P
    RW = self.R_work
    m = RW.ptr
    NT = 256
    xs = [P.sbuf(f"xs{i}", [128, KC, NT], F32, RW) for i in range(3)]
    sqs = [P.sbuf(f"sqw{i}", [128, KC, NT], BF16, RW) for i in range(2)]
    rss = [P.sbuf(f"rsw{i}", [128, NT], F32, RW) for i in range(2)]
    src = self.xw.h.rearrange("(c p) t -> p c t", p=128)
    for wt in range(WIN // NT):
        x = xs[wt % 3]
        sq, rs = sqs[wt % 2], rss[wt % 2]
        ts = slice(NT * wt, NT * wt + NT)
        P.dma("sync", out=x[:, :, :], in_=self.xw.v(src[:, :, ts]))
        P.op("scalar", "activation", out=sq[:, :, :], in_=x[:, :, :], func=AF.Square)
        pp = self.ps[5 + wt % 2]
        for kc in range(KC):
            P.op("tensor", "matmul", out=pp[:, 0:NT], lhsT=self.ones_b, rhs=sq[:, kc, :],
                 start=(kc == 0), stop=(kc == KC - 1))
        P.op("scalar", "activation", out=rs[:, :], in_=pp[:, 0:NT], func=AF.Ln, bias=self.epsc[:, 0:1],
             scale=1.0 / D)
        P.op("scalar", "activation", out=rs[:, :], in_=rs[:, :], func=AF.Exp, scale=-0.5)
        for kc in range(KC):
            P.op("vector", "scalar_tensor_tensor", out=self.xwT[:, kc, ts], in0=x[:, kc, :],
                 scalar=self.nwq[:, kc:kc + 1], in1=rs[:, :], op0=ALU.mult, op1=ALU.mult)
    P.barrier()
    RW.ptr = m


import os
NDUM_DEFAULT = 0


def _phase_attn(self, attnT):
    P = self.P
    ps = self.ps
    RW = self.R_work
    m = RW.ptr
    NWB = 3
    wb = [P.sbuf(f"awb{i}", [128, KC, 128], BF16, RW) for i in range(NWB)]
    Qn = P.sbuf("Qn", [128, 1024], BF16, RW)
    Qd4 = P.sbuf("Qd4", [128, 4, 256], BF16, RW)
    Qd16 = P.sbuf("Qd16", [128, 16, 64], BF16, RW)
    Kn = P.sbuf("Kn", [128, 1152], BF16, RW)
    Kd4 = P.sbuf("Kd4", [128, 4, 384], BF16, RW)
    Kd16 = P.sbuf("Kd16", [128, 16, 192], BF16, RW)
    Vn = P.sbuf("Vn", [128, 1152], BF16, RW)
    Vd4 = P.sbuf("Vd4", [128, 4, 384], BF16, RW)
    Vd16 = P.sbuf("Vd16", [128, 16, 192], BF16, RW)
    G = P.sbuf("G", [128, 6, 2, 128], F32, RW)
    es = [P.sbuf(f"es{i}", [128, 512], F32, RW) for i in range(3)]
    pT = [P.sbuf(f"pT{i}", [128, 512], BF16, RW) for i in range(3)]
    acc = [P.sbuf("accA", [65, 1024], F32, RW), P.sbuf("accB", [128, 1024], F32, RW)]
    NVT = 16
    VT0 = P.sbuf("VT0", [128, NVT, 194], BF16, RW)
    VTs = [VT0, VT0]
    valt = P.sbuf("valt", [128, 53], F32, RW)
    P.dma("sync", out=valt[:, :], in_=self.valtd[:, :])
    P.op("vector", "memset", ap=VT0[:, :, :], constant=0.0)
    ps_s = [ps[0], ps[1], ps[6]]
    ps_o = [ps[2], ps[3]]
    ps_t = self.ps_t
    ps_p = [ps[4], ps[5]]
    ps_b = ps[6]
    win_src = self.w_in.h.rearrange("(c p) n -> p c n", p=128)
    wcnt = [0]
    scnt = [0]
    pcnt = [0]
    ecnt = [0]
    ocnt = [0]

    def load_w(col0):
        b = wb[wcnt[0] % NWB]
        wcnt[0] += 1
        P.dma("gpsimd", out=b[:, :, :], in_=self.w_in.v(win_src[:, :, col0:col0 + 128]))
        return b

    def proj(wt_, tok0):
        pp = ps_p[pcnt[0] % 2]
        pcnt[0] += 1
        for kc in range(KC):
            P.op("tensor", "matmul", out=pp[:, :], lhsT=wt_[:, kc, :], rhs=self.xwT[:, kc, tok0:tok0 + 512],
                 start=(kc == 0), stop=(kc == KC - 1))
        return pp

    def evac(eng, out, in_):
        em = os.environ.get("EVAC", "")
        if em == "none":
            return
        if em == "vec":
            eng = "vector"
        if em == "nostride" and len(in_.ap.ap) > 2:
            return
        if eng == "scalar":
            P.op("scalar", "activation", out=out, in_=in_, func=AF.Copy)
        else:
            P.op("vector", "tensor_copy", out=out, in_=in_)

    def kv_evac(pp, wt, Nn, D4, D16):
        if wt == 1:
            evac("scalar", Nn[:, 0:64], V(pp.ap[:, 448:512], pp.trk))
        elif wt in (2, 3):
            evac("scalar", Nn[:, 64 + 512 * (wt - 2):64 + 512 * (wt - 2) + 512], pp)
        elif wt == 4:
            evac("scalar", Nn[:, 1088:1152], V(pp.ap[:, 0:64], pp.trk))
        if wt == 1:
            evac("vector", D4[:, :, 0:64], V(pp.ap[:, 256:512].rearrange("q (l p) -> q p l", p=4), pp.trk))
        elif wt in (2, 3):
            l0 = 64 + 128 * (wt - 2)
            evac("vector", D4[:, :, l0:l0 + 128], V(pp.ap.rearrange("q (l p) -> q p l", p=4), pp.trk))
        elif wt == 4:
            evac("vector", D4[:, :, 320:384], V(pp.ap[:, 0:256].rearrange("q (l p) -> q p l", p=4), pp.trk))
        evac("vector", D16[:, :, 32 * wt:32 * wt + 32],
             V(pp.ap.rearrange("q (l p) -> q p l", p=16), pp.trk))

    LVL = int(os.environ.get("ATTN_LVL", "9"))
    NDUM = int(os.environ.get("ATTN_DUMMY", str(NDUM_DEFAULT)))
    NCH = int(os.environ.get("ATTN_NCH", "8"))

    def norm_chunk(c):
        banks = [ps_s[0], ps_s[1], ps_o[0], ps_o[1]]
        for hl in range(2):
            a = acc[hl]
            dr = 64 if hl == 0 else 0
            P.op("vector", "reciprocal", out=a[dr:dr + 1, :], in_=a[dr:dr + 1, :])
        for hl in range(2):
            a = acc[hl]
            dr = 64 if hl == 0 else 0
            for tt in range(2):
                ts = slice(512 * tt, 512 * tt + 512)
                P.op("tensor", "matmul", out=banks[2 * hl + tt][64 * hl:64 * hl + 64, :],
                     lhsT=self.cf.v(self.cf.h[dr:dr + 1, C_ONES:C_ONES + 64]), rhs=a[dr:dr + 1, ts],
                     start=True, stop=True)
        for hl in range(2):
            a = acc[hl]
            for tt in range(2):
                ts = slice(512 * tt, 512 * tt + 512)
                P.op("vector", "tensor_tensor", out=attnT[64 * hl:64 * hl + 64, c, ts],
                     in0=a[64 * hl:64 * hl + 64, ts], in1=banks[2 * hl + tt][64 * hl:64 * hl + 64, :], op=ALU.mult)

    for c in range(NCH):
        srcap = bass.AP(tensor=self.uvd.h, offset=(2 * c * 3) * 512 + 64,
                        ap=[[1, 128], [512, 6], [128, 2], [1, 128]])
        for e_ in range(int(os.environ.get("NG_", "12"))):
            hl_, rem = divmod(e_, 6)
            pi_, tl_ = divmod(rem, 2)
            srcap = bass.AP(tensor=self.uvd.h, offset=((2 * c + hl_) * 3 + pi_) * 512 + 64 + 128 * tl_,
                            ap=[[1, 128], [1, 128]])
            P.dma("sync", out=G[:, hl_ * 3 + pi_, tl_, :], in_=self.uvd.v(srcap))
        if c == 0:
            wts = [load_w(128 * c), load_w(1024 + 128 * c), load_w(2048 + 128 * c)]
        wq, wk, wv = wts
        for tt in range(2):
            pp = proj(wq, 1024 + 512 * tt)
            evac("scalar", Qn[:, 512 * tt:512 * tt + 512], pp[:, :])
            evac("vector", Qd4[:, :, 128 * tt:128 * tt + 128],
                 V(pp.h[:, :].rearrange("q (l p) -> q p l", p=4), pp.trk))
            evac("vector", Qd16[:, :, 32 * tt:32 * tt + 32],
                 V(pp.h[:, :].rearrange("q (l p) -> q p l", p=16), pp.trk))
        if c > 0:
            norm_chunk(c - 1)
        for wt in range(6):
            pp = proj(wk, 512 * wt)
            kv_evac(V(pp.h[:, :], pp.trk), wt, Kn, Kd4, Kd16)
        for wt in range(6):
            pp = proj(wv, 512 * wt)
            kv_evac(V(pp.h[:, :], pp.trk), wt, Vn, Vd4, Vd16)
        if c + 1 < NCH:
            wts = [load_w(128 * (c + 1)), load_w(1024 + 128 * (c + 1)), load_w(2048 + 128 * (c + 1))]
        sets_ = []
        for pi, (r, QB) in enumerate(PATS):
            if r == 1:
                sets_.append((pi, r, QB, 0, [(Kn[:, 128 * j:128 * j + 128], Vn[:, 128 * j:128 * j + 128], j, 128)
                                             for j in range(9)]))
            elif r == 4:
                sets_.append((pi, r, QB, 0, [(Kd4[:, p, 128 * j:128 * j + 128], Vd4[:, p, 128 * j:128 * j + 128],
                                              9 + 3 * p + j, 128) for p in range(4) for j in range(3)]))
            else:
                for hf in range(2):
                    sets_.append((pi, r, QB, hf,
                                  [(Kd16[:, p, 128 * t:128 * t + (128 if t == 0 else 64)],
                                    Vd16[:, p, 128 * t:128 * t + (128 if t == 0 else 64)], 21 + 2 * p + t,
                                    128 if t == 0 else 64) for p in range(8 * hf, 8 * hf + 8) for t in range(2)]))

        def build_set(tiles, VT):
            pstride = VT.h[:, :, :].ap[0][0]
            v0 = tiles[0][2]
            for t0 in range(0, len(tiles), 4):
                nt_ = min(4, len(tiles) - t0)
                for ti in range(t0, t0 + nt_):
                    ksrc, vsrc, vcol, nk = tiles[ti]
                    tcol = 128 * (ti - t0)
                    P.op("tensor", "transpose", out=ps_t[0:nk, tcol:tcol + 128], in_=vsrc, identity=self.ident_b)
                oap = bass.AP(tensor=VT.h, offset=VT.h[:, t0, 0:1].offset,
                              ap=[[pstride, 128], [194, nt_], [129, 2], [1, 64]])
                P.op("vector", "tensor_copy", out=VT.v(oap),
                     in_=ps_t.v(ps_t.h[:, 0:128 * nt_].rearrange("k (t a b) -> k t a b", t=nt_, a=2)))
                P.op("gpsimd", "tensor_copy", out=VT[:, t0:t0 + nt_, 64:66],
                     in_=valt.v(valt.h[:, v0 + t0:v0 + t0 + nt_].unsqueeze(2).broadcast_to([128, nt_, 2])))

        for si, (pi, r, QB, hf, tiles) in enumerate(sets_):
            VT = VTs[0]
            build_set(tiles, VT)
            if True:
                if LVL < 3:
                    continue
                if r == 1:
                    groups = [[(hl, Qn[:, :], 128 * (2 * g + b), 2 * g + b, 2 * g + b + 1) for b in range(2)]
                              for g in range(4) for hl in range(2)]
                elif r == 4:
                    groups = [[(hl, Qd4[:, p, :], 128 * b, 3 * p + b, 3 * p + b + 1) for b in range(2)]
                              for p in range(4) for hl in range(2)]
                else:
                    groups = [[(hl, Qd16[:, 8 * hf + 4 * g + b, :], 0, 2 * (4 * g + b), 2 * (4 * g + b) + 1)
                               for b in range(4)] for g in range(2) for hl in range(2)]
                def stage1(grp):
                    hl = grp[0][0]
                    hp = slice(64 * hl, 64 * hl + 64)
                    sp = ps_s[ecnt[0] % 3]
                    est = es[ecnt[0] % 3]
                    ptt = pT[ecnt[0] % 3]
                    ecnt[0] += 1
                    nb = len(grp)
                    W = 2 * QB
                    for bi, (_, qsrc, q0, ta, tb) in enumerate(grp):
                        qap = qsrc.ap[hp, q0:q0 + QB]
                        qtrk = qsrc.trk
                        for t_, tix in enumerate((ta, tb)):
                            ksrc, vsrc, vcol, nk = tiles[tix]
                            P.op("tensor", "matmul", out=sp[0:nk, W * bi + QB * t_:W * bi + QB * t_ + QB],
                                 lhsT=V(ksrc.ap[hp], ksrc.trk), rhs=V(qap, qtrk), start=True, stop=True)
                    nkb = tiles[grp[0][4]][3]
                    for _ in range(NDUM):
                        P.op("tensor", "matmul", out=ps_p[0][:, :], lhsT=self.ident_b, rhs=self.xwT[:, 0, 0:512],
                             start=True, stop=True)
                    P.op("scalar", "activation", out=est[:, 0:W * nb], in_=sp[:, 0:W * nb], func=AF.Exp,
                         scale=0.125)
                    gsl = G.h[:, hl * 3 + pi, :, :]
                    erev = gsl[:, :, ::-1][:, :, 0:QB]
                    ebc = erev.unsqueeze(1).broadcast_to([128, nb, 2, QB])
                    P.op("vector", "tensor_tensor",
                         out=ptt.v(ptt.h[:, 0:W * nb].rearrange("k (b t q) -> k b t q", b=nb, t=2)),
                         in0=est.v(est.h[:, 0:W * nb].rearrange("k (b t q) -> k b t q", b=nb, t=2)),
                         in1=G.v(ebc), op=ALU.mult)
                    return (hl, ptt, W, nb)

                def stage2(grp, ctx):
                    hl, ptt, W, nb = ctx
                    op_ = ps_o[ocnt[0] % 2]
                    ocnt[0] += 1
                    rows = 65 if hl == 0 else 128
                    for bi, (_, qsrc, q0, ta, tb) in enumerate(grp):
                        for t_, tix in enumerate((ta, tb)):
                            ksrc, vsrc, vcol, nk = tiles[tix]
                            lh = VT[0:nk, tix, 0:65] if hl == 0 else VT[0:nk, tix, 65:193]
                            P.op("tensor", "matmul", out=op_[0:rows, QB * bi:QB * bi + QB], lhsT=lh,
                                 rhs=ptt[0:nk, W * bi + QB * t_:W * bi + QB * t_ + QB],
                                 start=(t_ == 0), stop=(t_ == 1))
                    a = acc[hl]
                    if r == 1:
                        g0 = grp[0][2]
                        P.op("scalar", "activation", out=a[0:rows, g0:g0 + 256], in_=op_[0:rows, 0:256],
                             func=AF.Copy)
                    elif r == 4:
                        p = grp[0][3] // 3
                        dst = a.h[0:rows, :].rearrange("r (l p) -> r p l", p=4)[:, p, :]
                        P.op("vector", "tensor_tensor", out=a.v(dst), in0=a.v(dst), in1=op_[0:rows, 0:256],
                             op=ALU.add)
                    else:
                        p0 = 8 * hf + grp[0][3] // 2
                        dst = a.h[0:rows, :].rearrange("r (l p) -> r p l", p=16)[:, p0:p0 + 4, :]
                        P.op("vector", "tensor_tensor", out=a.v(dst), in0=a.v(dst),
                             in1=op_.v(op_.h[0:rows, 0:256].rearrange("r (p l) -> r p l", p=4)), op=ALU.add)
                pend = []
                for grp in groups:
                    ctx = stage1(grp)
                    pend.append((grp, ctx))
                    if len(pend) > 2:
                        stage2(*pend.pop(0))
                while pend:
                    stage2(*pend.pop(0))
    if NCH > 0:
        norm_chunk(NCH - 1)
    P.barrier()
    RW.ptr = m


Builder.attn_setup = _attn_setup
Builder.phase_window = _phase_window
Builder.phase_attn = _phase_attn


C_TRIF, C_TRIB, C_TRIRF, C_TRIRB, C_CI = 256, 384, 512, 640, 768
C_N2 = 772


def build_consts2():
    c = np.zeros((128, C_N2), np.float32)
    c[:, :C_N] = build_consts()
    s = np.arange(128)[:, None]
    t = np.arange(128)[None, :]
    same = (s // 64) == (t // 64)
    c[:, C_TRIF:C_TRIF + 128] = (same & (s <= t))
    c[:, C_TRIB:C_TRIB + 128] = (same & (s >= t))
    c[:, C_TRIRF:C_TRIRF + 128] = (same & (s > t))
    c[:, C_TRIRB:C_TRIRB + 128] = (same & (s < t))
    c[:, C_CI] = (np.arange(128) < 64)
    c[:, C_CI + 1] = (np.arange(128) >= 64)
    return c


def _cfv(self, c0, n, rows=slice(0, 128)):
    return self.cf.v(self.cf.h[rows, c0:c0 + n])


def _norm_tile(self, src_ap, x, sq, rs, dst_fn, nt, pp=None):
    P = self.P
    P.dma("sync", out=x[:, :, :], in_=src_ap)
    P.op("scalar", "activation", out=sq[:, :, :], in_=x[:, :, :], func=AF.Square)
    if pp is None:
        pp = self.ps[6]
    for kc in range(KC):
        P.op("tensor", "matmul", out=pp[:, 0:nt], lhsT=self.ones_b, rhs=sq[:, kc, :],
             start=(kc == 0), stop=(kc == KC - 1))
    P.op("scalar", "activation", out=rs[:, :], in_=pp[:, 0:nt], func=AF.Ln, bias=self.epsc[:, 0:1],
         scale=1.0 / D)
    P.op("scalar", "activation", out=rs[:, :], in_=rs[:, :], func=AF.Exp, scale=-0.5)
    for kc in range(KC):
        P.op("vector", "scalar_tensor_tensor", out=dst_fn(kc), in0=x[:, kc, :],
             scalar=self.nwq[:, kc:kc + 1], in1=rs[:, :], op0=ALU.mult, op1=ALU.mult)


def _gate_l(self, lrT, gw, gb, l_out, pz, ework):
    P = self.P
    P.op("tensor", "matmul", out=pz[:, :], lhsT=lrT, rhs=gw[:, :], start=True, stop=True)
    P.op("scalar", "activation", out=ework[:, :], in_=pz[:, :], func=AF.Exp, scale=-1.0)
    P.op("scalar", "activation", out=l_out, in_=ework[:, :], func=AF.Ln, bias=self.onec[:, 0:1], scale=1.0)


def _phase_gla_others(self, SF, SB):
    P = self.P
    ps = self.ps
    RA, RW, RC = self.RA, self.RB, self.RC
    RA.reset()
    RW.reset()
    RC.reset()
    NT = 256
    Wgk = P.sbuf("Wgk", [128, KC, 512], BF16, RA)
    Wgv = P.sbuf("Wgv", [128, KC, 1024], BF16, RA)
    self.Wgk, self.Wgv = Wgk, Wgv
    wsrc = self.w_in.h.rearrange("(c p) n -> p c n", p=128)
    for q in range(2):
        P.dma("gpsimd", out=Wgk[:, 8 * q:8 * q + 8, :], in_=self.w_in.v(wsrc[:, 8 * q:8 * q + 8, 3584:4096]))
    for q in range(4):
        P.dma("gpsimd", out=Wgv[:, 4 * q:4 * q + 4, :], in_=self.w_in.v(wsrc[:, 4 * q:4 * q + 4, 4096:5120]))
    xs = [P.sbuf(f"oxs{i}", [128, KC, NT], F32, RW) for i in range(2)]
    sqs = [P.sbuf(f"osq{i}", [128, KC, NT], BF16, RW) for i in range(2)]
    rss = [P.sbuf(f"ors{i}", [128, NT], F32, RW) for i in range(2)]
    xn = [P.sbuf(f"oxn{i}", [128, KC, NT], BF16, RW) for i in range(3)]
    wlr = [P.sbuf(f"owlr{i}", [128, KC, 16], BF16, RC) for i in range(3)]
    gw = [P.sbuf(f"ogw{i}", [17, 512], F32, RC) for i in range(3)]
    gb = [None, None, None]
    lrT = [P.sbuf(f"olrT{i}", [17, 128], F32, RC) for i in range(2)]
    for i in range(2):
        P.op("vector", "memset", ap=lrT[i][:, :], constant=1.0)
    vtm = [P.sbuf(f"ovtm{i}", [128, 1024], BF16, RC) for i in range(2)]
    ew = P.sbuf("oew", [128, 512], F32, RC)
    l = [P.sbuf(f"ol{i}", [128, 512], F32, RC) for i in range(2)]
    wend = P.sbuf("owend", [128, 512], F32, RC)
    kend = [P.sbuf(f"okend{i}", [128, 512], BF16, RC) for i in range(2)]
    dec = [P.sbuf(f"odec{i}", [128, 4, 2], F32, RC) for i in range(2)]
    csum = P.sbuf("ocsum", [128, 4], F32, RC)
    dtot = P.sbuf("odtot", [128, 4], F32, RC)
    dF = P.sbuf("odF", [128, 4], F32, RC)
    Rj_all = P.sbuf("oRj", [128, 4, 256], F32, RC)
    Rj = [Rj_all.alias(f"oRj{h}") for h in range(4)]
    P.op("vector", "memset", ap=SF[:, :, :], constant=0.0)
    P.op("vector", "memset", ap=SB[:, :, :], constant=0.0)
    xsrc = self.xo.h.rearrange("j (c p) t -> j p c t", p=128)
    for j in range(3):
        P.dma("gpsimd", out=wlr[j][:, :, :],
              in_=self.wlr_o.v(self.wlr_o.h[j].rearrange("(c p) n -> p c n", p=128)))
        P.dma("sync", out=gw[j][0:16, :], in_=self.gw_o[j])
        P.dma("sync", out=gw[j][16:17, :], in_=self.gb_o[j])
    subs = [(j, tt, sub) for j in range(3) for tt in range(TOK // NT) for sub in range(NT // 128)]
    state = {"tcnt": 0}

    tiles_ = [(j, tt) for j in range(3) for tt in range(TOK // NT)]

    def stage0(t):
        j, tt = tiles_[t]
        x, xnn = xs[t % 2], xn[t % 3]
        _norm_tile(self, self.xo.v(xsrc[j, :, :, NT * tt:NT * tt + NT]), x, sqs[t % 2], rss[t % 2],
                   lambda kc: xnn[:, kc, :], NT, ps[3])

    def stage1(idx):
        j, tt, sub = subs[idx]
        xnn = xn[(idx // 2) % 3]
        tsl = slice(128 * sub, 128 * sub + 128)
        pk = ps[idx % 2]
        for kc in range(KC):
            P.op("tensor", "matmul", out=pk[:, :], lhsT=xnn[:, kc, tsl], rhs=Wgk[:, kc, :],
                 start=(kc == 0), stop=(kc == KC - 1))
        vt = vtm[idx % 2]
        for hv in range(2):
            pv = ps[2]
            for kc in range(KC):
                P.op("tensor", "matmul", out=pv[:, :], lhsT=xnn[:, kc, tsl],
                     rhs=Wgv[:, kc, 512 * hv:512 * hv + 512], start=(kc == 0), stop=(kc == KC - 1))
            P.op("scalar", "activation", out=vt[:, 512 * hv:512 * hv + 512], in_=pv[:, :], func=AF.Copy)
        pl = ps[4]
        for kc in range(KC):
            P.op("tensor", "matmul", out=pl[0:16, 0:128], lhsT=wlr[j][:, kc, :], rhs=xnn[:, kc, tsl],
                 start=(kc == 0), stop=(kc == KC - 1))
        P.op("vector", "tensor_copy", out=lrT[idx % 2][0:16, :], in_=pl[0:16, 0:128])

    def stage2(idx):
        j, tt, sub = subs[idx]
        pk = ps[idx % 2]
        vt = vtm[idx % 2]
        lt = l[idx % 2]
        ke = kend[idx % 2]
        de = dec[idx % 2]
        if tt == 0 and sub == 0:
            for h in range(4):
                P.op("vector", "memset", ap=Rj[h][:, h, :], constant=0.0)
            P.op("vector", "memset", ap=csum[:, :], constant=0.0)
        _gate_l(self, lrT[idx % 2][:, :], gw[j], gb[j], lt[:, :], ps[3], ew)
        pr = ps[5]
        P.op("tensor", "matmul", out=pr[:, :], lhsT=_cfv(self, C_TRIRF, 128), rhs=lt[:, :], start=True, stop=True)
        P.op("scalar", "activation", out=wend[:, :], in_=pr[:, :], func=AF.Exp, scale=-1.0 / 16)
        P.op("vector", "tensor_tensor", out=ke[:, :], in0=wend[:, :], in1=pk[:, :], op=ALU.mult)
        pc = ps[4]
        for h in range(4):
            P.op("tensor", "matmul", out=pc[:, 256 + 2 * h:256 + 2 * h + 2], lhsT=lt[:, 128 * h:128 * h + 128],
                 rhs=_cfv(self, C_CI, 2), start=True, stop=True)
        pcv = pc.v(pc.h[:, 256:264].rearrange("d (h c) -> d h c", h=4))
        P.op("scalar", "activation", out=de[:, :, :], in_=pcv, func=AF.Exp, scale=-1.0 / 16)
        P.op("vector", "tensor_reduce", out=dtot[:, :], in_=pcv, axis=AX.X, op=ALU.add)
        P.op("vector", "tensor_tensor", out=csum[:, :], in0=csum[:, :], in1=dtot[:, :], op=ALU.add)
        for ci in range(2):
            cp = slice(64 * ci, 64 * ci + 64)
            for h in range(4):
                pu = ps[5 + h % 2]
                P.op("tensor", "matmul", out=pu[:, 0:256], lhsT=ke[cp, 128 * h:128 * h + 128],
                     rhs=vt[cp, 256 * h:256 * h + 256], start=True, stop=True)
                P.op("vector", "scalar_tensor_tensor", out=Rj[h][:, h, :], in0=Rj[h][:, h, :],
                     scalar=de[:, h, ci:ci + 1], in1=pu[:, 0:256], op0=ALU.mult, op1=ALU.add)
        if tt == TOK // NT - 1 and sub == NT // 128 - 1:
            P.op("scalar", "activation", out=dtot[:, :], in_=csum[:, :], func=AF.Exp, scale=-1.0 / 16)
            for (S, fc) in ((SF, j), (SB, 3 + j)):
                P.op("vector", "tensor_scalar", out=dF[:, :], in0=dtot[:, :], scalar1=-1.0,
                     scalar2=self.flg[:, fc:fc + 1], op0=ALU.add, op1=ALU.mult)
                P.op("vector", "tensor_scalar", out=dF[:, :], in0=dF[:, :], scalar1=1.0, scalar2=None,
                     op0=ALU.add)
                for h in range(4):
                    P.op("vector", "tensor_scalar", out=S[:, h, :], in0=S[:, h, :], scalar1=dF[:, h:h + 1],
                         scalar2=None, op0=ALU.mult)
                    P.op("vector", "scalar_tensor_tensor", out=S[:, h, :], in0=Rj[h][:, h, :],
                         scalar=self.flg[:, fc:fc + 1], in1=S[:, h, :], op0=ALU.mult, op1=ALU.add)

    stage0(0)
    stage0(1)
    stage1(0)
    for idx in range(len(subs)):
        if idx % 2 == 0 and idx // 2 + 2 < len(tiles_):
            stage0(idx // 2 + 2)
        if idx + 1 < len(subs):
            stage1(idx + 1)
        stage2(idx)
    P.barrier()
    RW.reset()
    RC.reset()


Builder.phase_gla_others = _phase_gla_others


def _phase_gla_own(self, SF, SB, glaT):
    P = self.P
    ps = self.ps
    RA, RW, RC = self.RA, self.RB, self.RC
    RW.reset()
    RC.reset()
    Wgk, Wgv = self.Wgk, self.Wgv
    NT = 256
    wsrc = self.w_in.h.rearrange("(c p) n -> p c n", p=128)
    gqT = P.sbuf("gqT", [128, 4, TOK], BF16, RW)
    gkT = P.sbuf("gkT", [128, 4, TOK], BF16, RW)
    gktm = P.sbuf("gktm", [128, 8, 512], BF16, RW)
    vtm = P.sbuf("gvtm", [128, 8, 1024], BF16, RW)
    sgr = P.sbuf("sgr", [128, 8, 1024], BF16, RW)
    lrT = [P.sbuf(f"glrT{d}", [17, TOK], F32, RW) for d in range(2)]
    gw = [P.sbuf(f"ggw{d}", [17, 512], F32, RW) for d in range(2)]
    gb = [None, None]
    gnw = P.sbuf("gnw", [128, 1024], F32, RW)
    xn = P.sbuf("gxn", [128, KC, TOK], BF16, RC)
    wq = [P.sbuf(f"gwq{i}", [128, KC, 128], BF16, RC) for i in range(1)]
    wlr = P.sbuf("gwlr", [128, KC, 32], BF16, RC)
    RS = Region(RW.base, 55 * 1024, "stage")
    xs = [P.sbuf(f"gxs{i}", [128, KC, NT], F32, RS) for i in range(2)]
    sqs = [P.sbuf(f"gsq{i}", [128, KC, NT], BF16, RS) for i in range(2)]
    rss = [P.sbuf(f"grs{i}", [128, NT], F32, RS) for i in range(2)]
    P.dma("sync", out=gw[0][0:16, :], in_=self.gwf[:, :])
    P.dma("sync", out=gw[1][0:16, :], in_=self.gwb[:, :])
    P.dma("sync", out=gw[0][16:17, :], in_=self.gbf[:, :])
    P.dma("sync", out=gw[1][16:17, :], in_=self.gbb[:, :])
    P.dma("sync", out=gnw[:, :], in_=self.gnwd[:, :])
    P.dma("gpsimd", out=wlr[:, :, :], in_=self.w_in.v(wsrc[:, :, 6144:6176]))
    xsrc = self.xw.h[:, 1024:2048].rearrange("(c p) t -> p c t", p=128)
    for tt in range(TOK // NT):
        _norm_tile(self, self.xw.v(xsrc[:, :, NT * tt:NT * tt + NT]), xs[tt % 2], sqs[tt % 2], rss[tt % 2],
                   lambda kc: xn[:, kc, NT * tt:NT * tt + NT], NT, ps[tt % 2])
    P.barrier()
    for d in range(2):
        P.op("vector", "memset", ap=lrT[d][:, :], constant=1.0)
    pc_ = [0]

    def nxt():
        pc_[0] += 1
        return ps[pc_[0] % 4]
    for h in range(4):
        w = wq[0]
        P.dma("gpsimd", out=w[:, :, :], in_=self.w_in.v(wsrc[:, :, 3072 + 128 * h:3072 + 128 * h + 128]))
        for tt in range(2):
            pp = nxt()
            for kc in range(KC):
                P.op("tensor", "matmul", out=pp[:, :], lhsT=w[:, kc, :], rhs=xn[:, kc, 512 * tt:512 * tt + 512],
                     start=(kc == 0), stop=(kc == KC - 1))
            P.op("scalar", "activation", out=gqT[:, h, 512 * tt:512 * tt + 512], in_=pp[:, :], func=AF.Copy)
        for tt in range(2):
            pp = nxt()
            for kc in range(KC):
                P.op("tensor", "matmul", out=pp[:, :], lhsT=Wgk[:, kc, 128 * h:128 * h + 128],
                     rhs=xn[:, kc, 512 * tt:512 * tt + 512], start=(kc == 0), stop=(kc == KC - 1))
            P.op("vector", "tensor_copy", out=gkT[:, h, 512 * tt:512 * tt + 512], in_=pp[:, :])
    for d in range(2):
        for tt in range(2):
            pp = nxt()
            for kc in range(KC):
                P.op("tensor", "matmul", out=pp[0:16, :], lhsT=wlr[:, kc, 16 * d:16 * d + 16],
                     rhs=xn[:, kc, 512 * tt:512 * tt + 512], start=(kc == 0), stop=(kc == KC - 1))
            P.op("vector", "tensor_copy", out=lrT[d][0:16, 512 * tt:512 * tt + 512], in_=pp[0:16, :])
    for st in range(8):
        tsl = slice(128 * st, 128 * st + 128)
        pp = nxt()
        for kc in range(KC):
            P.op("tensor", "matmul", out=pp[:, :], lhsT=xn[:, kc, tsl], rhs=Wgk[:, kc, :],
                 start=(kc == 0), stop=(kc == KC - 1))
        P.op("vector", "tensor_copy", out=gktm[:, st, :], in_=pp[:, :])
        for hv in range(2):
            pp = nxt()
            for kc in range(KC):
                P.op("tensor", "matmul", out=pp[:, :], lhsT=xn[:, kc, tsl], rhs=Wgv[:, kc, 512 * hv:512 * hv + 512],
                     start=(kc == 0), stop=(kc == KC - 1))
            P.op("scalar", "activation", out=vtm[:, st, 512 * hv:512 * hv + 512], in_=pp[:, :], func=AF.Copy)
    P.barrier()
    RA.reset()
    Wgr = P.sbuf("Wgr", [128, KC, 512], BF16, RA)
    for hv in range(2):
        for q in range(2):
            P.dma("gpsimd", out=Wgr[:, 8 * q:8 * q + 8, :],
                  in_=self.w_in.v(wsrc[:, 8 * q:8 * q + 8, 5120 + 512 * hv:5120 + 512 * hv + 512]))
        for st in range(8):
            tsl = slice(128 * st, 128 * st + 128)
            pp = nxt()
            for kc in range(KC):
                P.op("tensor", "matmul", out=pp[:, :], lhsT=xn[:, kc, tsl], rhs=Wgr[:, kc, :],
                     start=(kc == 0), stop=(kc == KC - 1))
            P.op("scalar", "activation", out=sgr[:, st, 512 * hv:512 * hv + 512], in_=pp[:, :], func=AF.Silu)
    P.barrier()
    RA.reset()
    RC.reset()
    RW = RC
    lg = [P.sbuf(f"lg{d}", [128, 8, 512], F32, RA) for d in range(2)]
    ew = P.sbuf("gew", [128, 512], F32, RW)
    for d in range(2):
        for st in range(8):
            _gate_l(self, lrT[d][:, 128 * st:128 * st + 128], gw[d], gb[d], lg[d][:, st, :], ps[st % 2], ew)
    eq = P.sbuf("geq", [128, 512], F32, RA)
    ek = P.sbuf("gek", [128, 512], F32, RA)
    qin = [P.sbuf(f"gqin{d}", [128, TOK], BF16, RA) for d in range(2)]
    kin = [P.sbuf(f"gkin{d}", [128, TOK], BF16, RA) for d in range(2)]
    wend = P.sbuf("gwend", [128, 128], F32, RW)
    kend = [P.sbuf(f"gkend{d}", [128, 8, 128], BF16, RW) for d in range(2)]
    dec = [P.sbuf(f"gdec{d}", [128, 16], F32, RW) for d in range(2)]
    Scur = [[P.sbuf(f"gScur{d}{i}", [128, 256], F32, RW) for i in range(2)] for d in range(2)]
    Sall = [P.sbuf(f"gSall{d}", [128, 16, 256], BF16, RW) for d in range(2)]
    Am = [[P.sbuf(f"gAm{d}{i}", [128, 128], BF16, RW) for i in range(2)] for d in range(2)]
    ot = [P.sbuf(f"got{i}", [128, 256], F32, RW) for i in range(2)]
    og = [P.sbuf(f"gog{i}", [128, 256], BF16, RW) for i in range(2)]
    ssum = [P.sbuf(f"gssum{i}", [128, 1], F32, RW) for i in range(2)]
    junk = P.sbuf("gjunk", [128, 256], F32, RW)
    TRI = (C_TRIF, C_TRIB)
    TRIR = (C_TRIRF, C_TRIRB)
    for h in range(4):
        hc = slice(128 * h, 128 * h + 128)
        for d in (1, 0):
            for g4 in range(2):
                pcs = ps[g4]
                for i in range(4):
                    st = 4 * g4 + i
                    P.op("tensor", "matmul", out=pcs[:, 128 * i:128 * i + 128], lhsT=lg[d][:, st, hc],
                         rhs=_cfv(self, TRI[d], 128), start=True, stop=True)
                ts4 = slice(512 * g4, 512 * g4 + 512)
                P.op("scalar", "activation", out=eq[:, :], in_=pcs[:, :], func=AF.Exp, scale=-1.0 / 16)
                P.op("scalar", "activation", out=ek[:, :], in_=pcs[:, :], func=AF.Exp, scale=1.0 / 16)
                P.op("vector", "scalar_tensor_tensor", out=qin[d][:, ts4], in0=gqT[:, h, ts4],
                     scalar=float(128 ** -0.5), in1=eq[:, :], op0=ALU.mult, op1=ALU.mult)
                P.op("vector", "tensor_tensor", out=kin[d][:, ts4], in0=gkT[:, h, ts4], in1=ek[:, :], op=ALU.mult)
            pdc = ps[2]
            for st in range(8):
                P.op("tensor", "matmul", out=pdc[:, 2 * st:2 * st + 2], lhsT=lg[d][:, st, hc],
                     rhs=_cfv(self, C_CI, 2), start=True, stop=True)
            P.op("scalar", "activation", out=dec[d][:, :], in_=pdc[:, 0:16], func=AF.Exp, scale=-1.0 / 16)
            for st in range(8):
                pr = ps[3 + st % 2]
                P.op("tensor", "matmul", out=pr[:, 0:128], lhsT=_cfv(self, TRIR[d], 128), rhs=lg[d][:, st, hc],
                     start=True, stop=True)
                P.op("scalar", "activation", out=wend[:, :], in_=pr[:, 0:128], func=AF.Exp, scale=-1.0 / 16)
                P.op("vector", "tensor_tensor", out=kend[d][:, st, :], in0=gktm[:, st, hc], in1=wend[:, :],
                     op=ALU.mult)
        order = {0: list(range(16)), 1: list(range(15, -1, -1))}
        P.op("vector", "tensor_copy", out=Scur[0][0][:, :], in_=SF[:, h, :])
        P.op("vector", "tensor_copy", out=Scur[1][0][:, :], in_=SB[:, h, :])
        for step in range(16):
            for d in (1, 0):
                n = order[d][step]
                st, ci = divmod(n, 2)
                cp = slice(64 * ci, 64 * ci + 64)
                sc, sn = Scur[d][step % 2], Scur[d][(step + 1) % 2]
                P.op("scalar", "activation", out=Sall[d][:, n, :], in_=sc[:, :], func=AF.Copy)
                pu = ps[3 + 2 * d + step % 2]
                P.op("tensor", "matmul", out=pu[:, 0:256], lhsT=kend[d][cp, st, :],
                     rhs=vtm[cp, st, 256 * h:256 * h + 256], start=True, stop=True)
                P.op("vector", "scalar_tensor_tensor", out=sn[:, :], in0=sc[:, :],
                     scalar=dec[d][:, n:n + 1], in1=pu[:, 0:256], op0=ALU.mult, op1=ALU.add)
        def outA(st):
            tsl = slice(128 * st, 128 * st + 128)
            for d in (1, 0):
                pa = ps[2 * d + st % 2]
                P.op("tensor", "matmul", out=pa[:, 0:128], lhsT=kin[d][:, tsl], rhs=qin[d][:, tsl],
                     start=True, stop=True)
                P.op("vector", "tensor_tensor", out=Am[d][st % 2][:, :], in0=pa[:, 0:128],
                     in1=_cfv(self, TRI[d], 128), op=ALU.mult)
            po = ps[4 + st % 2]
            P.op("tensor", "matmul", out=po[:, 0:256], lhsT=Am[1][st % 2][:, :],
                 rhs=vtm[:, st, 256 * h:256 * h + 256], start=True, stop=False)
            P.op("tensor", "matmul", out=po[:, 0:256], lhsT=Am[0][st % 2][:, :],
                 rhs=vtm[:, st, 256 * h:256 * h + 256], start=False, stop=False)
            for d in (1, 0):
                for ci in range(2):
                    n = 2 * st + ci
                    P.op("tensor", "matmul", out=po[64 * ci:64 * ci + 64, 0:256],
                         lhsT=qin[d][:, 128 * st + 64 * ci:128 * st + 64 * ci + 64], rhs=Sall[d][:, n, :],
                         start=False, stop=(d == 0))

        def outB(st):
            tsl = slice(128 * st, 128 * st + 128)
            po = ps[4 + st % 2]
            o_, g_, s_ = ot[st % 2], og[st % 2], ssum[st % 2]
            P.op("scalar", "activation", out=junk[:, :], in_=po[:, 0:256], func=AF.Square, accum_out=s_[:, :])
            P.op("scalar", "activation", out=s_[:, :], in_=s_[:, :], func=AF.Sqrt, bias=self.epsc[:, 0:1],
                 scale=1.0 / 256)
            P.op("vector", "reciprocal", out=s_[:, :], in_=s_[:, :])
            P.op("vector", "scalar_tensor_tensor", out=o_[:, :], in0=po[:, 0:256], scalar=s_[:, 0:1],
                 in1=gnw[:, 256 * h:256 * h + 256], op0=ALU.mult, op1=ALU.mult)
            P.op("vector", "tensor_tensor", out=g_[:, :], in0=o_[:, :], in1=sgr[:, st, 256 * h:256 * h + 256],
                 op=ALU.mult)
            for i in range(2):
                P.op("tensor", "transpose", out=self.ps_t[:, 256 * (st % 2) + 128 * i:256 * (st % 2) + 128 * i + 128],
                     in_=g_[:, 128 * i:128 * i + 128], identity=self.ident_b)
            P.op("vector", "tensor_copy", out=glaT[:, 2 * h:2 * h + 2, tsl],
                 in_=self.ps_t.v(self.ps_t.h[:, 256 * (st % 2):256 * (st % 2) + 256].rearrange("f (i t) -> f i t", i=2)))

        outA(0)
        for st in range(8):
            if st + 1 < 8:
                outA(st + 1)
            outB(st)
    P.barrier()


Builder.phase_gla_own = _phase_gla_own


def _norm_layout(w):
    return np.ascontiguousarray(np.asarray(w, np.float32).reshape(KC, 128).T)


def shared_inputs(inp):
    f = lambda a: np.ascontiguousarray(np.asarray(a, np.float32))
    nw = np.concatenate([_norm_layout(inp["attn_norm_w"][0]), _norm_layout(inp["ffn_norm_w"][0]),
                         _norm_layout(inp["final_norm_w"])], axis=1)
    return {
        "nw": nw, "cst": build_consts2(), "w_out": f(inp["w_out"][0]), "w_gate": f(inp["w_gate"][0]),
        "w_up": f(inp["w_up"][0]), "w_down": f(inp["w_down"][0]), "w_in": f(inp["w_in"][0]),
        "relt": f(inp["rel_bias_table"]), "ohd": build_onehot(),
        "gwf": f(inp["gla_gate_w_fwd"][0]), "gwb": f(inp["gla_gate_w_bwd"][0]),
        "gbf": f(inp["gla_gate_b_fwd"][0])[None, :], "gbb": f(inp["gla_gate_b_bwd"][0])[None, :],
        "gnwd": np.ascontiguousarray(np.broadcast_to(f(inp["gla_norm_w"][0])[None, :], (128, 1024))),
    }


def core_inputs(inp, c):
    b, g = divmod(c, 4)
    start = TOK * g
    x = np.asarray(inp["x"][b], np.float32)
    w_in = np.asarray(inp["w_in"][0], np.float32)
    xw = np.zeros((D, WIN), np.float32)
    lo, hi = max(0, start - 1024), min(SEQ, start + 2048)
    xw[:, lo - (start - 1024):hi - (start - 1024)] = x[lo:hi].T
    slots = [(blk, 0) for blk in range(g)] + [(blk, 1) for blk in range(3, g, -1)]
    xo = np.zeros((3, D, TOK), np.float32)
    wlr_o = np.zeros((3, D, 16), np.float32)
    gw_o = np.zeros((3, 16, 512), np.float32)
    gb_o = np.zeros((3, 1, 512), np.float32)
    flg = np.zeros((128, 8), np.float32)
    for j, (blk, isb) in enumerate(slots):
        xb = x[TOK * blk:TOK * blk + TOK]
        if isb:
            xb = xb[::-1]
        xo[j] = xb.T
        wlr_o[j] = w_in[:, 6160:6176] if isb else w_in[:, 6144:6160]
        gw_o[j] = inp["gla_gate_w_bwd"][0] if isb else inp["gla_gate_w_fwd"][0]
        gb_o[j, 0] = inp["gla_gate_b_bwd"][0] if isb else inp["gla_gate_b_fwd"][0]
        flg[:, 3 * isb + j] = 1.0
    return {"xw": xw, "xo": xo, "wlr_o": wlr_o, "gw_o": gw_o, "gb_o": gb_o, "flgd": flg,
            "valtd": build_valt(start)}


_CACHE = {}


def kernel(**inputs):
    if "nc" not in _CACHE:
        b = Builder(stage="full")
        _CACHE["nc"] = b.build()
    nc = _CACHE["nc"]
    sh = shared_inputs(inputs)
    in_maps = []
    for c in range(8):
        m = dict(sh)
        m.update(core_inputs(inputs, c))
        in_maps.append(m)
    res = run_bass_kernel_spmd(nc, in_maps, core_ids=list(range(8)))
    out = np.zeros((NB, SEQ, D), np.float32)
    for c in range(8):
        b, g = divmod(c, 4)
        out[b, TOK * g:TOK * g + TOK, :] = np.asarray(res.results[c]["out"]).T
    return out
```

```python
import numpy as np
import concourse.bass as bass
import concourse.mybir as mybir
from concourse.bass_utils import run_bass_kernel_spmd

F32 = mybir.dt.float32
BF16 = mybir.dt.bfloat16
AF = mybir.ActivationFunctionType
ALU = mybir.AluOpType
AX = mybir.AxisListType

ENGS = ("tensor", "vector", "scalar", "gpsimd", "sync")
SAME_ENG_ALL = False


ALL_TRKS = []


class Trk:
    __slots__ = ("last_w", "readers", "dma_sem", "dma_cnt", "name", "excl")

    def __init__(self, name):
        ALL_TRKS.append(self)
        self.name = name
        self.excl = False
        self.last_w = None
        self.readers = {}
        self.dma_sem = None
        self.dma_cnt = 0


class V:
    __slots__ = ("ap", "trk")

    def __init__(self, ap, trk):
        self.ap = ap
        self.trk = trk


class T:
    def __init__(self, h, name, trk=None):
        self.h = h
        self.name = name
        self.trk = trk if trk is not None else Trk(name)

    def __getitem__(self, idx):
        return V(self.h[idx], self.trk)

    def alias(self, name):
        return T(self.h, name, Trk(name))

    def v(self, ap):
        return V(ap, self.trk)


class Region:
    def __init__(self, base, size, name=""):
        self.base, self.size, self.ptr, self.name = base, size, base, name

    def alloc(self, nb, what=""):
        off = (self.ptr + 31) // 32 * 32
        assert off + nb <= self.base + self.size, (self.name, what, nb, off - self.base, self.size)
        self.ptr = off + nb
        return off

    def sub(self, size, name=""):
        off = self.alloc(size, name)
        return Region(off, size, name)

    def reset(self):
        self.ptr = self.base


SB_LO, SB_HI = 16512, 229344


class Ins:
    __slots__ = ("eng", "fn", "kw", "deps", "idx", "milestone", "count", "dma", "waits")

    def __init__(self, eng, fn, kw, deps, idx, dma=None):
        self.eng = eng
        self.fn = fn
        self.kw = kw
        self.deps = deps
        self.idx = idx
        self.milestone = False
        self.count = 0
        self.dma = dma
        self.waits = []


class Prog:
    def __init__(self, nc):
        ALL_TRKS.clear()
        self.nc = nc
        self.q = {e: [] for e in ENGS}
        self.sems = {}
        self.n_dma_sems = 0

    def sbuf(self, name, shape, dt, reg=None):
        nb = int(np.prod(shape[1:])) * (4 if dt == F32 else 2)
        off = reg.alloc(nb, name)
        self.nsb = getattr(self, "nsb", 0) + 1
        return T(self.nc.alloc_sbuf_tensor_at(f"{name}_{self.nsb}", list(shape), dt, offset=off), name)

    def psum(self, name, shape, dt=F32):
        t = T(self.nc.alloc_psum_tensor(name, list(shape), dt), name)
        t.trk.excl = True
        return t

    def dram(self, name, shape, dt, kind="Internal"):
        return T(self.nc.dram_tensor(name, list(shape), dt, kind=kind), name)

    def _collect(self, eng, reads, writes, same_eng_raw=True):
        deps = []
        for trk in reads:
            if trk.last_w is not None:
                deps.append(trk.last_w)
            if trk.excl:
                deps.extend(ev for key, ev in trk.readers.items() if key != eng)
        for trk in writes:
            if trk.last_w is not None:
                deps.append(trk.last_w)
            deps.extend(trk.readers.values())
        out = []
        for d in deps:
            if d[0] == "eng" and d[1] == eng:
                if eng == "tensor":
                    continue
                if not same_eng_raw:
                    continue
                if not SAME_ENG_ALL:
                    is_raw = any(t.last_w is d for t in reads)
                    if not is_raw:
                        continue
            out.append(d)
        return out

    def op(self, eng, fn, **kw):
        reads, writes = [], []
        kw2 = {}
        for k, v in kw.items():
            if isinstance(v, V):
                if k in ("out", "accum_out", "ap") or k.startswith("out"):
                    writes.append(v.trk)
                else:
                    reads.append(v.trk)
                kw2[k] = v.ap
            else:
                kw2[k] = v
        if fn == "matmul" and kw.get("start") is False:
            pass
        deps = self._collect(eng, reads, writes)
        ins = Ins(eng, fn, kw2, deps, len(self.q[eng]))
        self.q[eng].append(ins)
        ev = ("eng", eng, ins)
        for t in reads:
            t.readers[eng] = ev
        for t in writes:
            t.last_w = ev
            t.readers = {}
        return ins

    def dma(self, queue, out, in_, extra_w=(), **kw):
        deps = self._collect(queue, [in_.trk], [out.trk] + list(extra_w), same_eng_raw=True)
        trk = out.trk
        if trk.dma_sem is None:
            trk.dma_sem = self.nc.alloc_semaphore(name=f"dsem{self.n_dma_sems}")
            self.n_dma_sems += 1
        trk.dma_cnt += 16
        kw2 = dict(kw)
        kw2["out"] = out.ap
        kw2["in_"] = in_.ap
        ins = Ins(queue, "dma_start", kw2, deps, len(self.q[queue]), dma=(trk.dma_sem, trk.dma_cnt))
        self.q[queue].append(ins)
        ev = ("dma", trk.dma_sem, trk.dma_cnt)
        in_.trk.readers[("d", id(trk.dma_sem))] = ev
        for t in [trk] + list(extra_w):
            t.last_w = ev
            t.readers = {}
        return ins

    def coll(self, kind, op, groups, in_, out, sem_name="ccsem"):
        deps = self._collect("gpsimd", [in_.trk], [out.trk])
        trk = out.trk
        if trk.dma_sem is None:
            trk.dma_sem = self.nc.alloc_semaphore(name=f"dsem{self.n_dma_sems}")
            self.n_dma_sems += 1
        trk.dma_cnt += 1
        kw2 = dict(kind=kind, op=op, replica_groups=groups, ins=[in_.ap], outs=[out.ap])
        ins = Ins("gpsimd", "collective_compute", kw2, deps, len(self.q["gpsimd"]),
                  dma=(trk.dma_sem, trk.dma_cnt, 1))
        self.q["gpsimd"].append(ins)
        ev = ("dma", trk.dma_sem, trk.dma_cnt)
        in_.trk.readers[("d", id(trk.dma_sem))] = ev
        trk.last_w = ev
        trk.readers = {}
        return ins

    def wait_all(self, eng, trks):
        deps = [t.last_w for t in trks if t.last_w is not None]
        ins = Ins(eng, None, {}, deps, len(self.q[eng]))
        self.q[eng].append(ins)
        return ins

    def barrier(self):
        deps = []
        for t in ALL_TRKS:
            if t.last_w is not None:
                deps.append(t.last_w)
            deps.extend(t.readers.values())
        for e in ENGS:
            ins = Ins(e, None, {}, list(deps), len(self.q[e]))
            self.q[e].append(ins)
        for t in ALL_TRKS:
            t.last_w = None
            t.readers = {}

    def emit(self):
        nc = self.nc
        for e in ENGS:
            for ins in self.q[e]:
                for d in ins.deps:
                    if d[0] == "eng":
                        d[2].milestone = True
        for e in ENGS:
            c = 0
            for ins in self.q[e]:
                if ins.milestone:
                    c += 1
                ins.count = c
        esem = {e: nc.alloc_semaphore(name=f"esem_{e}") for e in ENGS}
        for e in ENGS:
            seen = {}
            for ins in self.q[e]:
                need = {}
                for d in ins.deps:
                    if d[0] == "eng":
                        key, sem, val = ("e", d[1]), esem[d[1]], d[2].count
                    else:
                        key, sem, val = ("d", id(d[1])), d[1], d[2]
                    if seen.get(key, 0) >= val:
                        continue
                    if key not in need or need[key][1] < val:
                        need[key] = (sem, val)
                for key, (sem, val) in need.items():
                    seen[key] = val
                ins.waits = list(need.values())
        stats = {e: (len(self.q[e]), sum(len(i.waits) for i in self.q[e]),
                     sum(1 for i in self.q[e] if i.milestone)) for e in ENGS}
        self.stats = stats

        def replay(e):
            def body(engine):
                for ins in self.q[e]:
                    for sem, val in ins.waits:
                        engine.wait_ge(sem, val)
                    if ins.fn is None:
                        continue
                    r = getattr(engine, ins.fn)(**ins.kw)
                    if ins.dma is not None:
                        if len(ins.dma) == 3:
                            r.then_inc(ins.dma[0], 1)
                        else:
                            r.then_inc(ins.dma[0], 16)
                    elif ins.milestone:
                        r.then_inc(esem[e], 1)
            return body

        with nc.Block() as block:
            block.tensor(replay("tensor"))
            block.vector(replay("vector"))
            block.scalar(replay("scalar"))
            block.gpsimd(replay("gpsimd"))
            block.sync(replay("sync"))
        return stats


D = 2048
SEQ = 4096
NB = 2
TOK = 1024
WIN = 3072
KC = 16
FH = 5632
FC = 44
EPS = 1e-6
INW = 6176
SQD = float(np.sqrt(D))

C_ONES = 0
C_IDENT = 128
C_N = 256


def build_consts():
    c = np.zeros((128, C_N), np.float32)
    c[:, C_ONES:C_ONES + 128] = 1.0
    c[:, C_IDENT:C_IDENT + 128] = np.eye(128, dtype=np.float32)
    return c


class Builder:
    def __init__(self, stage="full"):
        self.stage = stage
        self.nc = bass.Bass("TRN2", target_bir_lowering=False)
        self.P = Prog(self.nc)
        P = self.P
        self.xw = P.dram("xw", [D, WIN], F32, kind="ExternalInput")
        self.nw = P.dram("nw", [128, 48], F32, kind="ExternalInput")
        self.cst = P.dram("cst", [128, C_N2], F32, kind="ExternalInput")
        self.xo = P.dram("xo", [3, D, TOK], F32, kind="ExternalInput")
        self.wlr_o = P.dram("wlr_o", [3, D, 16], F32, kind="ExternalInput")
        self.gw_o = P.dram("gw_o", [3, 16, 512], F32, kind="ExternalInput")
        self.gb_o = P.dram("gb_o", [3, 1, 512], F32, kind="ExternalInput")
        self.flgd = P.dram("flgd", [128, 8], F32, kind="ExternalInput")
        self.gwf = P.dram("gwf", [16, 512], F32, kind="ExternalInput")
        self.gwb = P.dram("gwb", [16, 512], F32, kind="ExternalInput")
        self.gbf = P.dram("gbf", [1, 512], F32, kind="ExternalInput")
        self.gbb = P.dram("gbb", [1, 512], F32, kind="ExternalInput")
        self.gnwd = P.dram("gnwd", [128, 1024], F32, kind="ExternalInput")
        self.w_out = P.dram("w_out", [D, D], F32, kind="ExternalInput")
        self.w_gate = P.dram("w_gate", [D, FH], F32, kind="ExternalInput")
        self.w_up = P.dram("w_up", [D, FH], F32, kind="ExternalInput")
        self.w_down = P.dram("w_down", [FH, D], F32, kind="ExternalInput")
        self.out = P.dram("out", [D, TOK], F32, kind="ExternalOutput")
        if stage == "ffn":
            self.mixd = P.dram("mixd", [D, TOK], F32, kind="ExternalInput")
        self.w_in = P.dram("w_in", [D, INW], F32, kind="ExternalInput")
        self.relt = P.dram("relt", [32, 16], F32, kind="ExternalInput")
        self.ohd = P.dram("ohd", [33, 3, 512], F32, kind="ExternalInput")
        self.valtd = P.dram("valtd", [128, 53], F32, kind="ExternalInput")
        self.uvd = P.dram("uvd", [16, 3, 512], F32)
        if stage in ("attn", "gla"):
            self.dbg = P.dram("dbg", [128, 8, TOK], BF16, kind="ExternalOutput")
        if stage == "gla":
            self.dbg2 = P.dram("dbg2", [128, 2, 4, 256], F32, kind="ExternalOutput")
        top = Region(SB_LO, SB_HI - SB_LO, "top")
        self.R_const = top.sub(5 * 1024, "const")
        self.R_mix = top.sub(32 * 1024, "mix")
        self.R_work = top.sub(SB_HI - top.ptr - 64, "work")
        wb_ = self.R_work.base
        self.RA = Region(wb_, 48 * 1024 + 64, "RA")
        self.RB = Region(wb_ + 48 * 1024 + 64, 78 * 1024, "RB")
        self.RC = Region(self.RB.base + 78 * 1024, self.R_work.base + self.R_work.size - (self.RB.base + 78 * 1024), "RC")
        RC = self.R_const
        self.cf = P.sbuf("cf", [128, C_N2], F32, RC)
        self.cb = P.sbuf("cb", [128, C_N], BF16, RC)
        self.flg = P.sbuf("flg", [128, 8], F32, RC)
        self.onec = P.sbuf("onec", [128, 1], F32, RC)
        P.dma("sync", out=self.flg[:, :], in_=self.flgd[:, :])
        P.op("vector", "memset", ap=self.onec[:, :], constant=1.0)
        self.nws = P.sbuf("nws", [128, 48], F32, RC)
        self.nwq = P.sbuf("nwq", [128, 48], F32, RC)
        P.dma("sync", out=self.cf[:, :], in_=self.cst[:, :])
        P.dma("sync", out=self.nws[:, :], in_=self.nw[:, :])
        P.op("vector", "tensor_copy", out=self.cb[:, :], in_=self.cf[:, 0:C_N])
        P.op("vector", "tensor_copy", out=self.nwq[:, :], in_=self.nws[:, :])
        self.epsc = P.sbuf("epsc", [128, 1], F32, RC)
        P.op("vector", "memset", ap=self.epsc[:, :], constant=EPS)
        self.ones_b = self.cb.v(self.cb.h[:, C_ONES:C_ONES + 128])
        self.ps = [P.psum(f"ps{i}", [128, 512], F32) for i in range(7)]
        self.ps_t = P.psum("pst", [128, 1024], BF16)
        self.ident_b = self.cb.v(self.cb.h[:, C_IDENT:C_IDENT + 128])

    def rms_rstd(self, srcT, rstd, sq, ps):
        P = self.P
        for h in range(2):
            hs = slice(512 * h, 512 * h + 512)
            for kc in range(KC):
                P.op("scalar", "activation", out=sq[:, kc, :], in_=srcT[kc][:, kc, hs], func=AF.Square)
            for kc in range(KC):
                P.op("tensor", "matmul", out=ps[:, :], lhsT=self.ones_b, rhs=sq[:, kc, :],
                     start=(kc == 0), stop=(kc == KC - 1))
            P.op("scalar", "activation", out=rstd[:, hs], in_=ps[:, :], func=AF.Ln, bias=self.epsc[:, 0:1],
                 scale=1.0 / D)
            P.op("scalar", "activation", out=rstd[:, hs], in_=rstd[:, hs], func=AF.Exp, scale=-0.5)

    def phase_out(self, mixT_attn, mixT_gla):
        P = self.P
        ps = self.ps
        RW = self.R_work
        RW.reset()
        x1T_all = P.sbuf("x1T", [128, KC, TOK], F32, RW)
        x1c = [x1T_all.alias(f"x1T{n}") for n in range(KC)]
        rstd = P.sbuf("rstd", [128, TOK], F32, RW)
        mark = RW.ptr
        xsrc = self.xw.h[:, 1024:2048].rearrange("(c p) t -> p c t", p=128)
        for q in range(4):
            P.dma("sync", out=x1c[4 * q][:, 4 * q:4 * q + 4, :], in_=self.xw.v(xsrc[:, 4 * q:4 * q + 4, :]),
                  extra_w=[x1c[4 * q + i].trk for i in range(1, 4)])
        wob = [P.sbuf(f"wob{i}", [128, KC, 256], BF16, RW) for i in range(2)]
        wo_src = self.w_out.h.rearrange("(c p) n -> p c n", p=128)
        for nn in range(8):
            wo = wob[nn % 2]
            cs = slice(256 * nn, 256 * nn + 256)
            P.dma("gpsimd", out=wo[:, :, :], in_=self.w_out.v(wo_src[:, :, cs]))
            for j in range(2):
                n = 2 * nn + j
                for h in range(2):
                    hs = slice(512 * h, 512 * h + 512)
                    pp = ps[(2 * j + h) % 4]
                    for kc in range(KC):
                        rhs = mixT_attn[:, kc, hs] if kc < 8 else mixT_gla[:, kc - 8, hs]
                        P.op("tensor", "matmul", out=pp[:, :], lhsT=wo[:, kc, 128 * j:128 * j + 128],
                             rhs=rhs, start=(kc == 0), stop=(kc == KC - 1))
                    P.op("vector", "tensor_tensor", out=x1c[n][:, n, hs], in0=x1c[n][:, n, hs], in1=pp[:, :],
                         op=ALU.add)
        P.barrier()
        RW.ptr = mark
        self.R_mix.reset()
        hnT = P.sbuf("hnT", [128, KC, TOK], BF16, self.R_mix)
        sq = P.sbuf("sq", [128, KC, 512], BF16, RW)
        self.rms_rstd(x1c, rstd, sq, ps[4])
        for kc in range(KC):
            P.op("vector", "scalar_tensor_tensor", out=hnT[:, kc, :], in0=x1c[kc][:, kc, :],
                 scalar=self.nwq[:, 16 + kc:17 + kc], in1=rstd[:, :], op0=ALU.mult, op1=ALU.mult)
        NG = 11
        wgb = [P.sbuf(f"wgb{i}", [128, KC, 256], BF16, RW) for i in range(2)]
        wub = [P.sbuf(f"wub{i}", [128, KC, 256], BF16, RW) for i in range(2)]
        wdb = [P.sbuf(f"wdb{i}", [128, 4, 1024], BF16, RW) for i in range(2)]
        aT = [P.sbuf(f"aT{i}", [128, 4, TOK], BF16, RW) for i in range(2)]
        sg = [P.sbuf(f"sg{i}", [128, 512], F32, RW) for i in range(2)]
        wg_src = self.w_gate.h.rearrange("(c p) f -> p c f", p=128)
        wu_src = self.w_up.h.rearrange("(c p) f -> p c f", p=128)
        wd_src = self.w_down.h.rearrange("(c p) n -> p c n", p=128)

        def load_gu(si):
            b = si % 2
            fs = slice(256 * si, 256 * si + 256)
            P.dma("gpsimd", out=wgb[b][:, :, :], in_=self.w_gate.v(wg_src[:, :, fs]))
            P.dma("gpsimd", out=wub[b][:, :, :], in_=self.w_up.v(wu_src[:, :, fs]))

        def load_d(gi, nh):
            P.dma("gpsimd", out=wdb[nh][:, :, :],
                  in_=self.w_down.v(wd_src[:, 4 * gi:4 * gi + 4, 1024 * nh:1024 * nh + 1024]))

        dcnt = [0]

        def down_half(gi, nh):
            ab = aT[gi % 2]
            for n in range(8 * nh, 8 * nh + 8):
                for h in range(2):
                    hs = slice(512 * h, 512 * h + 512)
                    pp = ps[4 + dcnt[0] % 2]
                    dcnt[0] += 1
                    nl = n - 8 * nh
                    for j in range(4):
                        P.op("tensor", "matmul", out=pp[:, :], lhsT=wdb[nh][:, j, 128 * nl:128 * nl + 128],
                             rhs=ab[:, j, hs], start=(j == 0), stop=(j == 3))
                    P.op("vector", "tensor_tensor", out=x1c[n][:, n, hs], in0=x1c[n][:, n, hs], in1=pp[:, :],
                         op=ALU.add)

        load_gu(0)
        load_gu(1)
        load_d(0, 0)
        load_d(0, 1)
        cnt = 0
        for gi in range(NG):
            for s_ in range(2):
                si = 2 * gi + s_
                b = si % 2
                for jl in range(2):
                    j = 2 * s_ + jl
                    for h in range(2):
                        hs = slice(512 * h, 512 * h + 512)
                        pg = ps[cnt % 2]
                        pu = ps[2 + cnt % 2]
                        sgt = sg[cnt % 2]
                        cnt += 1
                        for kc in range(KC):
                            P.op("tensor", "matmul", out=pg[:, :], lhsT=wgb[b][:, kc, 128 * jl:128 * jl + 128],
                                 rhs=hnT[:, kc, hs], start=(kc == 0), stop=(kc == KC - 1))
                        for kc in range(KC):
                            P.op("tensor", "matmul", out=pu[:, :], lhsT=wub[b][:, kc, 128 * jl:128 * jl + 128],
                                 rhs=hnT[:, kc, hs], start=(kc == 0), stop=(kc == KC - 1))
                        P.op("scalar", "activation", out=sgt[:, :], in_=pg[:, :], func=AF.Silu)
                        P.op("vector", "tensor_tensor", out=aT[gi % 2][:, j, hs], in0=sgt[:, :], in1=pu[:, :],
                             op=ALU.mult)
                if si + 2 < 2 * NG:
                    load_gu(si + 2)
                if gi > 0:
                    down_half(gi - 1, s_)
                    load_d(gi, s_)
        down_half(NG - 1, 0)
        down_half(NG - 1, 1)
        self.rms_rstd(x1c, rstd, sq, ps[6])
        for kc in range(KC):
            P.op("vector", "scalar_tensor_tensor", out=x1c[kc][:, kc, :], in0=x1c[kc][:, kc, :],
                 scalar=self.nwq[:, 32 + kc:33 + kc], in1=rstd[:, :], op0=ALU.mult, op1=ALU.mult)
        odst = self.out.h.rearrange("(c p) t -> p c t", p=128)
        for n in range(KC):
            P.dma("sync", out=self.out.v(odst[:, n, :]), in_=x1c[n][:, n, :])
        P.wait_all("sync", [self.out.trk])

    def build(self):
        P = self.P
        if self.stage == "attn":
            attnT = P.sbuf("attnT", [128, 8, TOK], BF16, self.R_mix)
            self.xwT = P.sbuf("xwT", [128, KC, WIN], BF16, self.R_work)
            self.attn_setup()
            self.phase_window()
            self.phase_attn(attnT)
            P.dma("sync", out=self.dbg[:, :, :], in_=attnT[:, :, :])
            P.wait_all("sync", [self.dbg.trk])
        if self.stage == "full":
            attnT = P.sbuf("attnT", [128, 8, TOK], BF16, self.R_mix)
            glaT = P.sbuf("glaT", [128, 8, TOK], BF16, self.R_mix)
            RS_ = Region(self.R_mix.base, 16 * 1024, "states")
            SF = P.sbuf("SF", [128, 4, 256], F32, RS_)
            SB = P.sbuf("SB", [128, 4, 256], F32, RS_)
            self.R_work.reset()
            self.attn_setup()
            self.phase_gla_others(SF, SB)
            self.phase_gla_own(SF, SB, glaT)
            self.R_work.reset()
            self.xwT = P.sbuf("xwT", [128, KC, WIN], BF16, self.R_work)
            self.phase_window()
            self.phase_attn(attnT)
            self.phase_out(attnT, glaT)
        if self.stage == "gla":
            glaT = P.sbuf("glaT", [128, 8, TOK], BF16, self.R_mix)
            SF = P.sbuf("SF", [128, 4, 256], F32, self.R_mix)
            SB = P.sbuf("SB", [128, 4, 256], F32, self.R_mix)
            self.phase_gla_others(SF, SB)
            P.dma("sync", out=self.dbg2[:, 0, :, :], in_=SF[:, :, :])
            P.dma("sync", out=self.dbg2[:, 1, :, :], in_=SB[:, :, :])
            if not os.environ.get("GLA_OTHERS_ONLY"):
                self.phase_gla_own(SF, SB, glaT)
            P.dma("sync", out=self.dbg[:, :, :], in_=glaT[:, :, :])
            P.wait_all("sync", [self.dbg.trk, self.dbg2.trk])
        if self.stage == "ffn":
            ma = P.sbuf("mixa", [128, 8, TOK], BF16, self.R_mix)
            mg = P.sbuf("mixg", [128, 8, TOK], BF16, self.R_mix)
            P.dma("gpsimd", out=ma[:, :, :],
                  in_=self.mixd.v(self.mixd.h[0:1024, :].rearrange("(c p) t -> p c t", p=128)))
            P.dma("gpsimd", out=mg[:, :, :],
                  in_=self.mixd.v(self.mixd.h[1024:2048, :].rearrange("(c p) t -> p c t", p=128)))
            self.phase_out(ma, mg)
        self.stats = P.emit()
        return self.nc


PATS = ((1, 128), (4, 128), (16, 64))


def _t5_bucket_np(rel):
    nb = 16
    ret = np.where(rel > 0, nb, 0)
    n = np.abs(rel)
    max_exact = nb // 2
    large = max_exact + (np.log(np.maximum(n, 1) / max_exact) / np.log(1024 / max_exact)
                         * (nb - max_exact)).astype(np.int64)
    large = np.minimum(large, nb - 1)
    return (ret + np.where(n < max_exact, n, large)).astype(np.int32)


def build_onehot():
    oh = np.zeros((33, 3, 512), np.float32)
    for pi, (r, _) in enumerate(PATS):
        for i in range(512):
            dlt = i - 255
            if abs(dlt) <= 64:
                oh[int(_t5_bucket_np(np.array(dlt * r))), pi, i] = 1.0
            else:
                oh[32, pi, i] = 1.0
    return oh


def build_valt(start):
    v = np.zeros((128, 53), np.float32)
    k = np.arange(128)

    def ok(w):
        t = start - 1024 + w
        return ((t >= 0) & (t < SEQ)).astype(np.float32)
    for j in range(9):
        v[:, j] = ok(960 + 128 * j + k)
    for p in range(4):
        for j in range(3):
            v[:, 9 + 3 * p + j] = ok(4 * (192 + 128 * j + k) + p)
    for p in range(16):
        v[:, 21 + 2 * p] = ok(16 * k + p)
        vb = ok(16 * (128 + k) + p)
        vb[64:] = 0.0
        v[:, 21 + 2 * p + 1] = vb
    return v


def _attn_setup(self):
    P = self.P
    RW = Region(self.R_mix.base + 16 * 1024, 16 * 1024, "setup")
    m = RW.ptr
    tab = P.sbuf("tab", [33, 16], F32, RW)
    ohs = P.sbuf("ohs", [33, 3, 512], F32, RW)
    ev = P.sbuf("ev", [16, 3, 512], F32, RW)
    P.op("vector", "memset", ap=tab[32:33, :], constant=-10000.0)
    P.dma("sync", out=tab[0:32, :], in_=self.relt[:, :])
    P.dma("sync", out=ohs[:, :, :], in_=self.ohd[:, :, :])
    for pi in range(3):
        pp = self.ps[pi]
        P.op("tensor", "matmul", out=pp[0:16, :], lhsT=tab[:, :], rhs=ohs[:, pi, :], start=True, stop=True)
        P.op("scalar", "activation", out=ev[:, pi, :], in_=pp[0:16, :], func=AF.Exp)
    P.dma("sync", out=self.uvd[:, :, :], in_=ev[:, :, :])
    RW.ptr = m


def _phase_window(self):
    P = self.P
    RW = self.R_work
    m = RW.ptr
    NT = 256
    xs = [P.sbuf(f"xs{i}", [128, KC, NT], F32, RW) for i in range(3)]
    sqs = [P.sbuf(f"sqw{i}", [128, KC, NT], BF16, RW) for i in range(2)]
    rss = [P.sbuf(f"rsw{i}", [128, NT], F32, RW) for i in range(2)]
    src = self.xw.h.rearrange("(c p) t -> p c t", p=128)
    for wt in range(WIN // NT):
        x = xs[wt % 3]
        sq, rs = sqs[wt % 2], rss[wt % 2]
        ts = slice(NT * wt, NT * wt + NT)
        P.dma("sync", out=x[:, :, :], in_=self.xw.v(src[:, :, ts]))
        P.op("scalar", "activation", out=sq[:, :, :], in_=x[:, :, :], func=AF.Square)
        pp = self.ps[5 + wt % 2]
        for kc in range(KC):
            P.op("tensor", "matmul", out=pp[:, 0:NT], lhsT=self.ones_b, rhs=sq[:, kc, :],
                 start=(kc == 0), stop=(kc == KC - 1))
        P.op("scalar", "activation", out=rs[:, :], in_=pp[:, 0:NT], func=AF.Ln, bias=self.epsc[:, 0:1],
             scale=1.0 / D)
        P.op("scalar", "activation", out=rs[:, :], in_=rs[:, :], func=AF.Exp, scale=-0.5)
        for kc in range(KC):
            P.op("vector", "scalar_tensor_tensor", out=self.xwT[:, kc, ts], in0=x[:, kc, :],
                 scalar=self.nwq[:, kc:kc + 1], in1=rs[:, :], op0=ALU.mult, op1=ALU.mult)
    P.barrier()
    RW.ptr = m


import os
NDUM_DEFAULT = 0


def _phase_attn(self, attnT):
    P = self.P
    ps = self.ps
    RW = self.R_work
    m = RW.ptr
    NWB = 3
    wb = [P.sbuf(f"awb{i}", [128, KC, 128], BF16, RW) for i in range(NWB)]
    Qn = P.sbuf("Qn", [128, 1024], BF16, RW)
    Qd4 = P.sbuf("Qd4", [128, 4, 256], BF16, RW)
    Qd16 = P.sbuf("Qd16", [128, 16, 64], BF16, RW)
    Kn = P.sbuf("Kn", [128, 1152], BF16, RW)
    Kd4 = P.sbuf("Kd4", [128, 4, 384], BF16, RW)
    Kd16 = P.sbuf("Kd16", [128, 16, 192], BF16, RW)
    Vn = P.sbuf("Vn", [128, 1152], BF16, RW)
    Vd4 = P.sbuf("Vd4", [128, 4, 384], BF16, RW)
    Vd16 = P.sbuf("Vd16", [128, 16, 192], BF16, RW)
    G = P.sbuf("G", [128, 6, 2, 128], F32, RW)
    es = [P.sbuf(f"es{i}", [128, 512], F32, RW) for i in range(3)]
    pT = [P.sbuf(f"pT{i}", [128, 512], BF16, RW) for i in range(3)]
    acc = [P.sbuf("accA", [65, 1024], F32, RW), P.sbuf("accB", [128, 1024], F32, RW)]
    NVT = 16
    VT0 = P.sbuf("VT0", [128, NVT, 194], BF16, RW)
    VTs = [VT0, VT0]
    valt = P.sbuf("valt", [128, 53], F32, RW)
    P.dma("sync", out=valt[:, :], in_=self.valtd[:, :])
    P.op("vector", "memset", ap=VT0[:, :, :], constant=0.0)
    ps_s = [ps[0], ps[1], ps[6]]
    ps_o = [ps[2], ps[3]]
    ps_t = self.ps_t
    ps_p = [ps[4], ps[5]]
    ps_b = ps[6]
    win_src = self.w_in.h.rearrange("(c p) n -> p c n", p=128)
    wcnt = [0]
    scnt = [0]
    pcnt = [0]
    ecnt = [0]
    ocnt = [0]

    def load_w(col0):
        b = wb[wcnt[0] % NWB]
        wcnt[0] += 1
        P.dma("gpsimd", out=b[:, :, :], in_=self.w_in.v(win_src[:, :, col0:col0 + 128]))
        return b

    def proj(wt_, tok0):
        pp = ps_p[pcnt[0] % 2]
        pcnt[0] += 1
        for kc in range(KC):
            P.op("tensor", "matmul", out=pp[:, :], lhsT=wt_[:, kc, :], rhs=self.xwT[:, kc, tok0:tok0 + 512],
                 start=(kc == 0), stop=(kc == KC - 1))
        return pp

    def evac(eng, out, in_):
        em = os.environ.get("EVAC", "")
        if em == "none":
            return
        if em == "vec":
            eng = "vector"
        if em == "nostride" and len(in_.ap.ap) > 2:
            return
        if eng == "scalar":
            P.op("scalar", "activation", out=out, in_=in_, func=AF.Copy)
        else:
            P.op("vector", "tensor_copy", out=out, in_=in_)

    def kv_evac(pp, wt, Nn, D4, D16):
        if wt == 1:
            evac("scalar", Nn[:, 0:64], V(pp.ap[:, 448:512], pp.trk))
        elif wt in (2, 3):
            evac("scalar", Nn[:, 64 + 512 * (wt - 2):64 + 512 * (wt - 2) + 512], pp)
        elif wt == 4:
            evac("scalar", Nn[:, 1088:1152], V(pp.ap[:, 0:64], pp.trk))
        if wt == 1:
            evac("vector", D4[:, :, 0:64], V(pp.ap[:, 256:512].rearrange("q (l p) -> q p l", p=4), pp.trk))
        elif wt in (2, 3):
            l0 = 64 + 128 * (wt - 2)
            evac("vector", D4[:, :, l0:l0 + 128], V(pp.ap.rearrange("q (l p) -> q p l", p=4), pp.trk))
        elif wt == 4:
            evac("vector", D4[:, :, 320:384], V(pp.ap[:, 0:256].rearrange("q (l p) -> q p l", p=4), pp.trk))
        evac("vector", D16[:, :, 32 * wt:32 * wt + 32],
             V(pp.ap.rearrange("q (l p) -> q p l", p=16), pp.trk))

    LVL = int(os.environ.get("ATTN_LVL", "9"))
    NDUM = int(os.environ.get("ATTN_DUMMY", str(NDUM_DEFAULT)))
    NCH = int(os.environ.get("ATTN_NCH", "8"))

    def norm_chunk(c):
        banks = [ps_s[0], ps_s[1], ps_o[0], ps_o[1]]
        for hl in range(2):
            a = acc[hl]
            dr = 64 if hl == 0 else 0
            P.op("vector", "reciprocal", out=a[dr:dr + 1, :], in_=a[dr:dr + 1, :])
        for hl in range(2):
            a = acc[hl]
            dr = 64 if hl == 0 else 0
            for tt in range(2):
                ts = slice(512 * tt, 512 * tt + 512)
                P.op("tensor", "matmul", out=banks[2 * hl + tt][64 * hl:64 * hl + 64, :],
                     lhsT=self.cf.v(self.cf.h[dr:dr + 1, C_ONES:C_ONES + 64]), rhs=a[dr:dr + 1, ts],
                     start=True, stop=True)
        for hl in range(2):
            a = acc[hl]
            for tt in range(2):
                ts = slice(512 * tt, 512 * tt + 512)
                P.op("vector", "tensor_tensor", out=attnT[64 * hl:64 * hl + 64, c, ts],
                     in0=a[64 * hl:64 * hl + 64, ts], in1=banks[2 * hl + tt][64 * hl:64 * hl + 64, :], op=ALU.mult)

    for c in range(NCH):
        srcap = bass.AP(tensor=self.uvd.h, offset=(2 * c * 3) * 512 + 64,
                        ap=[[1, 128], [512, 6], [128, 2], [1, 128]])
        for e_ in range(int(os.environ.get("NG_", "12"))):
            hl_, rem = divmod(e_, 6)
            pi_, tl_ = divmod(rem, 2)
            srcap = bass.AP(tensor=self.uvd.h, offset=((2 * c + hl_) * 3 + pi_) * 512 + 64 + 128 * tl_,
                            ap=[[1, 128], [1, 128]])
            P.dma("sync", out=G[:, hl_ * 3 + pi_, tl_, :], in_=self.uvd.v(srcap))
        if c == 0:
            wts = [load_w(128 * c), load_w(1024 + 128 * c), load_w(2048 + 128 * c)]
        wq, wk, wv = wts
        for tt in range(2):
            pp = proj(wq, 1024 + 512 * tt)
            evac("scalar", Qn[:, 512 * tt:512 * tt + 512], pp[:, :])
            evac("vector", Qd4[:, :, 128 * tt:128 * tt + 128],
                 V(pp.h[:, :].rearrange("q (l p) -> q p l", p=4), pp.trk))
            evac("vector", Qd16[:, :, 32 * tt:32 * tt + 32],
                 V(pp.h[:, :].rearrange("q (l p) -> q p l", p=16), pp.trk))
        if c > 0:
            norm_chunk(c - 1)
        for wt in range(6):
            pp = proj(wk, 512 * wt)
            kv_evac(V(pp.h[:, :], pp.trk), wt, Kn, Kd4, Kd16)
        for wt in range(6):
            pp = proj(wv, 512 * wt)
            kv_evac(V(pp.h[:, :], pp.trk), wt, Vn, Vd4, Vd16)
        if c + 1 < NCH:
            wts = [load_w(128 * (c + 1)), load_w(1024 + 128 * (c + 1)), load_w(2048 + 128 * (c + 1))]
        sets_ = []
        for pi, (r, QB) in enumerate(PATS):
            if r == 1:
                sets_.append((pi, r, QB, 0, [(Kn[:, 128 * j:128 * j + 128], Vn[:, 128 * j:128 * j + 128], j, 128)
                                             for j in range(9)]))
            elif r == 4:
                sets_.append((pi, r, QB, 0, [(Kd4[:, p, 128 * j:128 * j + 128], Vd4[:, p, 128 * j:128 * j + 128],
                                              9 + 3 * p + j, 128) for p in range(4) for j in range(3)]))
            else:
                for hf in range(2):
                    sets_.append((pi, r, QB, hf,
                                  [(Kd16[:, p, 128 * t:128 * t + (128 if t == 0 else 64)],
                                    Vd16[:, p, 128 * t:128 * t + (128 if t == 0 else 64)], 21 + 2 * p + t,
                                    128 if t == 0 else 64) for p in range(8 * hf, 8 * hf + 8) for t in range(2)]))

        def build_set(tiles, VT):
            pstride = VT.h[:, :, :].ap[0][0]
            v0 = tiles[0][2]
            for t0 in range(0, len(tiles), 4):
                nt_ = min(4, len(tiles) - t0)
                for ti in range(t0, t0 + nt_):
                    ksrc, vsrc, vcol, nk = tiles[ti]
                    tcol = 128 * (ti - t0)
                    P.op("tensor", "transpose", out=ps_t[0:nk, tcol:tcol + 128], in_=vsrc, identity=self.ident_b)
                oap = bass.AP(tensor=VT.h, offset=VT.h[:, t0, 0:1].offset,
                              ap=[[pstride, 128], [194, nt_], [129, 2], [1, 64]])
                P.op("vector", "tensor_copy", out=VT.v(oap),
                     in_=ps_t.v(ps_t.h[:, 0:128 * nt_].rearrange("k (t a b) -> k t a b", t=nt_, a=2)))
                P.op("gpsimd", "tensor_copy", out=VT[:, t0:t0 + nt_, 64:66],
                     in_=valt.v(valt.h[:, v0 + t0:v0 + t0 + nt_].unsqueeze(2).broadcast_to([128, nt_, 2])))

        for si, (pi, r, QB, hf, tiles) in enumerate(sets_):
            VT = VTs[0]
            build_set(tiles, VT)
            if True:
                if LVL < 3:
                    continue
                if r == 1:
                    groups = [[(hl, Qn[:, :], 128 * (2 * g + b), 2 * g + b, 2 * g + b + 1) for b in range(2)]
                              for g in range(4) for hl in range(2)]
                elif r == 4:
                    groups = [[(hl, Qd4[:, p, :], 128 * b, 3 * p + b, 3 * p + b + 1) for b in range(2)]
                              for p in range(4) for hl in range(2)]
                else:
                    groups = [[(hl, Qd16[:, 8 * hf + 4 * g + b, :], 0, 2 * (4 * g + b), 2 * (4 * g + b) + 1)
                               for b in range(4)] for g in range(2) for hl in range(2)]
                def stage1(grp):
                    hl = grp[0][0]
                    hp = slice(64 * hl, 64 * hl + 64)
                    sp = ps_s[ecnt[0] % 3]
                    est = es[ecnt[0] % 3]
                    ptt = pT[ecnt[0] % 3]
                    ecnt[0] += 1
                    nb = len(grp)
                    W = 2 * QB
                    for bi, (_, qsrc, q0, ta, tb) in enumerate(grp):
                        qap = qsrc.ap[hp, q0:q0 + QB]
                        qtrk = qsrc.trk
                        for t_, tix in enumerate((ta, tb)):
                            ksrc, vsrc, vcol, nk = tiles[tix]
                            P.op("tensor", "matmul", out=sp[0:nk, W * bi + QB * t_:W * bi + QB * t_ + QB],
                                 lhsT=V(ksrc.ap[hp], ksrc.trk), rhs=V(qap, qtrk), start=True, stop=True)
                    nkb = tiles[grp[0][4]][3]
                    for _ in range(NDUM):
                        P.op("tensor", "matmul", out=ps_p[0][:, :], lhsT=self.ident_b, rhs=self.xwT[:, 0, 0:512],
                             start=True, stop=True)
                    P.op("scalar", "activation", out=est[:, 0:W * nb], in_=sp[:, 0:W * nb], func=AF.Exp,
                         scale=0.125)
                    gsl = G.h[:, hl * 3 + pi, :, :]
                    erev = gsl[:, :, ::-1][:, :, 0:QB]
                    ebc = erev.unsqueeze(1).broadcast_to([128, nb, 2, QB])
                    P.op("vector", "tensor_tensor",
                         out=ptt.v(ptt.h[:, 0:W * nb].rearrange("k (b t q) -> k b t q", b=nb, t=2)),
                         in0=est.v(est.h[:, 0:W * nb].rearrange("k (b t q) -> k b t q", b=nb, t=2)),
                         in1=G.v(ebc), op=ALU.mult)
                    return (hl, ptt, W, nb)

                def stage2(grp, ctx):
                    hl, ptt, W, nb = ctx
                    op_ = ps_o[ocnt[0] % 2]
                    ocnt[0] += 1
                    rows = 65 if hl == 0 else 128
                    for bi, (_, qsrc, q0, ta, tb) in enumerate(grp):
                        for t_, tix in enumerate((ta, tb)):
                            ksrc, vsrc, vcol, nk = tiles[tix]
                            lh = VT[0:nk, tix, 0:65] if hl == 0 else VT[0:nk, tix, 65:193]
                            P.op("tensor", "matmul", out=op_[0:rows, QB * bi:QB * bi + QB], lhsT=lh,
                                 rhs=ptt[0:nk, W * bi + QB * t_:W * bi + QB * t_ + QB],
                                 start=(t_ == 0), stop=(t_ == 1))
                    a = acc[hl]
                    if r == 1:
                        g0 = grp[0][2]
                        P.op("scalar", "activation", out=a[0:rows, g0:g0 + 256], in_=op_[0:rows, 0:256],
                             func=AF.Copy)
                    elif r == 4:
                        p = grp[0][3] // 3
                        dst = a.h[0:rows, :].rearrange("r (l p) -> r p l", p=4)[:, p, :]
                        P.op("vector", "tensor_tensor", out=a.v(dst), in0=a.v(dst), in1=op_[0:rows, 0:256],
                             op=ALU.add)
                    else:
                        p0 = 8 * hf + grp[0][3] // 2
                        dst = a.h[0:rows, :].rearrange("r (l p) -> r p l", p=16)[:, p0:p0 + 4, :]
                        P.op("vector", "tensor_tensor", out=a.v(dst), in0=a.v(dst),
                             in1=op_.v(op_.h[0:rows, 0:256].rearrange("r (p l) -> r p l", p=4)), op=ALU.add)
                pend = []
                for grp in groups:
                    ctx = stage1(grp)
                    pend.append((grp, ctx))
                    if len(pend) > 2:
                        stage2(*pend.pop(0))
                while pend:
                    stage2(*pend.pop(0))
    if NCH > 0:
        norm_chunk(NCH - 1)
    P.barrier()
    RW.ptr = m


Builder.attn_setup = _attn_setup
Builder.phase_window = _phase_window
Builder.phase_attn = _phase_attn


C_TRIF, C_TRIB, C_TRIRF, C_TRIRB, C_CI = 256, 384, 512, 640, 768
C_N2 = 772


def build_consts2():
    c = np.zeros((128, C_N2), np.float32)
    c[:, :C_N] = build_consts()
    s = np.arange(128)[:, None]
    t = np.arange(128)[None, :]
    same = (s // 64) == (t // 64)
    c[:, C_TRIF:C_TRIF + 128] = (same & (s <= t))
    c[:, C_TRIB:C_TRIB + 128] = (same & (s >= t))
    c[:, C_TRIRF:C_TRIRF + 128] = (same & (s > t))
    c[:, C_TRIRB:C_TRIRB + 128] = (same & (s < t))
    c[:, C_CI] = (np.arange(128) < 64)
    c[:, C_CI + 1] = (np.arange(128) >= 64)
    return c


def _cfv(self, c0, n, rows=slice(0, 128)):
    return self.cf.v(self.cf.h[rows, c0:c0 + n])


def _norm_tile(self, src_ap, x, sq, rs, dst_fn, nt, pp=None):
    P = self.P
    P.dma("sync", out=x[:, :, :], in_=src_ap)
    P.op("scalar", "activation", out=sq[:, :, :], in_=x[:, :, :], func=AF.Square)
    if pp is None:
        pp = self.ps[6]
    for kc in range(KC):
        P.op("tensor", "matmul", out=pp[:, 0:nt], lhsT=self.ones_b, rhs=sq[:, kc, :],
             start=(kc == 0), stop=(kc == KC - 1))
    P.op("scalar", "activation", out=rs[:, :], in_=pp[:, 0:nt], func=AF.Ln, bias=self.epsc[:, 0:1],
         scale=1.0 / D)
    P.op("scalar", "activation", out=rs[:, :], in_=rs[:, :], func=AF.Exp, scale=-0.5)
    for kc in range(KC):
        P.op("vector", "scalar_tensor_tensor", out=dst_fn(kc), in0=x[:, kc, :],
             scalar=self.nwq[:, kc:kc + 1], in1=rs[:, :], op0=ALU.mult, op1=ALU.mult)


def _gate_l(self, lrT, gw, gb, l_out, pz, ework):
    P = self.P
    P.op("tensor", "matmul", out=pz[:, :], lhsT=lrT, rhs=gw[:, :], start=True, stop=True)
    P.op("scalar", "activation", out=ework[:, :], in_=pz[:, :], func=AF.Exp, scale=-1.0)
    P.op("scalar", "activation", out=l_out, in_=ework[:, :], func=AF.Ln, bias=self.onec[:, 0:1], scale=1.0)


def _phase_gla_others(self, SF, SB):
    P = self.P
    ps = self.ps
    RA, RW, RC = self.RA, self.RB, self.RC
    RA.reset()
    RW.reset()
    RC.reset()
    NT = 256
    Wgk = P.sbuf("Wgk", [128, KC, 512], BF16, RA)
    Wgv = P.sbuf("Wgv", [128, KC, 1024], BF16, RA)
    self.Wgk, self.Wgv = Wgk, Wgv
    wsrc = self.w_in.h.rearrange("(c p) n -> p c n", p=128)
    for q in range(2):
        P.dma("gpsimd", out=Wgk[:, 8 * q:8 * q + 8, :], in_=self.w_in.v(wsrc[:, 8 * q:8 * q + 8, 3584:4096]))
    for q in range(4):
        P.dma("gpsimd", out=Wgv[:, 4 * q:4 * q + 4, :], in_=self.w_in.v(wsrc[:, 4 * q:4 * q + 4, 4096:5120]))
    xs = [P.sbuf(f"oxs{i}", [128, KC, NT], F32, RW) for i in range(2)]
    sqs = [P.sbuf(f"osq{i}", [128, KC, NT], BF16, RW) for i in range(2)]
    rss = [P.sbuf(f"ors{i}", [128, NT], F32, RW) for i in range(2)]
    xn = [P.sbuf(f"oxn{i}", [128, KC, NT], BF16, RW) for i in range(3)]
    wlr = [P.sbuf(f"owlr{i}", [128, KC, 16], BF16, RC) for i in range(3)]
    gw = [P.sbuf(f"ogw{i}", [17, 512], F32, RC) for i in range(3)]
    gb = [None, None, None]
    lrT = [P.sbuf(f"olrT{i}", [17, 128], F32, RC) for i in range(2)]
    for i in range(2):
        P.op("vector", "memset", ap=lrT[i][:, :], constant=1.0)
    vtm = [P.sbuf(f"ovtm{i}", [128, 1024], BF16, RC) for i in range(2)]
    ew = P.sbuf("oew", [128, 512], F32, RC)
    l = [P.sbuf(f"ol{i}", [128, 512], F32, RC) for i in range(2)]
    wend = P.sbuf("owend", [128, 512], F32, RC)
    kend = [P.sbuf(f"okend{i}", [128, 512], BF16, RC) for i in range(2)]
    dec = [P.sbuf(f"odec{i}", [128, 4, 2], F32, RC) for i in range(2)]
    csum = P.sbuf("ocsum", [128, 4], F32, RC)
    dtot = P.sbuf("odtot", [128, 4], F32, RC)
    dF = P.sbuf("odF", [128, 4], F32, RC)
    Rj_all = P.sbuf("oRj", [128, 4, 256], F32, RC)
    Rj = [Rj_all.alias(f"oRj{h}") for h in range(4)]
    P.op("vector", "memset", ap=SF[:, :, :], constant=0.0)
    P.op("vector", "memset", ap=SB[:, :, :], constant=0.0)
    xsrc = self.xo.h.rearrange("j (c p) t -> j p c t", p=128)
    for j in range(3):
        P.dma("gpsimd", out=wlr[j][:, :, :],
              in_=self.wlr_o.v(self.wlr_o.h[j].rearrange("(c p) n -> p c n", p=128)))
        P.dma("sync", out=gw[j][0:16, :], in_=self.gw_o[j])
        P.dma("sync", out=gw[j][16:17, :], in_=self.gb_o[j])
    subs = [(j, tt, sub) for j in range(3) for tt in range(TOK // NT) for sub in range(NT // 128)]
    state = {"tcnt": 0}

    tiles_ = [(j, tt) for j in range(3) for tt in range(TOK // NT)]

    def stage0(t):
        j, tt = tiles_[t]
        x, xnn = xs[t % 2], xn[t % 3]
        _norm_tile(self, self.xo.v(xsrc[j, :, :, NT * tt:NT * tt + NT]), x, sqs[t % 2], rss[t % 2],
                   lambda kc: xnn[:, kc, :], NT, ps[3])

    def stage1(idx):
        j, tt, sub = subs[idx]
        xnn = xn[(idx // 2) % 3]
        tsl = slice(128 * sub, 128 * sub + 128)
        pk = ps[idx % 2]
        for kc in range(KC):
            P.op("tensor", "matmul", out=pk[:, :], lhsT=xnn[:, kc, tsl], rhs=Wgk[:, kc, :],
                 start=(kc == 0), stop=(kc == KC - 1))
        vt = vtm[idx % 2]
        for hv in range(2):
            pv = ps[2]
            for kc in range(KC):
                P.op("tensor", "matmul", out=pv[:, :], lhsT=xnn[:, kc, tsl],
                     rhs=Wgv[:, kc, 512 * hv:512 * hv + 512], start=(kc == 0), stop=(kc == KC - 1))
            P.op("scalar", "activation", out=vt[:, 512 * hv:512 * hv + 512], in_=pv[:, :], func=AF.Copy)
        pl = ps[4]
        for kc in range(KC):
            P.op("tensor", "matmul", out=pl[0:16, 0:128], lhsT=wlr[j][:, kc, :], rhs=xnn[:, kc, tsl],
                 start=(kc == 0), stop=(kc == KC - 1))
        P.op("vector", "tensor_copy", out=lrT[idx % 2][0:16, :], in_=pl[0:16, 0:128])

    def stage2(idx):
        j, tt, sub = subs[idx]
        pk = ps[idx % 2]
        vt = vtm[idx % 2]
        lt = l[idx % 2]
        ke = kend[idx % 2]
        de = dec[idx % 2]
        if tt == 0 and sub == 0:
            for h in range(4):
                P.op("vector", "memset", ap=Rj[h][:, h, :], constant=0.0)
            P.op("vector", "memset", ap=csum[:, :], constant=0.0)
        _gate_l(self, lrT[idx % 2][:, :], gw[j], gb[j], lt[:, :], ps[3], ew)
        pr = ps[5]
        P.op("tensor", "matmul", out=pr[:, :], lhsT=_cfv(self, C_TRIRF, 128), rhs=lt[:, :], start=True, stop=True)
        P.op("scalar", "activation", out=wend[:, :], in_=pr[:, :], func=AF.Exp, scale=-1.0 / 16)
        P.op("vector", "tensor_tensor", out=ke[:, :], in0=wend[:, :], in1=pk[:, :], op=ALU.mult)
        pc = ps[4]
        for h in range(4):
            P.op("tensor", "matmul", out=pc[:, 256 + 2 * h:256 + 2 * h + 2], lhsT=lt[:, 128 * h:128 * h + 128],
                 rhs=_cfv(self, C_CI, 2), start=True, stop=True)
        pcv = pc.v(pc.h[:, 256:264].rearrange("d (h c) -> d h c", h=4))
        P.op("scalar", "activation", out=de[:, :, :], in_=pcv, func=AF.Exp, scale=-1.0 / 16)
        P.op("vector", "tensor_reduce", out=dtot[:, :], in_=pcv, axis=AX.X, op=ALU.add)
        P.op("vector", "tensor_tensor", out=csum[:, :], in0=csum[:, :], in1=dtot[:, :], op=ALU.add)
        for ci in range(2):
            cp = slice(64 * ci, 64 * ci + 64)
            for h in range(4):
                pu = ps[5 + h % 2]
                P.op("tensor", "matmul", out=pu[:, 0:256], lhsT=ke[cp, 128 * h:128 * h + 128],
                     rhs=vt[cp, 256 * h:256 * h + 256], start=True, stop=True)
                P.op("vector", "scalar_tensor_tensor", out=Rj[h][:, h, :], in0=Rj[h][:, h, :],
                     scalar=de[:, h, ci:ci + 1], in1=pu[:, 0:256], op0=ALU.mult, op1=ALU.add)
        if tt == TOK // NT - 1 and sub == NT // 128 - 1:
            P.op("scalar", "activation", out=dtot[:, :], in_=csum[:, :], func=AF.Exp, scale=-1.0 / 16)
            for (S, fc) in ((SF, j), (SB, 3 + j)):
                P.op("vector", "tensor_scalar", out=dF[:, :], in0=dtot[:, :], scalar1=-1.0,
                     scalar2=self.flg[:, fc:fc + 1], op0=ALU.add, op1=ALU.mult)
                P.op("vector", "tensor_scalar", out=dF[:, :], in0=dF[:, :], scalar1=1.0, scalar2=None,
                     op0=ALU.add)
                for h in range(4):
                    P.op("vector", "tensor_scalar", out=S[:, h, :], in0=S[:, h, :], scalar1=dF[:, h:h + 1],
                         scalar2=None, op0=ALU.mult)
                    P.op("vector", "scalar_tensor_tensor", out=S[:, h, :], in0=Rj[h][:, h, :],
                         scalar=self.flg[:, fc:fc + 1], in1=S[:, h, :], op0=ALU.mult, op1=ALU.add)

    stage0(0)
    stage0(1)
    stage1(0)
    for idx in range(len(subs)):
        if idx % 2 == 0 and idx // 2 + 2 < len(tiles_):
            stage0(idx // 2 + 2)
        if idx + 1 < len(subs):
            stage1(idx + 1)
        stage2(idx)
    P.barrier()
    RW.reset()
    RC.reset()


Builder.phase_gla_others = _phase_gla_others


def _phase_gla_own(self, SF, SB, glaT):
    P = self.P
    ps = self.ps
    RA, RW, RC = self.RA, self.RB, self.RC
    RW.reset()
    RC.reset()
    Wgk, Wgv = self.Wgk, self.Wgv
    NT = 256
    wsrc = self.w_in.h.rearrange("(c p) n -> p c n", p=128)
    gqT = P.sbuf("gqT", [128, 4, TOK], BF16, RW)
    gkT = P.sbuf("gkT", [128, 4, TOK], BF16, RW)
    gktm = P.sbuf("gktm", [128, 8, 512], BF16, RW)
    vtm = P.sbuf("gvtm", [128, 8, 1024], BF16, RW)
    sgr = P.sbuf("sgr", [128, 8, 1024], BF16, RW)
    lrT = [P.sbuf(f"glrT{d}", [17, TOK], F32, RW) for d in range(2)]
    gw = [P.sbuf(f"ggw{d}", [17, 512], F32, RW) for d in range(2)]
    gb = [None, None]
    gnw = P.sbuf("gnw", [128, 1024], F32, RW)
    xn = P.sbuf("gxn", [128, KC, TOK], BF16, RC)
    wq = [P.sbuf(f"gwq{i}", [128, KC, 128], BF16, RC) for i in range(1)]
    wlr = P.sbuf("gwlr", [128, KC, 32], BF16, RC)
    RS = Region(RW.base, 55 * 1024, "stage")
    xs = [P.sbuf(f"gxs{i}", [128, KC, NT], F32, RS) for i in range(2)]
    sqs = [P.sbuf(f"gsq{i}", [128, KC, NT], BF16, RS) for i in range(2)]
    rss = [P.sbuf(f"grs{i}", [128, NT], F32, RS) for i in range(2)]
    P.dma("sync", out=gw[0][0:16, :], in_=self.gwf[:, :])
    P.dma("sync", out=gw[1][0:16, :], in_=self.gwb[:, :])
    P.dma("sync", out=gw[0][16:17, :], in_=self.gbf[:, :])
    P.dma("sync", out=gw[1][16:17, :], in_=self.gbb[:, :])
    P.dma("sync", out=gnw[:, :], in_=self.gnwd[:, :])
    P.dma("gpsimd", out=wlr[:, :, :], in_=self.w_in.v(wsrc[:, :, 6144:6176]))
    xsrc = self.xw.h[:, 1024:2048].rearrange("(c p) t -> p c t", p=128)
    for tt in range(TOK // NT):
        _norm_tile(self, self.xw.v(xsrc[:, :, NT * tt:NT * tt + NT]), xs[tt % 2], sqs[tt % 2], rss[tt % 2],
                   lambda kc: xn[:, kc, NT * tt:NT * tt + NT], NT, ps[tt % 2])
    P.barrier()
    for d in range(2):
        P.op("vector", "memset", ap=lrT[d][:, :], constant=1.0)
    pc_ = [0]

    def nxt():
        pc_[0] += 1
        return ps[pc_[0] % 4]
    for h in range(4):
        w = wq[0]
        P.dma("gpsimd", out=w[:, :, :], in_=self.w_in.v(wsrc[:, :, 3072 + 128 * h:3072 + 128 * h + 128]))
        for tt in range(2):
            pp = nxt()
            for kc in range(KC):
                P.op("tensor", "matmul", out=pp[:, :], lhsT=w[:, kc, :], rhs=xn[:, kc, 512 * tt:512 * tt + 512],
                     start=(kc == 0), stop=(kc == KC - 1))
            P.op("scalar", "activation", out=gqT[:, h, 512 * tt:512 * tt + 512], in_=pp[:, :], func=AF.Copy)
        for tt in range(2):
            pp = nxt()
            for kc in range(KC):
                P.op("tensor", "matmul", out=pp[:, :], lhsT=Wgk[:, kc, 128 * h:128 * h + 128],
                     rhs=xn[:, kc, 512 * tt:512 * tt + 512], start=(kc == 0), stop=(kc == KC - 1))
            P.op("vector", "tensor_copy", out=gkT[:, h, 512 * tt:512 * tt + 512], in_=pp[:, :])
    for d in range(2):
        for tt in range(2):
            pp = nxt()
            for kc in range(KC):
                P.op("tensor", "matmul", out=pp[0:16, :], lhsT=wlr[:, kc, 16 * d:16 * d + 16],
                     rhs=xn[:, kc, 512 * tt:512 * tt + 512], start=(kc == 0), stop=(kc == KC - 1))
            P.op("vector", "tensor_copy", out=lrT[d][0:16, 512 * tt:512 * tt + 512], in_=pp[0:16, :])
    for st in range(8):
        tsl = slice(128 * st, 128 * st + 128)
        pp = nxt()
        for kc in range(KC):
            P.op("tensor", "matmul", out=pp[:, :], lhsT=xn[:, kc, tsl], rhs=Wgk[:, kc, :],
                 start=(kc == 0), stop=(kc == KC - 1))
        P.op("vector", "tensor_copy", out=gktm[:, st, :], in_=pp[:, :])
        for hv in range(2):
            pp = nxt()
            for kc in range(KC):
                P.op("tensor", "matmul", out=pp[:, :], lhsT=xn[:, kc, tsl], rhs=Wgv[:, kc, 512 * hv:512 * hv + 512],
                     start=(kc == 0), stop=(kc == KC - 1))
            P.op("scalar", "activation", out=vtm[:, st, 512 * hv:512 * hv + 512], in_=pp[:, :], func=AF.Copy)
    P.barrier()
    RA.reset()
    Wgr = P.sbuf("Wgr", [128, KC, 512], BF16, RA)
    for hv in range(2):
        for q in range(2):
            P.dma("gpsimd", out=Wgr[:, 8 * q:8 * q + 8, :],
                  in_=self.w_in.v(wsrc[:, 8 * q:8 * q + 8, 5120 + 512 * hv:5120 + 512 * hv + 512]))
        for st in range(8):
            tsl = slice(128 * st, 128 * st + 128)
            pp = nxt()
            for kc in range(KC):
                P.op("tensor", "matmul", out=pp[:, :], lhsT=xn[:, kc, tsl], rhs=Wgr[:, kc, :],
                     start=(kc == 0), stop=(kc == KC - 1))
            P.op("scalar", "activation", out=sgr[:, st, 512 * hv:512 * hv + 512], in_=pp[:, :], func=AF.Silu)
    P.barrier()
    RA.reset()
    RC.reset()
    RW = RC
    lg = [P.sbuf(f"lg{d}", [128, 8, 512], F32, RA) for d in range(2)]
    ew = P.sbuf("gew", [128, 512], F32, RW)
    for d in range(2):
        for st in range(8):
            _gate_l(self, lrT[d][:, 128 * st:128 * st + 128], gw[d], gb[d], lg[d][:, st, :], ps[st % 2], ew)
    eq = P.sbuf("geq", [128, 512], F32, RA)
    ek = P.sbuf("gek", [128, 512], F32, RA)
    qin = [P.sbuf(f"gqin{d}", [128, TOK], BF16, RA) for d in range(2)]
    kin = [P.sbuf(f"gkin{d}", [128, TOK], BF16, RA) for d in range(2)]
    wend = P.sbuf("gwend", [128, 128], F32, RW)
    kend = [P.sbuf(f"gkend{d}", [128, 8, 128], BF16, RW) for d in range(2)]
    dec = [P.sbuf(f"gdec{d}", [128, 16], F32, RW) for d in range(2)]
    Scur = [[P.sbuf(f"gScur{d}{i}", [128, 256], F32, RW) for i in range(2)] for d in range(2)]
    Sall = [P.sbuf(f"gSall{d}", [128, 16, 256], BF16, RW) for d in range(2)]
    Am = [[P.sbuf(f"gAm{d}{i}", [128, 128], BF16, RW) for i in range(2)] for d in range(2)]
    ot = [P.sbuf(f"got{i}", [128, 256], F32, RW) for i in range(2)]
    og = [P.sbuf(f"gog{i}", [128, 256], BF16, RW) for i in range(2)]
    ssum = [P.sbuf(f"gssum{i}", [128, 1], F32, RW) for i in range(2)]
    junk = P.sbuf("gjunk", [128, 256], F32, RW)
    TRI = (C_TRIF, C_TRIB)
    TRIR = (C_TRIRF, C_TRIRB)
    for h in range(4):
        hc = slice(128 * h, 128 * h + 128)
        for d in (1, 0):
            for g4 in range(2):
                pcs = ps[g4]
                for i in range(4):
                    st = 4 * g4 + i
                    P.op("tensor", "matmul", out=pcs[:, 128 * i:128 * i + 128], lhsT=lg[d][:, st, hc],
                         rhs=_cfv(self, TRI[d], 128), start=True, stop=True)
                ts4 = slice(512 * g4, 512 * g4 + 512)
                P.op("scalar", "activation", out=eq[:, :], in_=pcs[:, :], func=AF.Exp, scale=-1.0 / 16)
                P.op("scalar", "activation", out=ek[:, :], in_=pcs[:, :], func=AF.Exp, scale=1.0 / 16)
                P.op("vector", "scalar_tensor_tensor", out=qin[d][:, ts4], in0=gqT[:, h, ts4],
                     scalar=float(128 ** -0.5), in1=eq[:, :], op0=ALU.mult, op1=ALU.mult)
                P.op("vector", "tensor_tensor", out=kin[d][:, ts4], in0=gkT[:, h, ts4], in1=ek[:, :], op=ALU.mult)
            pdc = ps[2]
            for st in range(8):
                P.op("tensor", "matmul", out=pdc[:, 2 * st:2 * st + 2], lhsT=lg[d][:, st, hc],
                     rhs=_cfv(self, C_CI, 2), start=True, stop=True)
            P.op("scalar", "activation", out=dec[d][:, :], in_=pdc[:, 0:16], func=AF.Exp, scale=-1.0 / 16)
            for st in range(8):
                pr = ps[3 + st % 2]
                P.op("tensor", "matmul", out=pr[:, 0:128], lhsT=_cfv(self, TRIR[d], 128), rhs=lg[d][:, st, hc],
                     start=True, stop=True)
                P.op("scalar", "activation", out=wend[:, :], in_=pr[:, 0:128], func=AF.Exp, scale=-1.0 / 16)
                P.op("vector", "tensor_tensor", out=kend[d][:, st, :], in0=gktm[:, st, hc], in1=wend[:, :],
                     op=ALU.mult)
        order = {0: list(range(16)), 1: list(range(15, -1, -1))}
        P.op("vector", "tensor_copy", out=Scur[0][0][:, :], in_=SF[:, h, :])
        P.op("vector", "tensor_copy", out=Scur[1][0][:, :], in_=SB[:, h, :])
        for step in range(16):
            for d in (1, 0):
                n = order[d][step]
                st, ci = divmod(n, 2)
                cp = slice(64 * ci, 64 * ci + 64)
                sc, sn = Scur[d][step % 2], Scur[d][(step + 1) % 2]
                P.op("scalar", "activation", out=Sall[d][:, n, :], in_=sc[:, :], func=AF.Copy)
                pu = ps[3 + 2 * d + step % 2]
                P.op("tensor", "matmul", out=pu[:, 0:256], lhsT=kend[d][cp, st, :],
                     rhs=vtm[cp, st, 256 * h:256 * h + 256], start=True, stop=True)
                P.op("vector", "scalar_tensor_tensor", out=sn[:, :], in0=sc[:, :],
                     scalar=dec[d][:, n:n + 1], in1=pu[:, 0:256], op0=ALU.mult, op1=ALU.add)
        def outA(st):
            tsl = slice(128 * st, 128 * st + 128)
            for d in (1, 0):
                pa = ps[2 * d + st % 2]
                P.op("tensor", "matmul", out=pa[:, 0:128], lhsT=kin[d][:, tsl], rhs=qin[d][:, tsl],
                     start=True, stop=True)
                P.op("vector", "tensor_tensor", out=Am[d][st % 2][:, :], in0=pa[:, 0:128],
                     in1=_cfv(self, TRI[d], 128), op=ALU.mult)
            po = ps[4 + st % 2]
            P.op("tensor", "matmul", out=po[:, 0:256], lhsT=Am[1][st % 2][:, :],
                 rhs=vtm[:, st, 256 * h:256 * h + 256], start=True, stop=False)
            P.op("tensor", "matmul", out=po[:, 0:256], lhsT=Am[0][st % 2][:, :],
                 rhs=vtm[:, st, 256 * h:256 * h + 256], start=False, stop=False)
            for d in (1, 0):
                for ci in range(2):
                    n = 2 * st + ci
                    P.op("tensor", "matmul", out=po[64 * ci:64 * ci + 64, 0:256],
                         lhsT=qin[d][:, 128 * st + 64 * ci:128 * st + 64 * ci + 64], rhs=Sall[d][:, n, :],
                         start=False, stop=(d == 0))

        def outB(st):
            tsl = slice(128 * st, 128 * st + 128)
            po = ps[4 + st % 2]
            o_, g_, s_ = ot[st % 2], og[st % 2], ssum[st % 2]
            P.op("scalar", "activation", out=junk[:, :], in_=po[:, 0:256], func=AF.Square, accum_out=s_[:, :])
            P.op("scalar", "activation", out=s_[:, :], in_=s_[:, :], func=AF.Sqrt, bias=self.epsc[:, 0:1],
                 scale=1.0 / 256)
            P.op("vector", "reciprocal", out=s_[:, :], in_=s_[:, :])
            P.op("vector", "scalar_tensor_tensor", out=o_[:, :], in0=po[:, 0:256], scalar=s_[:, 0:1],
                 in1=gnw[:, 256 * h:256 * h + 256], op0=ALU.mult, op1=ALU.mult)
            P.op("vector", "tensor_tensor", out=g_[:, :], in0=o_[:, :], in1=sgr[:, st, 256 * h:256 * h + 256],
                 op=ALU.mult)
            for i in range(2):
                P.op("tensor", "transpose", out=self.ps_t[:, 256 * (st % 2) + 128 * i:256 * (st % 2) + 128 * i + 128],
                     in_=g_[:, 128 * i:128 * i + 128], identity=self.ident_b)
            P.op("vector", "tensor_copy", out=glaT[:, 2 * h:2 * h + 2, tsl],
                 in_=self.ps_t.v(self.ps_t.h[:, 256 * (st % 2):256 * (st % 2) + 256].rearrange("f (i t) -> f i t", i=2)))

        outA(0)
        for st in range(8):
            if st + 1 < 8:
                outA(st + 1)
            outB(st)
    P.barrier()


Builder.phase_gla_own = _phase_gla_own


def _norm_layout(w):
    return np.ascontiguousarray(np.asarray(w, np.float32).reshape(KC, 128).T)


def shared_inputs(inp):
    f = lambda a: np.ascontiguousarray(np.asarray(a, np.float32))
    nw = np.concatenate([_norm_layout(inp["attn_norm_w"][0]), _norm_layout(inp["ffn_norm_w"][0]),
                         _norm_layout(inp["final_norm_w"])], axis=1)
    return {
        "nw": nw, "cst": build_consts2(), "w_out": f(inp["w_out"][0]), "w_gate": f(inp["w_gate"][0]),
        "w_up": f(inp["w_up"][0]), "w_down": f(inp["w_down"][0]), "w_in": f(inp["w_in"][0]),
        "relt": f(inp["rel_bias_table"]), "ohd": build_onehot(),
        "gwf": f(inp["gla_gate_w_fwd"][0]), "gwb": f(inp["gla_gate_w_bwd"][0]),
        "gbf": f(inp["gla_gate_b_fwd"][0])[None, :], "gbb": f(inp["gla_gate_b_bwd"][0])[None, :],
        "gnwd": np.ascontiguousarray(np.broadcast_to(f(inp["gla_norm_w"][0])[None, :], (128, 1024))),
    }


def core_inputs(inp, c):
    b, g = divmod(c, 4)
    start = TOK * g
    x = np.asarray(inp["x"][b], np.float32)
    w_in = np.asarray(inp["w_in"][0], np.float32)
    xw = np.zeros((D, WIN), np.float32)
    lo, hi = max(0, start - 1024), min(SEQ, start + 2048)
    xw[:, lo - (start - 1024):hi - (start - 1024)] = x[lo:hi].T
    slots = [(blk, 0) for blk in range(g)] + [(blk, 1) for blk in range(3, g, -1)]
    xo = np.zeros((3, D, TOK), np.float32)
    wlr_o = np.zeros((3, D, 16), np.float32)
    gw_o = np.zeros((3, 16, 512), np.float32)
    gb_o = np.zeros((3, 1, 512), np.float32)
    flg = np.zeros((128, 8), np.float32)
    for j, (blk, isb) in enumerate(slots):
        xb = x[TOK * blk:TOK * blk + TOK]
        if isb:
            xb = xb[::-1]
        xo[j] = xb.T
        wlr_o[j] = w_in[:, 6160:6176] if isb else w_in[:, 6144:6160]
        gw_o[j] = inp["gla_gate_w_bwd"][0] if isb else inp["gla_gate_w_fwd"][0]
        gb_o[j, 0] = inp["gla_gate_b_bwd"][0] if isb else inp["gla_gate_b_fwd"][0]
        flg[:, 3 * isb + j] = 1.0
    return {"xw": xw, "xo": xo, "wlr_o": wlr_o, "gw_o": gw_o, "gb_o": gb_o, "flgd": flg,
            "valtd": build_valt(start)}


_CACHE = {}


def kernel(**inputs):
    if "nc" not in _CACHE:
        b = Builder(stage="full")
        _CACHE["nc"] = b.build()
    nc = _CACHE["nc"]
    sh = shared_inputs(inputs)
    in_maps = []
    for c in range(8):
        m = dict(sh)
        m.update(core_inputs(inputs, c))
        in_maps.append(m)
    res = run_bass_kernel_spmd(nc, in_maps, core_ids=list(range(8)))
    out = np.zeros((NB, SEQ, D), np.float32)
    for c in range(8):
        b, g = divmod(c, 4)
        out[b, TOK * g:TOK * g + TOK, :] = np.asarray(res.results[c]["out"]).T
    return out
```
